# Optimizing a Trainium2 kernel written in Bass

```python
import math
import jax, jax.numpy as jnp
from jax import lax
import numpy as np

D_MODEL = 1024
BATCH = 1
SEQ = 16384
DEPTH = 2
DEC_BATCH = 128
DEC_SEQ = 1
PAST_LEN = 16384
PAGE_SIZE = 128

N_META = 16
HEAD_DIM = 64
N_HEADS = D_MODEL // HEAD_DIM
N_KV_HEADS = N_HEADS // 4
GROUP = N_HEADS // N_KV_HEADS
D_ATT = N_HEADS * HEAD_DIM
D_KV = N_KV_HEADS * HEAD_DIM
WINDOW = 128
BLOCK = 128
D_CONV = D_MODEL
CONV_K = 31
D_FF = 2816
FFN_K = 3
D_IN = 2 * D_CONV + D_ATT + 2 * D_KV + 2 * D_MODEL
EPS = 1e-6
ATT_SCALE = 1.0 / math.sqrt(HEAD_DIM)
NEG = -1e30

kernel_name = 'hybrid_conformer_swa_sink_convffn_step'


def rmsnorm(x, g):
    xf = x.astype(jnp.float32)
    y = xf * lax.rsqrt(jnp.mean(xf * xf, axis=-1, keepdims=True) + EPS)
    return (y * g.astype(jnp.float32)).astype(x.dtype)


def layernorm(x, g, b):
    xf = x.astype(jnp.float32)
    mu = jnp.mean(xf, axis=-1, keepdims=True)
    var = jnp.mean(jnp.square(xf - mu), axis=-1, keepdims=True)
    y = (xf - mu) * lax.rsqrt(var + EPS)
    return (y * g.astype(jnp.float32) + b.astype(jnp.float32)).astype(x.dtype)


def causal_dwconv(x_hist, w, b):
    C = x_hist.shape[-1]
    out = lax.conv_general_dilated(x_hist, w[:, None, :], window_strides=(1,), padding='VALID',
                                   dimension_numbers=('NWC', 'WIO', 'NWC'), feature_group_count=C)
    return out + b


def split_in(h):
    offs = [D_CONV, 2 * D_CONV, 2 * D_CONV + D_ATT, 2 * D_CONV + D_ATT + D_KV,
            2 * D_CONV + D_ATT + 2 * D_KV, 2 * D_CONV + D_ATT + 2 * D_KV + D_MODEL]
    return jnp.split(h, offs, axis=-1)


def sink_softmax(s, sink, mask):
    s = jnp.where(mask, s, NEG)
    m = jnp.maximum(jnp.max(s, axis=-1, keepdims=True), sink)
    e = jnp.exp(s - m)
    return e / (jnp.sum(e, axis=-1, keepdims=True) + jnp.exp(sink - m))


def swa_banded(q, k, v, sinks):
    B, L = q.shape[:2]
    pad = (-L) % BLOCK
    nb = (L + pad) // BLOCK
    pw = ((0, 0), (pad, 0), (0, 0), (0, 0))
    qb = jnp.pad(q, pw).reshape(B, nb, BLOCK, N_KV_HEADS, GROUP, HEAD_DIM)
    kb = jnp.pad(k, pw).reshape(B, nb, BLOCK, N_KV_HEADS, HEAD_DIM)
    vb = jnp.pad(v, pw).reshape(B, nb, BLOCK, N_KV_HEADS, HEAD_DIM)
    pb = ((0, 0), (1, 0), (0, 0), (0, 0), (0, 0))
    kk = jnp.concatenate([jnp.pad(kb, pb)[:, :-1], kb], axis=2)
    vv = jnp.concatenate([jnp.pad(vb, pb)[:, :-1], vb], axis=2)
    s = jnp.einsum('bnqkgd,bnskd->bnkgqs', qb, kk, preferred_element_type=jnp.float32) * ATT_SCALE
    blk = jnp.arange(nb)[:, None, None] * BLOCK
    qi = blk + jnp.arange(BLOCK)[None, :, None]
    ki = blk - BLOCK + jnp.arange(2 * BLOCK)[None, None, :]
    rel = qi - ki
    mask = ((rel >= 0) & (rel <= WINDOW) & (ki >= pad))[None, :, None, None]
    sink = sinks.astype(jnp.float32).reshape(1, 1, N_KV_HEADS, GROUP, 1, 1)
    pr = sink_softmax(s, sink, mask)
    o = jnp.einsum('bnkgqs,bnskd->bnqkgd', pr.astype(v.dtype), vv)
    return o.reshape(B, nb * BLOCK, D_ATT)[:, pad:]


def swa_decode(q, k, v, k_buf, v_buf, sinks):
    B, S = q.shape[:2]
    kk = jnp.concatenate([k_buf, k], axis=1)
    vv = jnp.concatenate([v_buf, v], axis=1)
    qg = q.reshape(B, S, N_KV_HEADS, GROUP, HEAD_DIM)
    s = jnp.einsum('bqkgd,bskd->bkgqs', qg, kk, preferred_element_type=jnp.float32) * ATT_SCALE
    qpos = PAST_LEN + jnp.arange(S)
    kpos = PAST_LEN - WINDOW + jnp.arange(WINDOW + S)
    rel = qpos[:, None] - kpos[None, :]
    mask = ((rel >= 0) & (rel <= WINDOW))[None, None, None]
    sink = sinks.astype(jnp.float32).reshape(1, N_KV_HEADS, GROUP, 1, 1)
    pr = sink_softmax(s, sink, mask)
    o = jnp.einsum('bkgqs,bskd->bqkgd', pr.astype(v.dtype), vv).reshape(B, S, D_ATT)
    return o, kk[:, -WINDOW:], vv[:, -WINDOW:]


def token_mixer(x, p, conv_hist, kv_buf):
    B, T, _ = x.shape
    xn = rmsnorm(x, p['norm_mix'])
    ua, ub, q, k, v, g_conv, g_att = split_in(xn @ p['w_in'])
    u = ua * jax.nn.sigmoid(ub)
    u_hist = jnp.concatenate([conv_hist, u], axis=1)
    c = causal_dwconv(u_hist, p['conv_dw'], p['conv_db'])
    c = jax.nn.silu(layernorm(c, p['conv_ln_g'], p['conv_ln_b'])) @ p['w_conv_pw']
    q = q.reshape(B, T, N_HEADS, HEAD_DIM)
    k = k.reshape(B, T, N_KV_HEADS, HEAD_DIM)
    v = v.reshape(B, T, N_KV_HEADS, HEAD_DIM)
    if kv_buf is None:
        a = swa_banded(q, k, v, p['attn_sinks'])
        new_k, new_v = k[:, -WINDOW:], v[:, -WINDOW:]
    else:
        a, new_k, new_v = swa_decode(q, k, v, kv_buf[0], kv_buf[1], p['attn_sinks'])
    a = a @ p['w_attn_o']
    m = jax.nn.sigmoid(g_conv) * c + jax.nn.sigmoid(g_att) * a
    return m @ p['w_out'], new_k, new_v, u_hist[:, -(CONV_K - 1):]


def conv_ffn(x, p, ffn_hist):
    xn = rmsnorm(x, p['norm_ffn'])
    h, g = jnp.split(xn @ p['w_ffn_up'], [D_FF], axis=-1)
    h_hist = jnp.concatenate([ffn_hist, h], axis=1)
    c = causal_dwconv(h_hist, p['ffn_dw'], p['ffn_db'])
    return (jax.nn.gelu(c) * g) @ p['w_ffn_down'], h_hist[:, -(FFN_K - 1):]


def block(x, p, conv_hist, ffn_hist, kv_buf):
    dx, nk, nv, nc = token_mixer(x, p, conv_hist, kv_buf)
    x = x + dx
    dx, nf = conv_ffn(x, p, ffn_hist)
    return x + dx, nk, nv, nc, nf


def setup_inputs(seed: int = 0) -> dict:
    key = jax.random.key(seed)
    ks = jax.random.split(key, 24)
    f32 = jnp.float32
    nrm = lambda k, shape, s: jax.random.normal(k, shape, f32) * s
    return {
        'x_prompt': nrm(ks[0], (BATCH, SEQ, D_MODEL), 1.0),
        'x_sample': nrm(ks[1], (DEC_BATCH, DEC_SEQ, D_MODEL), 1.0),
        'cache_swa_k': nrm(ks[2], (DEPTH, DEC_BATCH, WINDOW, N_KV_HEADS, HEAD_DIM), 1.0),
        'cache_swa_v': nrm(ks[3], (DEPTH, DEC_BATCH, WINDOW, N_KV_HEADS, HEAD_DIM), 1.0),
        'state_conv': nrm(ks[4], (DEPTH, DEC_BATCH, CONV_K - 1, D_CONV), 0.5),
        'state_ffn_conv': nrm(ks[5], (DEPTH, DEC_BATCH, FFN_K - 1, D_FF), 1.0),
        'meta_tokens': nrm(ks[6], (N_META, D_MODEL), 1.0),
        'norm_mix': 1.0 + nrm(ks[7], (DEPTH, D_MODEL), 0.02),
        'w_in': nrm(ks[8], (DEPTH, D_MODEL, D_IN), D_MODEL ** -0.5),
        'conv_dw': nrm(ks[9], (DEPTH, CONV_K, D_CONV), CONV_K ** -0.5),
        'conv_db': nrm(ks[10], (DEPTH, D_CONV), 0.02),
        'conv_ln_g': 1.0 + nrm(ks[11], (DEPTH, D_CONV), 0.02),
        'conv_ln_b': nrm(ks[12], (DEPTH, D_CONV), 0.02),
        'w_conv_pw': nrm(ks[13], (DEPTH, D_CONV, D_MODEL), D_CONV ** -0.5),
        'attn_sinks': nrm(ks[14], (DEPTH, N_HEADS), 0.5),
        'w_attn_o': nrm(ks[15], (DEPTH, D_ATT, D_MODEL), D_ATT ** -0.5),
        'w_out': nrm(ks[16], (DEPTH, D_MODEL, D_MODEL), D_MODEL ** -0.5),
        'norm_ffn': 1.0 + nrm(ks[17], (DEPTH, D_MODEL), 0.02),
        'w_ffn_up': nrm(ks[18], (DEPTH, D_MODEL, 2 * D_FF), D_MODEL ** -0.5),
        'ffn_dw': nrm(ks[19], (DEPTH, FFN_K, D_FF), FFN_K ** -0.5),
        'ffn_db': nrm(ks[20], (DEPTH, D_FF), 0.02),
        'w_ffn_down': nrm(ks[21], (DEPTH, D_FF, D_MODEL), D_FF ** -0.5),
        'norm_final': 1.0 + nrm(ks[22], (D_MODEL,), 0.02),
    }


def reference(x_prompt, x_sample, cache_swa_k, cache_swa_v, state_conv, state_ffn_conv,
              meta_tokens, norm_mix, w_in, conv_dw, conv_db, conv_ln_g, conv_ln_b, w_conv_pw,
              attn_sinks, w_attn_o, w_out, norm_ffn, w_ffn_up, ffn_dw, ffn_db, w_ffn_down, norm_final):
    B = x_prompt.shape[0]
    meta = jnp.broadcast_to(meta_tokens[None].astype(x_prompt.dtype), (B, N_META, D_MODEL))
    xp = jnp.concatenate([meta, x_prompt], axis=1)
    xs = x_sample
    kp_l, vp_l, cp_l, fp_l = [], [], [], []
    ks_l, vs_l, cs_l, fs_l = [], [], [], []
    for l in range(DEPTH):
        p = dict(norm_mix=norm_mix[l], w_in=w_in[l], conv_dw=conv_dw[l], conv_db=conv_db[l],
                 conv_ln_g=conv_ln_g[l], conv_ln_b=conv_ln_b[l], w_conv_pw=w_conv_pw[l],
                 attn_sinks=attn_sinks[l], w_attn_o=w_attn_o[l], w_out=w_out[l], norm_ffn=norm_ffn[l],
                 w_ffn_up=w_ffn_up[l], ffn_dw=ffn_dw[l], ffn_db=ffn_db[l], w_ffn_down=w_ffn_down[l])
        zc = jnp.zeros((B, CONV_K - 1, D_CONV), xp.dtype)
        zf = jnp.zeros((B, FFN_K - 1, D_FF), xp.dtype)
        xp, nk, nv, nc, nf = block(xp, p, zc, zf, None)
        kp_l.append(nk); vp_l.append(nv); cp_l.append(nc); fp_l.append(nf)
        xs, nk, nv, nc, nf = block(xs, p, state_conv[l], state_ffn_conv[l], (cache_swa_k[l], cache_swa_v[l]))
        ks_l.append(nk); vs_l.append(nv); cs_l.append(nc); fs_l.append(nf)
    y_prompt = rmsnorm(xp, norm_final)[:, N_META:]
    y_sample = rmsnorm(xs, norm_final)
    return (y_prompt, y_sample,
            jnp.stack(kp_l), jnp.stack(vp_l), jnp.stack(cp_l), jnp.stack(fp_l),
            jnp.stack(ks_l), jnp.stack(vs_l), jnp.stack(cs_l), jnp.stack(fs_l))
```

```python
import numpy as np
from contextlib import ExitStack
import concourse.bass as bass
import concourse.mybir as mybir
from concourse.bass_utils import run_bass_kernel_spmd

F32 = mybir.dt.float32
BF16 = mybir.dt.bfloat16
AF = mybir.ActivationFunctionType
ALU = mybir.AluOpType
AX = mybir.AxisListType

NCORES = 8
NT = 19
NOWN = 16
SCS = [(0, 5), (5, 5), (10, 5), (15, 4)]
TCM = 640
NS = 16
D = 1024
DFF = 2816
NFC = 22
DIN = 5632
EPS = 1e-6
SCALE = 0.125
LV = 8 * 5 + 22 + 8 * 31 + 22 * 3
NV = 2 * LV + 8
O_NM, O_CDB, O_LG, O_LB, O_NF, O_FDB, O_CDW, O_FDW = 0, 8, 16, 24, 32, 40, 62, 62 + 248
NEGM = -30000.0

ENGS = ("tensor", "vector", "scalar", "gpsimd", "sync")


class Sched:
    def __init__(self, nc, stack):
        self.nc = nc
        self.stack = stack
        self.ops = {e: [] for e in ENGS}
        self.sem = {}
        self.cnt = {}
        self.seen = {e: {} for e in ENGS}
        self.res = {}
        self.chan = {}
        self.nsem = 0
        self.pb = 0
        self.reserved = set()
        self.new_phase()

    debug_tags = False
    names = []

    def _tag(self):
        if not self.debug_tags:
            return ""
        import traceback
        st = traceback.extract_stack(limit=6)
        return ">".join(str(f.lineno) for f in st[:-2])

    def _newsem(self, name):
        self.nsem += 1
        return self.stack.enter_context(self.nc.semaphore(name))

    def new_phase(self):
        for e in ("tensor", "vector", "scalar", "gpsimd"):
            self.sem[e] = self._newsem("s_%s_%d" % (e, self.nsem))
            self.cnt[e] = 0

    def bank(self, n=1):
        while True:
            if self.pb % 8 + n > 8:
                self.pb += 8 - self.pb % 8
            b = self.pb % 8
            if any((b + i) in self.reserved for i in range(n)):
                self.pb += 1
                continue
            self.pb += n
            return b

    def _deps(self, eng, reads, writes):
        evs = []
        for r in reads:
            st = self.res.get(r)
            if st is not None and st[0] is not None:
                evs.append(st[0])
            if st is not None and isinstance(r, tuple) and r[0] == "ps":
                evs.extend(ev for ev in st[1] if ev[2] != eng)
        for w in writes:
            st = self.res.get(w)
            if st is not None:
                if st[0] is not None:
                    evs.append(st[0])
                evs.extend(st[1])
        need = {}
        for (s, v, e) in evs:
            if e == "tensor" and eng == "tensor":
                continue
            k = id(s)
            if self.seen[eng].get(k, 0) >= v:
                continue
            if k not in need or need[k][1] < v:
                need[k] = (s, v)
        waits = []
        for k, (s, v) in need.items():
            self.seen[eng][k] = v
            waits.append((s, v))
        return waits

    def _commit(self, ev, reads, writes):
        for r in reads:
            st = self.res.setdefault(r, [None, []])
            st[1].append(ev)
        for w in writes:
            self.res[w] = [ev, []]

    def op(self, eng, fns, reads=(), writes=()):
        if callable(fns):
            fns = [fns]
        waits = self._deps(eng, reads, writes)
        self.cnt[eng] += 1
        ev = (self.sem[eng], self.cnt[eng], eng)
        self.ops[eng].append((waits, fns, (self.sem[eng], 1), self._tag()))
        self._commit(ev, reads, writes)

    def dma(self, queue, out, in_, chan, reads=(), writes=()):
        self.nout = getattr(self, "nout", 0) + 1
        writes = [("OUT", self.nout) if w == "OUT" else w for w in writes]
        if chan not in self.chan:
            self.chan[chan] = [self._newsem("d_%s" % chan), 0]
        c = self.chan[chan]
        waits = self._deps(queue, reads, writes)
        c[1] += 16
        ev = (c[0], c[1], "dma")
        self.ops[queue].append((waits, [lambda e: e.dma_start(out=out, in_=in_)], (c[0], 16), self._tag()))
        self._commit(ev, reads, writes)

    def barrier(self):
        evs = []
        for e in ("tensor", "vector", "scalar", "gpsimd"):
            if self.cnt[e] > 0:
                evs.append((self.sem[e], self.cnt[e]))
        for c in self.chan.values():
            evs.append((c[0], c[1]))
        for eng in ENGS:
            waits = []
            for (s, v) in evs:
                if self.seen[eng].get(id(s), 0) < v:
                    self.seen[eng][id(s)] = v
                    waits.append((s, v))
            self.ops[eng].append((waits, [], None, ""))

    def emit(self):
        nc = self.nc
        with nc.Block() as block:
            for ename in ENGS:
                lst = self.ops[ename]

                def body(e, lst=lst):
                    for waits, fns, inc, tag in lst:
                        for (s, v) in waits:
                            e.wait_ge(s, v)
                        ins = None
                        for f in fns:
                            ins = f(e)
                            if self.debug_tags:
                                ins.annotate(tag)
                                self.names.append((ins.ins.name, ename, tag, str(ins)[:300]))
                        if inc is not None and ins is not None:
                            ins.then_inc(inc[0], inc[1])

                getattr(block, ename)(body)


def ntile_list(tc):
    if tc <= 320:
        return [(0, tc)]
    h = tc // 2
    return [(0, h), (h, tc - h)]


def build(dbg=None):
    nc = bass.Bass("TRN2", target_bir_lowering=False)

    def din(name, shape):
        return nc.dram_tensor(name, shape, F32, kind="ExternalInput").ap()

    def dout(name, shape):
        return nc.dram_tensor(name, shape, F32, kind="ExternalOutput").ap()

    xin = din("xin", [NT * 128, D])
    xs = din("xs", [NS, D])
    masks = din("masks", [128, 5, 256])
    masks_s = din("masks_s", [128, 16])
    rowsel = din("rowsel", [128, 16])
    valid = din("valid", [128, 384])
    vecs = din("vecs", [128, NV])
    sinkrow = din("sinkrow", [128, 2, 16])
    sinkcol = din("sinkcol", [128, 2, 4])
    w_in = din("w_in", [2, D, DIN])
    w_pw = din("w_pw", [2, D, D])
    w_ao = din("w_ao", [2, D, D])
    w_out = din("w_out", [2, D, D])
    w_up = din("w_up", [2, D, 2 * DFF])
    w_dn = din("w_dn", [2, DFF, D])
    ck = din("ck", [2, NS, 128, 256])
    cv = din("cv", [2, NS, 128, 256])
    sconv = din("sconv", [2, NS, 30, D])
    sffn = din("sffn", [2, NS, 2, DFF])

    y = dout("y", [NOWN * 128 + NS, D])
    kv_p = dout("kv_p", [2, 2, 128, 256])
    conv_p = dout("conv_p", [2, 30, D])
    ffn_p = dout("ffn_p", [2, 2, DFF])
    k_s = dout("k_s", [2, NS, 128, 256])
    v_s = dout("v_s", [2, NS, 128, 256])
    conv_s = dout("conv_s", [2, NS, 30, D])
    ffn_s = dout("ffn_s", [2, NS, 2, DFF])

    with ExitStack() as es:
        def sb(name, shape, dt):
            return es.enter_context(nc.sbuf_tensor(name, shape, dt))

        idf = sb("idf", [128, 128], F32)
        idb = sb("idb", [128, 128], BF16)
        onesb = sb("onesb", [128, 128], BF16)
        epsc = sb("epsc", [128, 1], F32)
        vec = sb("vec", [128, NV], F32)
        maskb = sb("maskb", [128, 5, 256], BF16)
        masksb = sb("masksb", [128, 4, 16], BF16)
        rowselt = sb("rowselt", [128, 16], F32)
        validb = sb("validb", [128, 384], BF16)
        snk = sb("snk", [128, 2, 16], F32)
        nsnk = sb("nsnk", [128, 2, 16], F32)
        snkc = sb("snkc", [128, 2, 4], F32)
        nsnkc = sb("nsnkc", [128, 2, 4], F32)

        xT = sb("xT", [128, 8, TCM], F32)
        xnT = sb("xnT", [128, 8, TCM], BF16)
        R = sb("R", [128, 24, TCM], BF16)
        aT = sb("aT", [128, 8, TCM], BF16)
        kT = sb("kT", [128, 4, 128 + TCM], BF16)
        Vt = sb("Vt", [128, 6, 4, 128], BF16)
        kprev = sb("kprev", [128, 2, 4, 128], BF16)
        vprev = sb("vprev", [128, 2, 4, 128], BF16)
        uT = sb("uT", [128, 2, 30 + TCM], BF16)
        uhist = sb("uhist", [128, 2, 8, 30], BF16)
        hT = sb("hT", [128, 2, 2 + TCM], BF16)
        hhist = sb("hhist", [128, 2, 22, 2], BF16)
        NWS = 6
        wsl = sb("wsl", [128, NWS, 8, 128], BF16)
        wdsl = sb("wdsl", [128, 2, 22, 128], BF16)
        wv = sb("wv", [128, 8, 256], BF16)
        wk = sb("wk", [128, 8, 256], BF16)
        dg = sb("dg", [128, 31, 128], BF16)
        dgf = sb("dgf", [128, 2, 3, 128], BF16)
        xstage = sb("xstage", [128, 2, 1024], F32)
        ostage = sb("ostage", [128, 2, 512], F32)
        sq = sb("sq", [128, 8, 320], BF16)
        st1 = sb("st1", [128, 2, 512], F32)
        st2 = sb("st2", [128, 2, 512], F32)
        tmp = sb("tmp", [128, 4, 512], F32)
        esb = sb("esb", [128, 3, 4, 256], F32)
        esnk = sb("esnk", [128, 2, 16], F32)
        pT = sb("pT", [128, 2, 4, 2, 128], BF16)
        pbf = sb("pbf", [128, 3, 4, 256], BF16)
        maskb2 = sb("maskb2", [128, 5, 512], BF16)
        sm = sb("sm", [128, 8, 8, 4], F32)
        ytmp = sb("ytmp", [128, 8, 128], F32)
        uf32 = sb("uf32", [128, 8, 30], F32)
        hf32 = sb("hf32", [128, 22, 16], F32)
        uTs = R[:, 0:8, 64:64 + 31 * NS].rearrange("p j (k b) -> p j k b", k=31)
        hTs = R[:, 0:22, 576:576 + 3 * NS].rearrange("p j (k b) -> p j k b", k=3)
        ess = xT[:, :, 64:208].rearrange("p (a k) c -> p a k c", a=2)
        kdup = xT[:, :, 256:384].rearrange("p (a k) c -> p a k c", a=2)
        kvst = xT[:, 0:4, 384:640].rearrange("p (a b) c -> p a b c", a=2)
        pTs = aT[:, :, 64:192].rearrange("p (a k) c -> p a k c", a=2)
        KTb = aT[:, :, 192:320].rearrange("p (a k) c -> p a k c", a=2)
        Vb = aT[:, :, 320:448].rearrange("p (a k) c -> p a k c", a=2)
        pTn = aT[0:NS, :, 448:576].rearrange("p (a k) c -> p a k c", a=2)
        Vnew = xnT[0:NS, 0:4, 64:192]

        ps = es.enter_context(nc.psum_tensor("ps", [128, 8, 512], F32))

        S = Sched(nc, es)
        ctr = {"w": 0, "wd": 0, "xs": 0, "os": 0, "tmp": 0, "st": 0, "es": 0, "sm": 0, "kv": 0, "ev": 0}

        def rot(name, n):
            v = ctr[name] % n
            ctr[name] += 1
            return v

        def evac_eng():
            return "vector" if rot("ev", 2) == 0 else "scalar"

        def copy_op(eng, out, in_, reads, writes):
            if eng == "scalar":
                S.op("scalar", lambda e: e.activation(out=out, in_=in_, func=AF.Copy), reads=reads, writes=writes)
            else:
                S.op(eng, lambda e: e.tensor_copy(out, in_), reads=reads, writes=writes)

        S.dma("sync", vec[:], vecs[:, :], "c_vec", writes=["vec"])
        S.dma("sync", snk[:], sinkrow[:, :, :], "c_snk", writes=["snk"])
        S.dma("sync", snkc[:], sinkcol[:, :, :], "c_snkc", writes=["snkc"])
        S.dma("sync", rowselt[:], rowsel[:, :], "c_rs", writes=["rowsel"])
        S.dma("gpsimd", maskb[:], masks[:, :, :], "c_mask", writes=["maskb"])
        S.dma("gpsimd", validb[:], valid[:, :], "c_valid", writes=["validb"])
        for k4 in range(4):
            S.dma("gpsimd", masksb[:, k4, :], masks_s[:, :], "c_masks", writes=["masksb"])
        S.op("gpsimd", lambda e: e.memset(idf[:], 1.0), writes=["idf"])
        S.op("gpsimd", lambda e: e.affine_select(idf[:], idf[:], pattern=[[-1, 128]], compare_op=ALU.is_equal,
                                                 fill=0.0, base=0, channel_multiplier=1), reads=["idf"], writes=["idf"])
        S.op("vector", lambda e: e.tensor_copy(idb[:], idf[:]), reads=["idf"], writes=["idb"])
        S.op("vector", lambda e: e.tensor_copy(maskb2[:].rearrange("p m (u c) -> p m u c", u=2), maskb[:, :, None, :].broadcast_to([128, 5, 2, 256])),
             reads=["maskb"], writes=["maskb"])
        S.op("vector", lambda e: e.memset(onesb[:], 1.0 / 1024.0), writes=["onesb"])
        S.op("vector", lambda e: e.memset(epsc[:], EPS), writes=["epsc"])
        S.op("vector", lambda e: e.memset(kprev[:], 0.0), writes=[("kprev", 0), ("kprev", 1)])
        S.op("vector", lambda e: e.memset(vprev[:], 0.0), writes=[("vprev", 0), ("vprev", 1)])
        S.op("vector", lambda e: e.memset(uhist[:], 0.0), writes=[("uhist", l, j) for l in range(2) for j in range(8)])
        S.op("vector", lambda e: e.memset(hhist[:], 0.0), writes=[("hhist", l, j) for l in range(2) for j in range(22)])
        S.op("vector", lambda e: e.tensor_scalar(nsnk[:], snk[:], -1.0, None, ALU.mult), reads=["snk"], writes=["nsnk"])
        S.op("scalar", lambda e: e.activation(out=esnk[:], in_=snk[:], func=AF.Exp), reads=["snk"], writes=["esnk"])
        S.op("vector", lambda e: e.tensor_scalar(nsnkc[:], snkc[:], -1.0, None, ALU.mult), reads=["snkc"], writes=["nsnkc"])

        def vcol(l, off, j):
            c = l * LV + off + j
            return vec[:, c:c + 1]

        def load_w(src, k_chunks=8):
            if k_chunks == 8:
                s = rot("w", NWS)
                S.dma("gpsimd", wsl[:, s, :, :], src.rearrange("(k p) m -> p k m", p=128), "w%d" % s, writes=[("w", s)])
                return ("w", s), (lambda k, s=s: wsl[:, s, k, :])
            s = rot("wd", 2)
            S.dma("gpsimd", wdsl[:, s, :, :], src.rearrange("(k p) m -> p k m", p=128), "wd%d" % s, writes=[("wd", s)])
            return ("wd", s), (lambda k, s=s: wdsl[:, s, k, :])

        def mm_group(out_ap, pairs, reads, bank_keys):
            n = len(pairs)
            fns = []
            for i, (l_, r_) in enumerate(pairs):
                fns.append(lambda e, l_=l_, r_=r_, i=i: e.matmul(out_ap, l_, r_, start=(i == 0), stop=(i == n - 1)))
            S.op("tensor", fns, reads=reads, writes=bank_keys)

        def proj(wkey, wfn, src, skeys, c0, n, kc=8):
            b = S.bank()
            mm_group(ps[:, b, 0:n], [(wfn(k), src(k, c0, n)) for k in range(kc)],
                     reads=[wkey] + skeys, bank_keys=[("ps", b)])
            return b

        class V:
            pass

        def make_views(tc):
            v = V()
            v.tc = tc
            v.nts = ntile_list(tc)
            v.x = lambda j, c0, n: xT[:, j, c0:c0 + n]
            v.xn = lambda j, c0, n: xnT[:, j, c0:c0 + n]
            v.q = lambda j, c0, n: R[:, j, c0:c0 + n]
            v.m = lambda j, c0, n: R[:, 8 + j, c0:c0 + n]
            v.s = lambda j, c0, n: R[:, 16 + j, c0:c0 + n]
            v.hg = lambda j, c0, n: R[:, j, c0:c0 + n]
            v.a = lambda j, c0, n: aT[:, j, c0:c0 + n]
            return v

        def kx(i):
            return [("xT", j, i) for j in range(8)]

        def kxn(i):
            return [("xnT", j, i) for j in range(8)]

        def kR(base, i, cnt=8):
            return [("R", base + j, i) for j in range(cnt)]

        def ka(i):
            return [("aT", j, i) for j in range(8)]

        def nts_of(v, c0, c1):
            return [i for i, (a, n) in enumerate(v.nts) if a < c1 and a + n > c0]

        def rms_stats(src_fn, src_keys, c0, n):
            S.op("scalar", lambda e: e.activation(out=sq[:, :, 0:n], in_=src_fn(c0, n), func=AF.Square),
                 reads=src_keys, writes=["sq"])
            b = S.bank()
            mm_group(ps[:, b, 0:n], [(onesb[:], sq[:, j, 0:n]) for j in range(8)], reads=["sq", "onesb"], bank_keys=[("ps", b)])
            sl = rot("st", 2)
            S.op("scalar", lambda e: e.activation(out=st1[:, sl, 0:n], in_=ps[:, b, 0:n], func=AF.Sqrt, bias=epsc[:, 0:1], scale=1.0),
                 reads=[("ps", b), "epsc"], writes=[("st1", sl)])
            S.op("vector", lambda e: e.reciprocal(st1[:, sl, 0:n], st1[:, sl, 0:n]), reads=[("st1", sl)], writes=[("st1", sl)])
            return sl

        def rmsnorm_to_xn(v, l, goff):
            for i, (c0, n) in enumerate(v.nts):
                sl = rms_stats(lambda c0, n: xT[:, :, c0:c0 + n], kx(i), c0, n)
                for j in range(8):
                    S.op("vector", lambda e, j=j, c0=c0, n=n, sl=sl: e.scalar_tensor_tensor(
                        out=xnT[:, j, c0:c0 + n], in0=xT[:, j, c0:c0 + n], scalar=vcol(l, goff, j) if l < 2 else vec[:, 2 * LV + j:2 * LV + j + 1],
                        in1=st1[:, sl, 0:n], op0=ALU.mult, op1=ALU.mult),
                        reads=[("xT", j, i), ("st1", sl), "vec"], writes=[("xnT", j, i)])

        def load_x(v, t0, ntl, sample):
            if sample:
                sl = rot("xs", 2)
                S.dma("sync", xstage[0:NS, sl, :], xs[:, :], "xs%d" % sl, writes=[("xstage", sl)])
                b = S.bank()
                fns = [lambda e, j=j: e.transpose(ps[:, b, j * NS:(j + 1) * NS], xstage[0:NS, sl, j * 128:(j + 1) * 128], idf[0:NS, 0:NS])
                       for j in range(8)]
                S.op("tensor", fns, reads=[("xstage", sl), "idf"], writes=[("ps", b)])
                S.op("vector", lambda e: e.tensor_copy(xT[:, :, 0:NS], ps[:, b, 0:8 * NS].rearrange("p (j t) -> p j t", j=8)),
                     reads=[("ps", b)], writes=kx(0))
                return
            for t in range(ntl):
                sl = rot("xs", 2)
                S.dma("sync", xstage[:, sl, :], xin[(t0 + t) * 128:(t0 + t + 1) * 128, :], "xs%d" % sl, writes=[("xstage", sl)])
                for half in range(2):
                    b = S.bank()
                    fns = [lambda e, j=j, b=b, sl=sl: e.transpose(ps[:, b, (j % 4) * 128:(j % 4 + 1) * 128],
                                                           xstage[:, sl, j * 128:(j + 1) * 128], idf[:])
                           for j in range(half * 4, half * 4 + 4)]
                    S.op("tensor", fns, reads=[("xstage", sl), "idf"], writes=[("ps", b)])
                    its = nts_of(v, t * 128, t * 128 + 128)
                    copy_op(evac_eng(), xT[:, half * 4:half * 4 + 4, t * 128:(t + 1) * 128],
                            ps[:, b, :].rearrange("p (j t) -> p j t", j=4),
                            reads=[("ps", b)], writes=[("xT", j, i) for j in range(half * 4, half * 4 + 4) for i in its])

        def conv_branch(v, l, sc_first, sc_last, sample, part=None):
            tc = v.tc
            if part != 1:
              conv_part(v, l, sc_first, sc_last, sample)
            if part != 0:
              ln_part(v, l)

        def conv_part(v, l, sc_first, sc_last, sample):
            tc = v.tc
            import os as _os
            if sample:
                for g in range(int(_os.environ.get("KG", "4"))):
                    sl = rot("xs", 2)
                    S.dma("sync", xstage[0:120, sl, :], sconv[l, 4 * g:4 * g + 4, :, :].rearrange("b k c -> (b k) c"),
                          "xs%d" % sl, writes=[("xstage", sl)])
                    for half in range(2):
                        b = S.bank()
                        kvar = _os.environ.get("KVAR", "")
                        W_ = 128 if "A" in kvar else 120
                        fns = [lambda e, j=j, b=b, sl=sl, W_=W_: e.transpose(ps[:, b, (j % 4) * W_:(j % 4) * W_ + 120],
                                                               xstage[0:120, sl, j * 128:(j + 1) * 128], idf[0:120, 0:120])
                               for j in range(half * 4, half * 4 + 4)]
                        S.op("tensor", fns, reads=[("xstage", sl), "idf"], writes=[("ps", b)])
                        eng_ = evac_eng()
                        for j in range(half * 4, half * 4 + 4):
                            copy_op(eng_, uTs[:, j, 0:30, 4 * g:4 * g + 4],
                                    ps[:, b, (j % 4) * W_:(j % 4) * W_ + 120].rearrange("p (b k) -> p k b", b=4),
                                    reads=[("ps", b)], writes=[("uTs", j)])
            import os as _os
            ksub = int(_os.environ.get("KSUB", "99"))
            if ksub <= 1:
                return
            for j in range(8):
                if ksub <= 2 and j >= 1:
                    return
                wa, fa = load_w(w_in[l, :, j * 128:(j + 1) * 128])
                wb, fb = load_w(w_in[l, :, D + j * 128:D + (j + 1) * 128])
                S.op("vector", lambda e, j=j: e.tensor_tensor(
                    dg[:], idb[:, None, :].broadcast_to([128, 31, 128]),
                    vec[:, l * LV + O_CDW + j * 31:l * LV + O_CDW + (j + 1) * 31, None].broadcast_to([128, 31, 128]), ALU.mult),
                    reads=["idb", "vec"], writes=["dg"])
                ub = j % 2
                if not sample:
                    S.op("vector", lambda e, j=j, ub=ub: e.tensor_copy(uT[:, ub, 0:30], uhist[:, l, j, :]),
                         reads=[("uhist", l, j)], writes=[("uT", ub, "h")])
                for i, (c0, n) in enumerate(v.nts):
                    ba = proj(wa, fa, v.xn, kxn(i), c0, n)
                    bb = proj(wb, fb, v.xn, kxn(i), c0, n)
                    ts = rot("tmp", 4)
                    S.op("scalar", lambda e, bb=bb, ts=ts, n=n: e.activation(out=tmp[:, ts, 0:n], in_=ps[:, bb, 0:n], func=AF.Sigmoid),
                         reads=[("ps", bb)], writes=[("tmp", ts)])
                    if sample:
                        S.op("vector", lambda e, ba=ba, ts=ts, j=j: e.tensor_tensor(uf32[:, j, 0:NS], ps[:, ba, 0:NS], tmp[:, ts, 0:NS], ALU.mult),
                             reads=[("ps", ba), ("tmp", ts)], writes=[("uf32", j)])
                        S.op("vector", lambda e, j=j: e.tensor_copy(uTs[:, j, 30, :], uf32[:, j, 0:NS]),
                             reads=[("uf32", j)], writes=[("uTs", j)])
                    else:
                        S.op("vector", lambda e, ba=ba, ts=ts, c0=c0, n=n, ub=ub: e.tensor_tensor(
                            uT[:, ub, 30 + c0:30 + c0 + n], ps[:, ba, 0:n], tmp[:, ts, 0:n], ALU.mult),
                            reads=[("ps", ba), ("tmp", ts)], writes=[("uT", ub, i)])
                        if sc_last and c0 + n == tc:
                            S.op("vector", lambda e, ba=ba, ts=ts, n=n, j=j: e.tensor_tensor(
                                uf32[:, j, :], ps[:, ba, n - 30:n], tmp[:, ts, n - 30:n], ALU.mult),
                                reads=[("ps", ba), ("tmp", ts)], writes=[("uf32", j)])
                if not sample:
                    nn = len(v.nts)
                    if sc_first:
                        S.op("vector", lambda e, ub=ub: e.tensor_tensor(uT[:, ub, 30:30 + 384], uT[:, ub, 30:30 + 384], validb[:], ALU.mult),
                             reads=[("uT", ub, i) for i in range(nn)] + ["validb"], writes=[("uT", ub, i) for i in range(nn)])
                    S.op("vector", lambda e, j=j, ub=ub: e.tensor_copy(uhist[:, l, j, :], uT[:, ub, tc:tc + 30]),
                         reads=[("uT", ub, i) for i in range(nn)] + [("uT", ub, "h")], writes=[("uhist", l, j)])
                for i, (c0, n) in enumerate(v.nts):
                    b = S.bank()
                    if sample:
                        pairs = [(dg[:, k, :], uTs[:, j, k, :]) for k in range(31)]
                        rd = ["dg", ("uTs", j)]
                    else:
                        pairs = [(dg[:, k, :], uT[:, ub, c0 + k:c0 + k + n]) for k in range(31)]
                        rd = ["dg", ("uT", ub, "h")] + [("uT", ub, ii) for ii in range(max(0, i - 1), i + 1)]
                    mm_group(ps[:, b, 0:n], pairs, reads=rd, bank_keys=[("ps", b)])
                    S.op("scalar", lambda e, b=b, j=j, c0=c0, n=n: e.activation(out=v.s(j, c0, n), in_=ps[:, b, 0:n], func=AF.Identity,
                                                                                 bias=vcol(l, O_CDB, j), scale=1.0),
                         reads=[("ps", b), "vec"], writes=[("R", 16 + j, i)])
        def ln_part(v, l):
            for i, (c0, n) in enumerate(v.nts):
                bm = S.bank()
                mm_group(ps[:, bm, 0:n], [(onesb[:], v.s(j, c0, n)) for j in range(8)], reads=kR(16, i) + ["onesb"], bank_keys=[("ps", bm)])
                S.op("scalar", lambda e, c0=c0, n=n: e.activation(out=sq[:, :, 0:n], in_=R[:, 16:24, c0:c0 + n], func=AF.Square),
                     reads=kR(16, i), writes=["sq"])
                b2 = S.bank()
                mm_group(ps[:, b2, 0:n], [(onesb[:], sq[:, j, 0:n]) for j in range(8)], reads=["sq", "onesb"], bank_keys=[("ps", b2)])
                sl = rot("st", 2)
                S.op("vector", lambda e, sl=sl, bm=bm, n=n: e.tensor_copy(st2[:, sl, 0:n], ps[:, bm, 0:n]), reads=[("ps", bm)], writes=[("st2", sl)])
                S.op("vector", lambda e, sl=sl, n=n: e.tensor_tensor(st1[:, sl, 0:n], st2[:, sl, 0:n], st2[:, sl, 0:n], ALU.mult),
                     reads=[("st2", sl)], writes=[("st1", sl)])
                S.op("vector", lambda e, sl=sl, b2=b2, n=n: e.tensor_tensor(st1[:, sl, 0:n], ps[:, b2, 0:n], st1[:, sl, 0:n], ALU.subtract),
                     reads=[("ps", b2), ("st1", sl)], writes=[("st1", sl)])
                S.op("vector", lambda e, sl=sl, n=n: e.tensor_scalar(st1[:, sl, 0:n], st1[:, sl, 0:n], 0.0, None, ALU.max),
                     reads=[("st1", sl)], writes=[("st1", sl)])
                S.op("scalar", lambda e, sl=sl, n=n: e.activation(out=st1[:, sl, 0:n], in_=st1[:, sl, 0:n], func=AF.Sqrt, bias=epsc[:, 0:1], scale=1.0),
                     reads=[("st1", sl), "epsc"], writes=[("st1", sl)])
                S.op("vector", lambda e, sl=sl, n=n: e.reciprocal(st1[:, sl, 0:n], st1[:, sl, 0:n]), reads=[("st1", sl)], writes=[("st1", sl)])
                for j in range(8):
                    ts = rot("tmp", 4)
                    S.op("vector", lambda e, j=j, ts=ts, sl=sl, c0=c0, n=n: e.tensor_tensor(tmp[:, ts, 0:n], v.s(j, c0, n), st2[:, sl, 0:n], ALU.subtract),
                         reads=[("R", 16 + j, i), ("st2", sl)], writes=[("tmp", ts)])
                    S.op("vector", lambda e, ts=ts, sl=sl, n=n: e.tensor_tensor(tmp[:, ts, 0:n], tmp[:, ts, 0:n], st1[:, sl, 0:n], ALU.mult),
                         reads=[("tmp", ts), ("st1", sl)], writes=[("tmp", ts)])
                    S.op("scalar", lambda e, j=j, ts=ts, c0=c0, n=n: e.activation(out=v.s(j, c0, n), in_=tmp[:, ts, 0:n], func=AF.Silu,
                                                                                  bias=vcol(l, O_LB, j), scale=vcol(l, O_LG, j)),
                         reads=[("tmp", ts), "vec"], writes=[("R", 16 + j, i)])

        def emit_rows(l, src_fn, ncols, nchunks, dst_fn, keys):
            for g0 in range(0, nchunks, 4):
                g1 = min(nchunks, g0 + 4)
                b = S.bank()
                fns = [lambda e, j=j, b=b, g0=g0: e.transpose(ps[0:ncols, b, (j - g0) * 128:(j - g0 + 1) * 128], src_fn(j), idf[:])
                       for j in range(g0, g1)]
                S.op("tensor", fns, reads=keys + ["idf"], writes=[("ps", b)])
                sl = rot("os", 2)
                w = (g1 - g0) * 128
                copy_op(evac_eng(), ostage[0:ncols, sl, 0:w], ps[0:ncols, b, 0:w], reads=[("ps", b)], writes=[("ostage", sl)])
                S.dma("sync", dst_fn(g0 * 128, w), ostage[0:ncols, sl, 0:w], "os%d" % sl, reads=[("ostage", sl)], writes=["OUT"])

        def q_proj(v, l):
            for j in range(8):
                wq, fq = load_w(w_in[l, :, 2 * D + j * 128:2 * D + (j + 1) * 128])
                for i, (c0, n) in enumerate(v.nts):
                    b = proj(wq, fq, v.xn, kxn(i), c0, n)
                    copy_op(evac_eng(), v.q(j, c0, n), ps[:, b, 0:n], reads=[("ps", b)], writes=[("R", j, i)])

        def qkv(v, l, sc_last, sample):
            tc = v.tc
            for kh in range(4):
                s = rot("w", NWS)
                src = w_in[l, :, 3 * D + kh * 64:3 * D + (kh + 1) * 64].rearrange("(k p) m -> p k m", p=128)
                S.dma("gpsimd", wsl[:, s, :, 0:64], src, "w%d" % s, writes=[("w", s)])
                S.dma("gpsimd", wsl[:, s, :, 64:128], src, "w%d" % s, writes=[("w", s)])
                fk = lambda k, s=s: wsl[:, s, k, :]
                for i, (c0, n) in enumerate(v.nts):
                    b = proj(("w", s), fk, v.xn, kxn(i), c0, n)
                    copy_op(evac_eng(), kT[:, kh, 128 + c0:128 + c0 + n], ps[:, b, 0:n], reads=[("ps", b)], writes=[("kT", kh, i)])
            S.dma("gpsimd", wv[:], w_in[l, :, 3 * D + 256:3 * D + 512].rearrange("(k p) m -> p k m", p=128), "wv", writes=["wv"])
            S.dma("gpsimd", wk[:], w_in[l, :, 3 * D:3 * D + 256].rearrange("(k p) m -> p k m", p=128), "wk", writes=["wk"])
            if sample:
                b = S.bank()
                mm_group(ps[0:NS, b, 0:256], [(xnT[:, k, 0:NS], wv[:, k, :]) for k in range(8)], reads=kxn(0) + ["wv"], bank_keys=[("ps", b)])
                S.op("vector", lambda e, b=b: e.tensor_copy(Vnew[:].rearrange("p k (u d) -> p k u d", u=2),
                                                          ps[0:NS, b, 0:256].rearrange("p (k d) -> p k d", k=4)[:, :, None, :].broadcast_to([NS, 4, 2, 64])),
                     reads=[("ps", b)], writes=["Vnew"])
                sl = rot("os", 2)
                S.op("vector", lambda e, b=b, sl=sl: e.tensor_copy(ostage[0:NS, sl, 0:256], ps[0:NS, b, 0:256]),
                     reads=[("ps", b)], writes=[("ostage", sl)])
                S.dma("sync", v_s[l, :, 127, :], ostage[0:NS, sl, 0:256], "os%d" % sl, reads=[("ostage", sl)], writes=["OUT"])
                b = S.bank()
                mm_group(ps[0:NS, b, 0:256], [(xnT[:, k, 0:NS], wk[:, k, :]) for k in range(8)], reads=kxn(0) + ["wk"], bank_keys=[("ps", b)])
                sl = rot("os", 2)
                S.op("scalar", lambda e, b=b, sl=sl: e.activation(out=ostage[0:NS, sl, 0:256], in_=ps[0:NS, b, 0:256], func=AF.Copy),
                     reads=[("ps", b)], writes=[("ostage", sl)])
                S.dma("sync", k_s[l, :, 127, :], ostage[0:NS, sl, 0:256], "os%d" % sl, reads=[("ostage", sl)], writes=["OUT"])
                return
            ntl = tc // 128
            for t in range(ntl):
                its = nts_of(v, t * 128, t * 128 + 128)
                rk = [("xnT", j, i) for j in range(8) for i in its]
                b = S.bank()
                mm_group(ps[:, b, 0:256], [(xnT[:, k, t * 128:(t + 1) * 128], wv[:, k, :]) for k in range(8)], reads=rk + ["wv"], bank_keys=[("ps", b)])
                S.op("vector", lambda e, b=b, t=t: e.tensor_copy(Vt[:, 1 + t].rearrange("p k (u d) -> p k u d", u=2),
                                                                 ps[:, b, 0:256].rearrange("p (k d) -> p k d", k=4)[:, :, None, :].broadcast_to([128, 4, 2, 64])),
                     reads=[("ps", b)], writes=[("Vt", 1 + t)])
                if sc_last and t == ntl - 1:
                    sl = rot("os", 2)
                    S.op("vector", lambda e, b=b, sl=sl: e.tensor_copy(ostage[:, sl, 0:256], ps[:, b, 0:256]),
                         reads=[("ps", b)], writes=[("ostage", sl)])
                    S.dma("sync", kv_p[l, 1, :, :], ostage[:, sl, 0:256], "os%d" % sl, reads=[("ostage", sl)], writes=["OUT"])
                    b = S.bank()
                    mm_group(ps[:, b, 0:256], [(xnT[:, k, t * 128:(t + 1) * 128], wk[:, k, :]) for k in range(8)], reads=rk + ["wk"], bank_keys=[("ps", b)])
                    sl = rot("os", 2)
                    S.op("scalar", lambda e, b=b, sl=sl: e.activation(out=ostage[:, sl, 0:256], in_=ps[:, b, 0:256], func=AF.Copy),
                         reads=[("ps", b)], writes=[("ostage", sl)])
                    S.dma("sync", kv_p[l, 0, :, :], ostage[:, sl, 0:256], "os%d" % sl, reads=[("ostage", sl)], writes=["OUT"])

        def attention(v, l, t0):
            ntl = v.tc // 128
            nn = len(v.nts)
            tc = v.tc
            S.op("vector", lambda e: e.tensor_copy(kT[:, :, 0:128], kprev[:, l]), reads=[("kprev", l)], writes=[("kT", kh, "prev") for kh in range(4)])
            S.op("vector", lambda e: e.tensor_copy(Vt[:, 0], vprev[:, l]), reads=[("vprev", l)], writes=[("Vt", 0)])
            units = [(t, kh) for t in range(ntl) for kh in range(4)]
            st_ = {}

            def stage_a(u):
                t, kh = u
                gi = t0 + t
                mi = gi if gi < 4 else 4
                its = nts_of(v, t * 128, t * 128 + 128)
                b = S.bank(2)
                sc = ps[:, b:b + 2, :].rearrange("p a (g c) -> p (a g) c", g=2)
                fns = []
                import os as _os
                oldmask = "M" in _os.environ.get("KATT", "")
                for g in range(4):
                    h = 4 * kh + g
                    jq, pb = h // 2, (h % 2) * 64
                    if oldmask:
                        fns.append(lambda e, g=g, jq=jq, pb=pb: e.matmul(sc[:, g, :], R[pb:pb + 64, jq, t * 128:(t + 1) * 128],
                                                                         kT[pb:pb + 64, kh, t * 128:t * 128 + 256], start=True, stop=False))
                        fns.append(lambda e, g=g: e.matmul(sc[:, g, :], idb[:], maskb[:, mi, :], start=False, stop=True))
                    else:
                        i4 = (g % 2) * 2 + g // 2
                        fns.append(lambda e, i4=i4, g=g, jq=jq, pb=pb: e.matmul(sc[:, i4, :], R[pb:pb + 64, jq, t * 128:(t + 1) * 128],
                                                                         kT[pb:pb + 64, kh, t * 128:t * 128 + 256],
                                                                         start=(g < 2), stop=False, skip_group_check=True))
                if not oldmask:
                    for a_ in range(2):
                        fns.append(lambda e, a_=a_: e.matmul(ps[:, b + a_, :], idb[:], maskb2[:, mi, :], start=False, stop=True, skip_group_check=True))
                rk = [("R", jq, i) for jq in (2 * kh, 2 * kh + 1) for i in its] + [("kT", kh, i) for i in range(nn)] + [("kT", kh, "prev"), "idb", "maskb"]
                S.op("tensor", fns, reads=rk, writes=[("ps", b), ("ps", b + 1)])
                S.reserved.update((b, b + 1))
                st_[u] = (b, sc, its)

            def stage_b1(u):
                t, kh = u
                b, sc, its = st_[u]
                s0 = rot("sm", 8)
                mx, ng, ssum, dd = sm[:, s0, 0, :], sm[:, s0, 1, :], sm[:, s0, 2, :], sm[:, s0, 3, :]
                S.op("vector", lambda e: e.tensor_reduce(mx, sc, AX.X, ALU.max), reads=[("ps", b), ("ps", b + 1)], writes=[("sm", s0, 0)])
                S.op("vector", lambda e: e.scalar_tensor_tensor(out=ng, in0=mx, scalar=-SCALE, in1=nsnk[:, l, 4 * kh:4 * kh + 4],
                                                                op0=ALU.mult, op1=ALU.min),
                     reads=[("sm", s0, 0), "nsnk"], writes=[("sm", s0, 1)])
                S.op("vector", lambda e: e.memset(ssum, 0.0), writes=[("sm", s0, 2)])
                S.op("scalar", lambda e: e.activation(out=dd, in_=ng, func=AF.Exp), reads=[("sm", s0, 1)], writes=[("sm", s0, 3)])
                S.op("vector", lambda e: e.tensor_tensor(dd, dd, esnk[:, l, 4 * kh:4 * kh + 4], ALU.mult), reads=[("sm", s0, 3), "esnk"], writes=[("sm", s0, 3)])
                st_[u] = (b, sc, its, s0)

            def stage_b2(u):
                t, kh = u
                b, sc, its, s0 = st_[u]
                ng, ssum = sm[:, s0, 1, :], sm[:, s0, 2, :]
                eb = rot("es", 3)
                for g in range(4):
                    S.op("scalar", lambda e, g=g: e.activation(
                        out=esb[:, eb, g, :], in_=sc[:, g, :], func=AF.Exp, bias=ng[:, g:g + 1], scale=SCALE, accum_out=ssum[:, g:g + 1]),
                        reads=[("ps", b), ("ps", b + 1), ("sm", s0, 1), ("sm", s0, 2)], writes=[("esb", eb, g), ("sm", s0, 2)])
                S.reserved.difference_update((b, b + 1))
                st_[u] = (b, sc, its, s0, eb)

            def stage_b3(u):
                t, kh = u
                b, sc, its, s0, eb = st_[u]
                ssum, dd = sm[:, s0, 2, :], sm[:, s0, 3, :]
                S.op("vector", lambda e: e.tensor_tensor(dd, dd, ssum, ALU.add), reads=[("sm", s0, 3), ("sm", s0, 2)], writes=[("sm", s0, 3)])
                S.op("vector", lambda e: e.reciprocal(dd, dd), reads=[("sm", s0, 3)], writes=[("sm", s0, 3)])
                S.op("gpsimd", lambda e: e.tensor_tensor(pbf[:, eb], esb[:, eb], dd[:, :, None].broadcast_to([128, 4, 256]), ALU.mult),
                     reads=[("esb", eb, g) for g in range(4)] + [("sm", s0, 3)], writes=[("pbf", eb)])
                st_[u] = (b, sc, its, eb)

            def stage_c(u):
                t, kh = u
                b, sc, its, eb = st_.pop(u)
                bt = S.bank()
                ptv = ps[:, bt, :].bitcast(BF16).rearrange("p (g k q) -> p g k q", g=4, k=2)
                fns = []
                for g in range(4):
                    for blk in range(2):
                        fns.append(lambda e, g=g, blk=blk: e.transpose(ptv[:, g, blk, :], pbf[:, eb, g, blk * 128:(blk + 1) * 128], idb[:]))
                S.op("tensor", fns, reads=[("pbf", eb), "idb"], writes=[("ps", bt)])
                pb_ = rot("kv", 2)
                copy_op("vector", pT[:, pb_], ptv, reads=[("ps", bt)], writes=[("pT", pb_)])
                bo = S.bank()
                fns = []
                for g in range(4):
                    for blk in range(2):
                        fns.append(lambda e, g=g, blk=blk: e.matmul(ps[:, bo, g * 128:(g + 1) * 128], Vt[:, t + blk, kh, :], pT[:, pb_, g, blk, :],
                                                                    start=(blk == 0), stop=(blk == 1)))
                S.op("tensor", fns, reads=[("pT", pb_), ("Vt", t), ("Vt", t + 1)], writes=[("ps", bo)])
                eng_ = evac_eng()
                for hh in range(2):
                    pb = hh * 64
                    copy_op(eng_, aT[pb:pb + 64, 2 * kh:2 * kh + 2, t * 128:(t + 1) * 128],
                            ps[pb:pb + 64, bo, :].rearrange("p (u j q) -> p u j q", u=2, j=2)[:, hh, :, :],
                            reads=[("ps", bo)], writes=[("aT", 2 * kh, i) for i in its] + [("aT", 2 * kh + 1, i) for i in its])

            nu = len(units)
            stage_a(units[0])
            if nu > 1:
                stage_a(units[1])
            stage_b1(units[0])
            stage_b2(units[0])
            for i, u in enumerate(units):
                if i + 2 < nu:
                    stage_a(units[i + 2])
                if i + 1 < nu:
                    stage_b1(units[i + 1])
                stage_b3(u)
                if i + 1 < nu:
                    stage_b2(units[i + 1])
                stage_c(u)
            S.op("vector", lambda e: e.tensor_copy(kprev[:, l], kT[:, :, tc:tc + 128]),
                 reads=[("kT", kh, i) for kh in range(4) for i in range(nn)], writes=[("kprev", l)])
            S.op("vector", lambda e: e.tensor_copy(vprev[:, l], Vt[:, ntl]), reads=[("Vt", ntl)], writes=[("vprev", l)])

        def attention_sample(v, l):
            bo = S.bank()
            S.reserved.add(bo)
            S.op("vector", lambda e: e.memset(R[:, 0:8, NS:32], 0.0), writes=[("R", j, 0) for j in range(8)], reads=[("R", j, 0) for j in range(8)])
            for bq in range(NS):
              def unit(bq=bq):
                st = rot("kv", 2)
                S.dma("sync", kvst[:, st, 0, :], ck[l, bq, :, :], "kvk%d" % st, writes=[("kvst", st, 0)])
                S.dma("sync", kvst[:, st, 1, :], cv[l, bq, :, :], "kvv%d" % st, writes=[("kvst", st, 1)])
                S.dma("sync", k_s[l, bq, 0:127, :], kvst[1:128, st, 0, :], "ksok%d" % st, reads=[("kvst", st, 0)], writes=["OUT"])
                S.dma("sync", v_s[l, bq, 0:127, :], kvst[1:128, st, 1, :], "ksov%d" % st, reads=[("kvst", st, 1)], writes=["OUT"])
                S.op("vector", lambda e, st=st: e.tensor_copy(kdup[:, st].rearrange("p k (u d) -> p k u d", u=2),
                                                            kvst[:, st, 0, :].rearrange("p (k d) -> p k d", k=4)[:, :, None, :].broadcast_to([128, 4, 2, 64])),
                     reads=[("kvst", st, 0)], writes=[("kdup", st)])
                S.op("vector", lambda e, st=st: e.tensor_copy(Vb[:, st].rearrange("p k (u d) -> p k u d", u=2),
                                                            kvst[:, st, 1, :].rearrange("p (k d) -> p k d", k=4)[:, :, None, :].broadcast_to([128, 4, 2, 64])),
                     reads=[("kvst", st, 1)], writes=[("Vb", st)])
                bt = S.bank()
                S.op("tensor", [lambda e, kh=kh: e.transpose(ps[:, bt, kh * 128:(kh + 1) * 128], kdup[:, st, kh, :], idf[:]) for kh in range(4)],
                     reads=[("kdup", st), "idf"], writes=[("ps", bt)])
                copy_op(evac_eng(), KTb[:, st], ps[:, bt, :].rearrange("p (k s) -> p k s", k=4), reads=[("ps", bt)], writes=[("KTb", st)])
                b = S.bank(2)
                scv = ps[:, b:b + 2, :].rearrange("p a (k c) -> p (a k) c", k=2)
                fns = []
                for pbsel in range(2):
                    for kh in range(4):
                        for g in (pbsel, pbsel + 2):
                            h = 4 * kh + g
                            jq, pb = h // 2, (h % 2) * 64
                            fns.append(lambda e, kh=kh, g=g, jq=jq, pb=pb: e.matmul(
                                scv[32 * g:32 * g + 32, kh, 0:128], R[pb:pb + 64, jq, 0:32], KTb[pb:pb + 64, st, kh, :],
                                start=True, stop=True, tile_position=(pb, 32 * g)))
                            fns.append(lambda e, kh=kh, g=g, jq=jq, pb=pb: e.matmul(
                                scv[32 * g:32 * g + 32, kh, 128:144], R[pb:pb + 64, jq, 0:32], kT[pb:pb + 64, kh, 128:128 + NS],
                                start=True, stop=True, tile_position=(pb, 32 * g)))
                S.op("tensor", fns, reads=[("R", j, 0) for j in range(8)] + [("KTb", st)] + [("kT", kh, 0) for kh in range(4)],
                     writes=[("ps", b), ("ps", b + 1)])
                eb = rot("es", 2)
                S.op("vector", lambda e, eb=eb: e.tensor_tensor(ess[:, eb, :, 128:144], scv[:, :, 128:144], masksb[:], ALU.add),
                     reads=[("ps", b), ("ps", b + 1), "masksb"], writes=[("essn", eb)])
                s0 = rot("sm", 8)
                mx, ng, ssum, dd = sm[:, s0, 0, :], sm[:, s0, 1, :], sm[:, s0, 2, :], sm[:, s0, 3, :]
                m2, s2 = sm[:, s0, 4, :], sm[:, s0, 5, :]
                S.op("vector", lambda e, mx=mx: e.tensor_reduce(mx, scv[:, :, 0:128], AX.X, ALU.max), reads=[("ps", b), ("ps", b + 1)], writes=[("sm", s0, 0)])
                S.op("vector", lambda e, m2=m2, eb=eb: e.tensor_reduce(m2, ess[:, eb, :, 128:144], AX.X, ALU.max), reads=[("essn", eb)], writes=[("sm", s0, 4)])
                S.op("vector", lambda e, mx=mx, m2=m2: e.tensor_tensor(mx, mx, m2, ALU.max), reads=[("sm", s0, 0), ("sm", s0, 4)], writes=[("sm", s0, 0)])
                S.op("vector", lambda e, mx=mx, ng=ng: e.scalar_tensor_tensor(out=ng, in0=mx, scalar=-SCALE, in1=nsnkc[:, l, :], op0=ALU.mult, op1=ALU.min),
                     reads=[("sm", s0, 0), "nsnkc"], writes=[("sm", s0, 1)])
                S.op("vector", lambda e, ssum=ssum, s2=s2: e.memset(sm[:, s0, 2:4, :], 0.0), writes=[("sm", s0, 2), ("sm", s0, 3)])
                S.op("vector", lambda e, s2=s2: e.memset(s2, 0.0), writes=[("sm", s0, 5)])
                for kh in range(4):
                    S.op("scalar", lambda e, kh=kh, eb=eb, ng=ng, ssum=ssum: e.activation(
                        out=ess[:, eb, kh, 0:128], in_=scv[:, kh, 0:128], func=AF.Exp, bias=ng[:, kh:kh + 1], scale=SCALE, accum_out=ssum[:, kh:kh + 1]),
                        reads=[("ps", b), ("ps", b + 1), ("sm", s0, 1), ("sm", s0, 2)], writes=[("essc", eb, kh), ("sm", s0, 2)])
                    S.op("scalar", lambda e, kh=kh, eb=eb, ng=ng, s2=s2: e.activation(
                        out=ess[:, eb, kh, 128:144], in_=ess[:, eb, kh, 128:144], func=AF.Exp, bias=ng[:, kh:kh + 1], scale=SCALE, accum_out=s2[:, kh:kh + 1]),
                        reads=[("essn", eb), ("sm", s0, 1), ("sm", s0, 5)], writes=[("essn", eb), ("sm", s0, 5)])
                S.op("vector", lambda e, dd=dd, ng=ng: e.tensor_tensor(dd, snkc[:, l, :], ng, ALU.add), reads=["snkc", ("sm", s0, 1)], writes=[("sm", s0, 3)])
                S.op("scalar", lambda e, dd=dd: e.activation(out=dd, in_=dd, func=AF.Exp), reads=[("sm", s0, 3)], writes=[("sm", s0, 3)])
                S.op("vector", lambda e, dd=dd, ssum=ssum: e.tensor_tensor(dd, dd, ssum, ALU.add), reads=[("sm", s0, 3), ("sm", s0, 2)], writes=[("sm", s0, 3)])
                S.op("vector", lambda e, dd=dd, s2=s2: e.tensor_tensor(dd, dd, s2, ALU.add), reads=[("sm", s0, 3), ("sm", s0, 5)], writes=[("sm", s0, 3)])
                S.op("vector", lambda e, dd=dd: e.reciprocal(dd, dd), reads=[("sm", s0, 3)], writes=[("sm", s0, 3)])
                S.op("vector", lambda e, dd=dd, bq=bq: e.tensor_scalar(dd, dd, rowselt[:, bq:bq + 1], None, ALU.mult), reads=[("sm", s0, 3), "rowsel"], writes=[("sm", s0, 3)])
                S.op("vector", lambda e, dd=dd, eb=eb: e.tensor_tensor(ess[:, eb], ess[:, eb], dd[:, :, None].broadcast_to([128, 4, 144]), ALU.mult),
                     reads=[("essc", eb, kh) for kh in range(4)] + [("essn", eb), ("sm", s0, 3)], writes=[("essc", eb, kh) for kh in range(4)] + [("essn", eb)])
                btp = S.bank()
                S.op("tensor", [lambda e, kh=kh: e.transpose(ps[:, btp, kh * 128:(kh + 1) * 128], ess[:, eb, kh, 0:128], idf[:]) for kh in range(4)],
                     reads=[("essc", eb, kh) for kh in range(4)] + ["idf"], writes=[("ps", btp)])
                copy_op(evac_eng(), pTs[:, st], ps[:, btp, :].rearrange("p (k s) -> p k s", k=4), reads=[("ps", btp)], writes=[("pTs", st)])
                bt2 = S.bank()
                S.op("tensor", [lambda e, kh=kh: e.transpose(ps[0:NS, bt2, kh * 128:(kh + 1) * 128], ess[:, eb, kh, 128:144], idf[:]) for kh in range(4)],
                     reads=[("essn", eb), "idf"], writes=[("ps", bt2)])
                copy_op(evac_eng(), pTn[:, st], ps[0:NS, bt2, :].rearrange("p (k s) -> p k s", k=4), reads=[("ps", bt2)], writes=[("pTn", st)])
                fns = []
                for kh in range(4):
                    fns.append(lambda e, kh=kh: e.matmul(ps[:, bo, kh * 128:(kh + 1) * 128], Vb[:, st, kh, :], pTs[:, st, kh, :],
                                                         start=(bq == 0 and kh == 0), stop=False, skip_group_check=True))
                    fns.append(lambda e, kh=kh: e.matmul(ps[:, bo, kh * 128:(kh + 1) * 128], Vnew[:, kh, :], pTn[:, st, kh, :],
                                                         start=False, stop=(bq == NS - 1 and kh == 3), skip_group_check=True))
                S.op("tensor", fns, reads=[("Vb", st), ("pTs", st), ("pTn", st), "Vnew"], writes=[("ps", bo)])
              unit()
            S.reserved.discard(bo)
            ov = ps[:, bo, :].rearrange("p (k g s) -> p k g s", k=4, g=4)
            for kh in range(4):
                for g in range(4):
                    h = 4 * kh + g
                    jq, pb = h // 2, (h % 2) * 64
                    copy_op("vector", aT[pb:pb + 64, jq, 0:NS], ov[pb:pb + 64, kh, g, 0:NS], reads=[("ps", bo)], writes=[("aT", jq, 0)])

        def gates_out(v, l):
            for j in range(8):
                w1, f1 = load_w(w_pw[l, :, j * 128:(j + 1) * 128])
                w2, f2 = load_w(w_in[l, :, 3 * D + 512 + j * 128:3 * D + 512 + (j + 1) * 128])
                w3, f3 = load_w(w_ao[l, :, j * 128:(j + 1) * 128])
                w4, f4 = load_w(w_in[l, :, 4 * D + 512 + j * 128:4 * D + 512 + (j + 1) * 128])
                for i, (c0, n) in enumerate(v.nts):
                    b1 = proj(w1, f1, v.s, kR(16, i), c0, n)
                    b2 = proj(w2, f2, v.xn, kxn(i), c0, n)
                    t1 = rot("tmp", 4)
                    S.op("scalar", lambda e, b2=b2, t1=t1, n=n: e.activation(out=tmp[:, t1, 0:n], in_=ps[:, b2, 0:n], func=AF.Sigmoid),
                         reads=[("ps", b2)], writes=[("tmp", t1)])
                    S.op("vector", lambda e, b1=b1, t1=t1, n=n: e.tensor_tensor(tmp[:, t1, 0:n], ps[:, b1, 0:n], tmp[:, t1, 0:n], ALU.mult),
                         reads=[("ps", b1), ("tmp", t1)], writes=[("tmp", t1)])
                    b3 = proj(w3, f3, v.a, ka(i), c0, n)
                    b4 = proj(w4, f4, v.xn, kxn(i), c0, n)
                    t2 = rot("tmp", 4)
                    S.op("scalar", lambda e, b4=b4, t2=t2, n=n: e.activation(out=tmp[:, t2, 0:n], in_=ps[:, b4, 0:n], func=AF.Sigmoid),
                         reads=[("ps", b4)], writes=[("tmp", t2)])
                    S.op("vector", lambda e, b3=b3, t2=t2, n=n: e.tensor_tensor(tmp[:, t2, 0:n], ps[:, b3, 0:n], tmp[:, t2, 0:n], ALU.mult),
                         reads=[("ps", b3), ("tmp", t2)], writes=[("tmp", t2)])
                    S.op("vector", lambda e, t1=t1, t2=t2, j=j, c0=c0, n=n: e.tensor_tensor(v.m(j, c0, n), tmp[:, t1, 0:n], tmp[:, t2, 0:n], ALU.add),
                         reads=[("tmp", t1), ("tmp", t2)], writes=[("R", 8 + j, i)])
            for j in range(8):
                wo, fo = load_w(w_out[l, :, j * 128:(j + 1) * 128])
                for i, (c0, n) in enumerate(v.nts):
                    b = proj(wo, fo, v.m, kR(8, i), c0, n)
                    S.op("vector", lambda e, b=b, j=j, c0=c0, n=n: e.tensor_tensor(xT[:, j, c0:c0 + n], xT[:, j, c0:c0 + n], ps[:, b, 0:n], ALU.add),
                         reads=[("ps", b), ("xT", j, i)], writes=[("xT", j, i)])

        def ffn(v, l, sc_first, sc_last, sample):
            tc = v.tc
            nn = len(v.nts)
            if sample:
                sl = rot("xs", 2)
                for k in range(2):
                    S.dma("sync", xstage[k * NS:(k + 1) * NS, sl, 0:1024], sffn[l, :, k, 0:1024], "xs%d" % sl, writes=[("xstage", sl)])
                sl2 = rot("xs", 2)
                for k in range(2):
                    S.dma("sync", xstage[k * NS:(k + 1) * NS, sl2, 0:1024], sffn[l, :, k, 1024:2048], "xs%d" % sl2, writes=[("xstage", sl2)])
                S.dma("sync", ffn_s[l, :, 0, :], sffn[l, :, 1, :], "ffs", writes=["OUT"])
                for part, slp in ((0, sl), (1, sl2)):
                    for g0 in range(0, 8, 4):
                        b = S.bank()
                        S.op("tensor", [lambda e, jj=jj, b=b, slp=slp: e.transpose(ps[:, b, (jj % 4) * 32:(jj % 4 + 1) * 32], xstage[0:32, slp, jj * 128:(jj + 1) * 128], idf[0:32, 0:32])
                                        for jj in range(g0, g0 + 4)], reads=[("xstage", slp), "idf"], writes=[("ps", b)])
                        eng_ = evac_eng()
                        for jj in range(g0, g0 + 4):
                            copy_op(eng_, hTs[:, part * 8 + jj, 0:2, :], ps[:, b, (jj % 4) * 32:(jj % 4 + 1) * 32].rearrange("p (k b) -> p k b", k=2),
                                    reads=[("ps", b)], writes=[("hTs", part * 8 + jj)])
                sl3 = rot("xs", 2)
                for k in range(2):
                    S.dma("sync", xstage[k * NS:(k + 1) * NS, sl3, 0:768], sffn[l, :, k, 2048:2816], "xs%d" % sl3, writes=[("xstage", sl3)])
                for g0 in range(0, 6, 3):
                    b = S.bank()
                    S.op("tensor", [lambda e, jj=jj, b=b: e.transpose(ps[:, b, (jj % 3) * 32:(jj % 3 + 1) * 32], xstage[0:32, sl3, jj * 128:(jj + 1) * 128], idf[0:32, 0:32])
                                    for jj in range(g0, g0 + 3)], reads=[("xstage", sl3), "idf"], writes=[("ps", b)])
                    eng_ = evac_eng()
                    for jj in range(g0, g0 + 3):
                        copy_op(eng_, hTs[:, 16 + jj, 0:2, :], ps[:, b, (jj % 3) * 32:(jj % 3 + 1) * 32].rearrange("p (k b) -> p k b", k=2),
                                reads=[("ps", b)], writes=[("hTs", 16 + jj)])
            rmsnorm_to_xn(v, l, O_NF)
            for jf in range(NFC):
                wh, fh = load_w(w_up[l, :, jf * 128:(jf + 1) * 128])
                wg, fg = load_w(w_up[l, :, DFF + jf * 128:DFF + (jf + 1) * 128])
                db_ = jf % 2
                S.op("vector", lambda e, jf=jf, db_=db_: e.tensor_tensor(
                    dgf[:, db_], idb[:, None, :].broadcast_to([128, 3, 128]),
                    vec[:, l * LV + O_FDW + jf * 3:l * LV + O_FDW + (jf + 1) * 3, None].broadcast_to([128, 3, 128]), ALU.mult),
                    reads=["idb", "vec"], writes=[("dgf", db_)])
                hb = jf % 2
                if not sample:
                    S.op("vector", lambda e, jf=jf, hb=hb: e.tensor_copy(hT[:, hb, 0:2], hhist[:, l, jf, :]),
                         reads=[("hhist", l, jf)], writes=[("hT", hb, "h")])
                bgs = []
                for i, (c0, n) in enumerate(v.nts):
                    bh = proj(wh, fh, v.xn, kxn(i), c0, n)
                    if sample:
                        S.op("scalar", lambda e, bh=bh, jf=jf: e.activation(out=hf32[:, jf, 0:NS], in_=ps[:, bh, 0:NS], func=AF.Copy),
                             reads=[("ps", bh)], writes=[("hf32", jf)])
                        S.op("vector", lambda e, jf=jf: e.tensor_copy(hTs[:, jf, 2, :], hf32[:, jf, 0:NS]), reads=[("hf32", jf)], writes=[("hTs", jf)])
                    else:
                        S.op("scalar", lambda e, bh=bh, hb=hb, c0=c0, n=n: e.activation(out=hT[:, hb, 2 + c0:2 + c0 + n], in_=ps[:, bh, 0:n], func=AF.Copy),
                             reads=[("ps", bh)], writes=[("hT", hb, i)])
                        if sc_last and c0 + n == tc:
                            S.op("scalar", lambda e, bh=bh, jf=jf, n=n: e.activation(out=hf32[:, jf, 0:2], in_=ps[:, bh, n - 2:n], func=AF.Copy), reads=[("ps", bh)], writes=[("hf32", jf)])
                if not sample:
                    if sc_first:
                        S.op("vector", lambda e, hb=hb: e.tensor_tensor(hT[:, hb, 2:2 + 384], hT[:, hb, 2:2 + 384], validb[:], ALU.mult),
                             reads=[("hT", hb, i) for i in range(nn)] + ["validb"], writes=[("hT", hb, i) for i in range(nn)])
                    S.op("vector", lambda e, jf=jf, hb=hb: e.tensor_copy(hhist[:, l, jf, :], hT[:, hb, tc:tc + 2]),
                         reads=[("hT", hb, i) for i in range(nn)] + [("hT", hb, "h")], writes=[("hhist", l, jf)])
                for i, (c0, n) in enumerate(v.nts):
                    bg = proj(wg, fg, v.xn, kxn(i), c0, n)
                    bc = S.bank()
                    if sample:
                        pairs = [(dgf[:, db_, k, :], hTs[:, jf, k, :]) for k in range(3)]
                        rd = [("dgf", db_), ("hTs", jf)]
                    else:
                        pairs = [(dgf[:, db_, k, :], hT[:, hb, c0 + k:c0 + k + n]) for k in range(3)]
                        rd = [("dgf", db_), ("hT", hb, "h")] + [("hT", hb, ii) for ii in range(max(0, i - 1), i + 1)]
                    mm_group(ps[:, bc, 0:n], pairs, reads=rd, bank_keys=[("ps", bc)])
                    ts = rot("tmp", 4)
                    S.op("scalar", lambda e, bc=bc, ts=ts, jf=jf, n=n: e.activation(out=tmp[:, ts, 0:n], in_=ps[:, bc, 0:n], func=AF.Gelu,
                                                                                  bias=vcol(l, O_FDB, jf), scale=1.0),
                         reads=[("ps", bc), "vec"], writes=[("tmp", ts)])
                    S.op("vector", lambda e, bg=bg, ts=ts, jf=jf, c0=c0, n=n: e.tensor_tensor(v.hg(jf, c0, n), tmp[:, ts, 0:n], ps[:, bg, 0:n], ALU.mult),
                         reads=[("ps", bg), ("tmp", ts)], writes=[("R", jf, i)])
            for j in range(8):
                wd_, fd = load_w(w_dn[l, :, j * 128:(j + 1) * 128], k_chunks=NFC)
                for i, (c0, n) in enumerate(v.nts):
                    b = proj(wd_, fd, v.hg, kR(0, i, NFC), c0, n, kc=NFC)
                    S.op("vector", lambda e, b=b, j=j, c0=c0, n=n: e.tensor_tensor(xT[:, j, c0:c0 + n], xT[:, j, c0:c0 + n], ps[:, b, 0:n], ALU.add),
                         reads=[("ps", b), ("xT", j, i)], writes=[("xT", j, i)])

        def final_out(v, t0, ntl, sample):
            cols = [(0, NS, NOWN * 128)] if sample else [(t * 128, 128, (t0 + t - 3) * 128) for t in range(ntl) if t0 + t >= 3]
            for (c0, n, row0) in cols:
                its = nts_of(v, c0, c0 + n)
                kk = [("xT", j, i) for j in range(8) for i in its]
                sl = rms_stats(lambda c0_, n_: xT[:, :, c0_:c0_ + n_], kk, c0, n)
                for j in range(8):
                    S.op("vector", lambda e, j=j, c0=c0, n=n, sl=sl: e.scalar_tensor_tensor(
                        out=ytmp[:, j, 0:n], in0=xT[:, j, c0:c0 + n], scalar=vec[:, 2 * LV + j:2 * LV + j + 1],
                        in1=st1[:, sl, 0:n], op0=ALU.mult, op1=ALU.mult),
                        reads=kk + [("st1", sl), "vec"], writes=[("ytmp", j)])
                xsl = rot("xs", 2)
                for half in range(2):
                    b = S.bank()
                    S.op("tensor", [lambda e, j=j, b=b, n=n: e.transpose(ps[0:n, b, (j % 4) * 128:(j % 4 + 1) * 128], ytmp[:, j, 0:n], idf[:])
                                    for j in range(half * 4, half * 4 + 4)], reads=[("ytmp", j) for j in range(8)] + ["idf"], writes=[("ps", b)])
                    copy_op(evac_eng(), xstage[0:n, xsl, half * 512:(half + 1) * 512], ps[0:n, b, :], reads=[("ps", b)], writes=[("xstage", xsl)])
                S.dma("sync", y[row0:row0 + n, :], xstage[0:n, xsl, :], "xs%d" % xsl, reads=[("xstage", xsl)], writes=["OUT"])

        for l in range(2):
            S.dma("sync", conv_s[l, :, 0:29, :], sconv[l, :, 1:30, :], "cvs", writes=["OUT"])

        sc_list = [(t0, ntl, False) for (t0, ntl) in SCS] + [(0, 0, True)]
        if dbg == "sample_l0":
            sc_list = [(0, 0, True)]
        if dbg == "sc0":
            sc_list = sc_list[:1]
        for si, (t0, ntl, sample) in enumerate(sc_list):
            if sample:
                S.barrier()
            S.new_phase()
            tc = NS if sample else ntl * 128
            v = make_views(tc)
            sc_first = (si == 0)
            sc_last = (si == len(SCS) - 1)
            import os as _os
            kstop = int(_os.environ.get("KSTOP", "99"))
            if kstop <= 0:
                break
            load_x(v, t0, ntl, sample)
            for l in range(2):
                if kstop <= 1:
                    break
                rmsnorm_to_xn(v, l, O_NM)
                if kstop <= 2:
                    break
                conv_branch(v, l, sc_first, sc_last, sample, part=0)
                q_proj(v, l)
                conv_branch(v, l, sc_first, sc_last, sample, part=1)
                if kstop <= 3:
                    break
                if sc_last or sample:
                    if sample:
                        emit_rows(l, lambda j: uf32[:, j, 0:NS], NS, 8, lambda c, w: conv_s[l, :, 29, c:c + w], [("uf32", j) for j in range(8)])
                    else:
                        emit_rows(l, lambda j: uf32[:, j, :], 30, 8, lambda c, w: conv_p[l, :, c:c + w], [("uf32", j) for j in range(8)])
                if kstop <= 4:
                    break
                qkv(v, l, sc_last, sample)
                if kstop <= 5:
                    break
                if sample:
                    attention_sample(v, l)
                else:
                    attention(v, l, t0)
                if kstop <= 6:
                    break
                gates_out(v, l)
                if dbg == "sample_l0":
                    break
                ffn(v, l, sc_first, sc_last, sample)
                if sc_last or sample:
                    if sample:
                        emit_rows(l, lambda j: hf32[:, j, 0:NS], NS, NFC, lambda c, w: ffn_s[l, :, 1, c:c + w], [("hf32", j) for j in range(NFC)])
                    else:
                        emit_rows(l, lambda j: hf32[:, j, 0:2], 2, NFC, lambda c, w: ffn_p[l, :, c:c + w], [("hf32", j) for j in range(NFC)])
            if dbg is None:
                final_out(v, t0, ntl, sample)

        waits = S._deps("sync", ["OUT"], [])
        allw = []
        for c in S.chan.values():
            allw.append((c[0], c[1]))
        S.ops["sync"].append((allw, [], None, ""))
        S.emit()
    return nc


_NC_CACHE = {}


def _host_layout(inputs):
    f = lambda a: np.ascontiguousarray(np.asarray(a, dtype=np.float32))
    xp = f(inputs["x_prompt"])[0]
    xsamp = f(inputs["x_sample"])[:, 0, :]
    meta = f(inputs["meta_tokens"])
    seq = np.concatenate([np.zeros((256 + 112, D), np.float32), meta, xp], axis=0)
    vecs = np.zeros((128, NV), np.float32)

    def colmaj(a):
        return a.reshape(-1, 128).T

    for l in range(2):
        o = l * LV
        vecs[:, o + O_NM:o + O_NM + 8] = colmaj(f(inputs["norm_mix"])[l])
        vecs[:, o + O_CDB:o + O_CDB + 8] = colmaj(f(inputs["conv_db"])[l])
        vecs[:, o + O_LG:o + O_LG + 8] = colmaj(f(inputs["conv_ln_g"])[l])
        vecs[:, o + O_LB:o + O_LB + 8] = colmaj(f(inputs["conv_ln_b"])[l])
        vecs[:, o + O_NF:o + O_NF + 8] = colmaj(f(inputs["norm_ffn"])[l])
        vecs[:, o + O_FDB:o + O_FDB + 22] = colmaj(f(inputs["ffn_db"])[l])
        cdw = f(inputs["conv_dw"])[l]
        vecs[:, o + O_CDW:o + O_CDW + 248] = cdw.T.reshape(8, 128, 31).transpose(1, 0, 2).reshape(128, 248)
        fdw = f(inputs["ffn_dw"])[l]
        vecs[:, o + O_FDW:o + O_FDW + 66] = fdw.T.reshape(22, 128, 3).transpose(1, 0, 2).reshape(128, 66)
    vecs[:, 2 * LV:2 * LV + 8] = colmaj(f(inputs["norm_final"]))
    sinks = f(inputs["attn_sinks"])
    perm = np.array([4 * k + g for k in range(4) for g in (0, 2, 1, 3)])
    sinkrow = np.ascontiguousarray(np.broadcast_to(sinks[None][:, :, perm], (128, 2, 16)))
    sinkcol = np.zeros((128, 2, 4), np.float32)
    for r in range(128):
        for kh in range(4):
            sinkcol[r, :, kh] = sinks[:, 4 * kh + r // 32]
    masks_s = np.full((128, 16), NEGM, np.float32)
    rowsel = np.zeros((128, 16), np.float32)
    for r in range(128):
        s = r % 32
        if s < 16:
            masks_s[r, s] = 0.0
            rowsel[r, s] = 1.0
    qi = np.arange(128)[:, None]
    kj = np.arange(256)[None, :]
    in_maps = []
    for c in range(NCORES):
        b0 = 16 * c - 2
        xin = seq[(b0 + 2) * 128:(b0 + 2 + NT) * 128]
        m = np.zeros((128, 5, 256), np.float32)
        for mi in range(5):
            gi = mi if mi < 4 else 8
            qpos = (b0 + gi) * 128 + qi
            kpos = (b0 + gi - 1) * 128 + kj
            rel = qpos - kpos
            ok = (rel >= 0) & (rel <= 128) & (kpos >= 112)
            if mi == 0:
                ok = ok & (kj >= 128)
            m[:, mi, :] = np.where(ok, 0.0, NEGM)
        pos = (b0 * 128 + np.arange(384))
        valid = np.ascontiguousarray(np.broadcast_to((pos >= 112).astype(np.float32)[None], (128, 384)))
        sl = slice(NS * c, NS * (c + 1))
        in_maps.append({
            "xin": np.ascontiguousarray(xin), "xs": np.ascontiguousarray(xsamp[sl]),
            "masks": m, "masks_s": masks_s, "rowsel": rowsel, "valid": valid, "vecs": vecs,
            "sinkrow": sinkrow, "sinkcol": sinkcol,
            "w_in": f(inputs["w_in"]), "w_pw": f(inputs["w_conv_pw"]), "w_ao": f(inputs["w_attn_o"]),
            "w_out": f(inputs["w_out"]), "w_up": f(inputs["w_ffn_up"]), "w_dn": f(inputs["w_ffn_down"]),
            "ck": np.ascontiguousarray(f(inputs["cache_swa_k"])[:, sl].reshape(2, NS, 128, 256)),
            "cv": np.ascontiguousarray(f(inputs["cache_swa_v"])[:, sl].reshape(2, NS, 128, 256)),
            "sconv": np.ascontiguousarray(f(inputs["state_conv"])[:, sl]),
            "sffn": np.ascontiguousarray(f(inputs["state_ffn_conv"])[:, sl]),
        })
    return in_maps


def kernel(**inputs):
    in_maps = _host_layout(inputs)
    if "nc" not in _NC_CACHE:
        _NC_CACHE["nc"] = build()
    nc = _NC_CACHE["nc"]
    res = run_bass_kernel_spmd(nc, in_maps, core_ids=list(range(NCORES)))
    r = res.results
    y_prompt = np.concatenate([r[c]["y"][:NOWN * 128] for c in range(NCORES)], axis=0)[None]
    y_sample = np.concatenate([r[c]["y"][NOWN * 128:] for c in range(NCORES)], axis=0)[:, None, :]
    last = r[NCORES - 1]
    k_p = last["kv_p"][:, 0].reshape(2, 1, 128, 4, 64)
    v_p = last["kv_p"][:, 1].reshape(2, 1, 128, 4, 64)
    c_p = last["conv_p"].reshape(2, 1, 30, D)
    f_p = last["ffn_p"].reshape(2, 1, 2, DFF)
    k_s = np.concatenate([r[c]["k_s"] for c in range(NCORES)], axis=1).reshape(2, 128, 128, 4, 64)
    v_s = np.concatenate([r[c]["v_s"] for c in range(NCORES)], axis=1).reshape(2, 128, 128, 4, 64)
    c_s = np.concatenate([r[c]["conv_s"] for c in range(NCORES)], axis=1)
    f_s = np.concatenate([r[c]["ffn_s"] for c in range(NCORES)], axis=1)
    f32 = lambda a: np.ascontiguousarray(a, dtype=np.float32)
    return (f32(y_prompt), f32(y_sample), f32(k_p), f32(v_p), f32(c_p), f32(f_p), f32(k_s), f32(v_s), f32(c_s), f32(f_s))
```

```python
import numpy as np
from contextlib import ExitStack
import concourse.bass as bass
import concourse.mybir as mybir
from concourse.bass_utils import run_bass_kernel_spmd

F32 = mybir.dt.float32
BF16 = mybir.dt.bfloat16
AF = mybir.ActivationFunctionType
ALU = mybir.AluOpType
AX = mybir.AxisListType

NCORES = 8
NT = 19
NOWN = 16
SCS = [(0, 5), (5, 5), (10, 5), (15, 4)]
TCM = 640
NS = 16
D = 1024
DFF = 2816
NFC = 22
DIN = 5632
EPS = 1e-6
SCALE = 0.125
LV = 8 * 5 + 22 + 8 * 31 + 22 * 3
NV = 2 * LV + 8
O_NM, O_CDB, O_LG, O_LB, O_NF, O_FDB, O_CDW, O_FDW = 0, 8, 16, 24, 32, 40, 62, 62 + 248
NEGM = -30000.0

ENGS = ("tensor", "vector", "scalar", "gpsimd", "sync")


class Sched:
    def __init__(self, nc, stack):
        self.nc = nc
        self.stack = stack
        self.ops = {e: [] for e in ENGS}
        self.sem = {}
        self.cnt = {}
        self.seen = {e: {} for e in ENGS}
        self.res = {}
        self.chan = {}
        self.nsem = 0
        self.pb = 0
        self.reserved = set()
        self.new_phase()

    debug_tags = False
    names = []

    def _tag(self):
        if not self.debug_tags:
            return ""
        import traceback
        st = traceback.extract_stack(limit=6)
        return ">".join(str(f.lineno) for f in st[:-2])

    def _newsem(self, name):
        self.nsem += 1
        return self.stack.enter_context(self.nc.semaphore(name))

    def new_phase(self):
        for e in ("tensor", "vector", "scalar", "gpsimd"):
            self.sem[e] = self._newsem("s_%s_%d" % (e, self.nsem))
            self.cnt[e] = 0

    def bank(self, n=1):
        while True:
            if self.pb % 8 + n > 8:
                self.pb += 8 - self.pb % 8
            b = self.pb % 8
            if any((b + i) in self.reserved for i in range(n)):
                self.pb += 1
                continue
            self.pb += n
            return b

    def _deps(self, eng, reads, writes):
        evs = []
        for r in reads:
            st = self.res.get(r)
            if st is not None and st[0] is not None:
                evs.append(st[0])
            if st is not None and isinstance(r, tuple) and r[0] == "ps":
                evs.extend(ev for ev in st[1] if ev[2] != eng)
        for w in writes:
            st = self.res.get(w)
            if st is not None:
                if st[0] is not None:
                    evs.append(st[0])
                evs.extend(st[1])
        need = {}
        for (s, v, e) in evs:
            if e == "tensor" and eng == "tensor":
                continue
            k = id(s)
            if self.seen[eng].get(k, 0) >= v:
                continue
            if k not in need or need[k][1] < v:
                need[k] = (s, v)
        waits = []
        for k, (s, v) in need.items():
            self.seen[eng][k] = v
            waits.append((s, v))
        return waits

    def _commit(self, ev, reads, writes):
        for r in reads:
            st = self.res.setdefault(r, [None, []])
            st[1].append(ev)
        for w in writes:
            self.res[w] = [ev, []]

    def op(self, eng, fns, reads=(), writes=()):
        if callable(fns):
            fns = [fns]
        waits = self._deps(eng, reads, writes)
        self.cnt[eng] += 1
        ev = (self.sem[eng], self.cnt[eng], eng)
        self.ops[eng].append((waits, fns, (self.sem[eng], 1), self._tag()))
        self._commit(ev, reads, writes)

    def dma(self, queue, out, in_, chan, reads=(), writes=()):
        self.nout = getattr(self, "nout", 0) + 1
        writes = [("OUT", self.nout) if w == "OUT" else w for w in writes]
        if chan not in self.chan:
            self.chan[chan] = [self._newsem("d_%s" % chan), 0]
        c = self.chan[chan]
        waits = self._deps(queue, reads, writes)
        c[1] += 16
        ev = (c[0], c[1], "dma")
        self.ops[queue].append((waits, [lambda e: e.dma_start(out=out, in_=in_)], (c[0], 16), self._tag()))
        self._commit(ev, reads, writes)

    def barrier(self):
        evs = []
        for e in ("tensor", "vector", "scalar", "gpsimd"):
            if self.cnt[e] > 0:
                evs.append((self.sem[e], self.cnt[e]))
        for c in self.chan.values():
            evs.append((c[0], c[1]))
        for eng in ENGS:
            waits = []
            for (s, v) in evs:
                if self.seen[eng].get(id(s), 0) < v:
                    self.seen[eng][id(s)] = v
                    waits.append((s, v))
            self.ops[eng].append((waits, [], None, ""))

    def emit(self):
        nc = self.nc
        with nc.Block() as block:
            for ename in ENGS:
                lst = self.ops[ename]

                def body(e, lst=lst):
                    for waits, fns, inc, tag in lst:
                        for (s, v) in waits:
                            e.wait_ge(s, v)
                        ins = None
                        for f in fns:
                            ins = f(e)
                            if self.debug_tags:
                                ins.annotate(tag)
                                self.names.append((ins.ins.name, ename, tag, str(ins)[:300]))
                        if inc is not None and ins is not None:
                            ins.then_inc(inc[0], inc[1])

                getattr(block, ename)(body)


def ntile_list(tc):
    if tc <= 320:
        return [(0, tc)]
    h = tc // 2
    return [(0, h), (h, tc - h)]


def build(dbg=None):
    nc = bass.Bass("TRN2", target_bir_lowering=False)

    def din(name, shape):
        return nc.dram_tensor(name, shape, F32, kind="ExternalInput").ap()

    def dout(name, shape):
        return nc.dram_tensor(name, shape, F32, kind="ExternalOutput").ap()

    xin = din("xin", [NT * 128, D])
    xs = din("xs", [NS, D])
    masks = din("masks", [128, 5, 256])
    masks_s = din("masks_s", [128, 16])
    rowsel = din("rowsel", [128, 16])
    valid = din("valid", [128, 384])
    vecs = din("vecs", [128, NV])
    sinkrow = din("sinkrow", [128, 2, 16])
    sinkcol = din("sinkcol", [128, 2, 4])
    w_in = din("w_in", [2, D, DIN])
    w_pw = din("w_pw", [2, D, D])
    w_ao = din("w_ao", [2, D, D])
    w_out = din("w_out", [2, D, D])
    w_up = din("w_up", [2, D, 2 * DFF])
    w_dn = din("w_dn", [2, DFF, D])
    ck = din("ck", [2, NS, 128, 256])
    cv = din("cv", [2, NS, 128, 256])
    sconv = din("sconv", [2, NS, 30, D])
    sffn = din("sffn", [2, NS, 2, DFF])

    y = dout("y", [NOWN * 128 + NS, D])
    kv_p = dout("kv_p", [2, 2, 128, 256])
    conv_p = dout("conv_p", [2, 30, D])
    ffn_p = dout("ffn_p", [2, 2, DFF])
    k_s = dout("k_s", [2, NS, 128, 256])
    v_s = dout("v_s", [2, NS, 128, 256])
    conv_s = dout("conv_s", [2, NS, 30, D])
    ffn_s = dout("ffn_s", [2, NS, 2, DFF])

    with ExitStack() as es:
        def sb(name, shape, dt):
            return es.enter_context(nc.sbuf_tensor(name, shape, dt))

        idf = sb("idf", [128, 128], F32)
        idb = sb("idb", [128, 128], BF16)
        onesb = sb("onesb", [128, 128], BF16)
        epsc = sb("epsc", [128, 1], F32)
        vec = sb("vec", [128, NV], F32)
        maskb = sb("maskb", [128, 5, 256], BF16)
        masksb = sb("masksb", [128, 4, 16], BF16)
        rowselt = sb("rowselt", [128, 16], F32)
        validb = sb("validb", [128, 384], BF16)
        snk = sb("snk", [128, 2, 16], F32)
        nsnk = sb("nsnk", [128, 2, 16], F32)
        snkc = sb("snkc", [128, 2, 4], F32)
        nsnkc = sb("nsnkc", [128, 2, 4], F32)

        xT = sb("xT", [128, 8, TCM], F32)
        xnT = sb("xnT", [128, 8, TCM], BF16)
        R = sb("R", [128, 24, TCM], BF16)
        aT = sb("aT", [128, 8, TCM], BF16)
        kT = sb("kT", [128, 4, 128 + TCM], BF16)
        Vt = sb("Vt", [128, 6, 4, 128], BF16)
        kprev = sb("kprev", [128, 2, 4, 128], BF16)
        vprev = sb("vprev", [128, 2, 4, 128], BF16)
        uT = sb("uT", [128, 2, 30 + TCM], BF16)
        uhist = sb("uhist", [128, 2, 8, 30], BF16)
        hT = sb("hT", [128, 2, 2 + TCM], BF16)
        hhist = sb("hhist", [128, 2, 22, 2], BF16)
        NWS = 6
        wsl = sb("wsl", [128, NWS, 8, 128], BF16)
        wdsl = sb("wdsl", [128, 2, 22, 128], BF16)
        wv = sb("wv", [128, 8, 256], BF16)
        wk = sb("wk", [128, 8, 256], BF16)
        dg = sb("dg", [128, 31, 128], BF16)
        dgf = sb("dgf", [128, 2, 3, 128], BF16)
        xstage = sb("xstage", [128, 2, 1024], F32)
        ostage = sb("ostage", [128, 2, 512], F32)
        sq = sb("sq", [128, 8, 320], BF16)
        st1 = sb("st1", [128, 2, 512], F32)
        st2 = sb("st2", [128, 2, 512], F32)
        tmp = sb("tmp", [128, 4, 512], F32)
        esb = sb("esb", [128, 3, 4, 256], F32)
        esnk = sb("esnk", [128, 2, 16], F32)
        pT = sb("pT", [128, 2, 4, 2, 128], BF16)
        pbf = sb("pbf", [128, 3, 4, 256], BF16)
        maskb2 = sb("maskb2", [128, 5, 512], BF16)
        sm = sb("sm", [128, 8, 8, 4], F32)
        ytmp = sb("ytmp", [128, 8, 128], F32)
        uf32 = sb("uf32", [128, 8, 30], F32)
        hf32 = sb("hf32", [128, 22, 16], F32)
        uTs = R[:, 0:8, 64:64 + 31 * NS].rearrange("p j (k b) -> p j k b", k=31)
        hTs = R[:, 0:22, 576:576 + 3 * NS].rearrange("p j (k b) -> p j k b", k=3)
        ess = xT[:, :, 64:208].rearrange("p (a k) c -> p a k c", a=2)
        kdup = xT[:, :, 256:384].rearrange("p (a k) c -> p a k c", a=2)
        kvst = xT[:, 0:4, 384:640].rearrange("p (a b) c -> p a b c", a=2)
        pTs = aT[:, :, 64:192].rearrange("p (a k) c -> p a k c", a=2)
        KTb = aT[:, :, 192:320].rearrange("p (a k) c -> p a k c", a=2)
        Vb = aT[:, :, 320:448].rearrange("p (a k) c -> p a k c", a=2)
        pTn = aT[0:NS, :, 448:576].rearrange("p (a k) c -> p a k c", a=2)
        Vnew = xnT[0:NS, 0:4, 64:192]

        ps = es.enter_context(nc.psum_tensor("ps", [128, 8, 512], F32))

        S = Sched(nc, es)
        ctr = {"w": 0, "wd": 0, "xs": 0, "os": 0, "tmp": 0, "st": 0, "es": 0, "sm": 0, "kv": 0, "ev": 0}

        def rot(name, n):
            v = ctr[name] % n
            ctr[name] += 1
            return v

        def evac_eng():
            return "vector" if rot("ev", 2) == 0 else "scalar"

        def copy_op(eng, out, in_, reads, writes):
            if eng == "scalar":
                S.op("scalar", lambda e: e.activation(out=out, in_=in_, func=AF.Copy), reads=reads, writes=writes)
            else:
                S.op(eng, lambda e: e.tensor_copy(out, in_), reads=reads, writes=writes)

        S.dma("sync", vec[:], vecs[:, :], "c_vec", writes=["vec"])
        S.dma("sync", snk[:], sinkrow[:, :, :], "c_snk", writes=["snk"])
        S.dma("sync", snkc[:], sinkcol[:, :, :], "c_snkc", writes=["snkc"])
        S.dma("sync", rowselt[:], rowsel[:, :], "c_rs", writes=["rowsel"])
        S.dma("gpsimd", maskb[:], masks[:, :, :], "c_mask", writes=["maskb"])
        S.dma("gpsimd", validb[:], valid[:, :], "c_valid", writes=["validb"])
        for k4 in range(4):
            S.dma("gpsimd", masksb[:, k4, :], masks_s[:, :], "c_masks", writes=["masksb"])
        S.op("gpsimd", lambda e: e.memset(idf[:], 1.0), writes=["idf"])
        S.op("gpsimd", lambda e: e.affine_select(idf[:], idf[:], pattern=[[-1, 128]], compare_op=ALU.is_equal,
                                                 fill=0.0, base=0, channel_multiplier=1), reads=["idf"], writes=["idf"])
        S.op("vector", lambda e: e.tensor_copy(idb[:], idf[:]), reads=["idf"], writes=["idb"])
        S.op("vector", lambda e: e.tensor_copy(maskb2[:].rearrange("p m (u c) -> p m u c", u=2), maskb[:, :, None, :].broadcast_to([128, 5, 2, 256])),
             reads=["maskb"], writes=["maskb"])
        S.op("vector", lambda e: e.memset(onesb[:], 1.0 / 1024.0), writes=["onesb"])
        S.op("vector", lambda e: e.memset(epsc[:], EPS), writes=["epsc"])
        S.op("vector", lambda e: e.memset(kprev[:], 0.0), writes=[("kprev", 0), ("kprev", 1)])
        S.op("vector", lambda e: e.memset(vprev[:], 0.0), writes=[("vprev", 0), ("vprev", 1)])
        S.op("vector", lambda e: e.memset(uhist[:], 0.0), writes=[("uhist", l, j) for l in range(2) for j in range(8)])
        S.op("vector", lambda e: e.memset(hhist[:], 0.0), writes=[("hhist", l, j) for l in range(2) for j in range(22)])
        S.op("vector", lambda e: e.tensor_scalar(nsnk[:], snk[:], -1.0, None, ALU.mult), reads=["snk"], writes=["nsnk"])
        S.op("scalar", lambda e: e.activation(out=esnk[:], in_=snk[:], func=AF.Exp), reads=["snk"], writes=["esnk"])
        S.op("vector", lambda e: e.tensor_scalar(nsnkc[:], snkc[:], -1.0, None, ALU.mult), reads=["snkc"], writes=["nsnkc"])

        def vcol(l, off, j):
            c = l * LV + off + j
            return vec[:, c:c + 1]

        def load_w(src, k_chunks=8):
            if k_chunks == 8:
                s = rot("w", NWS)
                S.dma("gpsimd", wsl[:, s, :, :], src.rearrange("(k p) m -> p k m", p=128), "w%d" % s, writes=[("w", s)])
                return ("w", s), (lambda k, s=s: wsl[:, s, k, :])
            s = rot("wd", 2)
            S.dma("gpsimd", wdsl[:, s, :, :], src.rearrange("(k p) m -> p k m", p=128), "wd%d" % s, writes=[("wd", s)])
            return ("wd", s), (lambda k, s=s: wdsl[:, s, k, :])

        def mm_group(out_ap, pairs, reads, bank_keys):
            n = len(pairs)
            fns = []
            for i, (l_, r_) in enumerate(pairs):
                fns.append(lambda e, l_=l_, r_=r_, i=i: e.matmul(out_ap, l_, r_, start=(i == 0), stop=(i == n - 1)))
            S.op("tensor", fns, reads=reads, writes=bank_keys)

        def proj(wkey, wfn, src, skeys, c0, n, kc=8):
            b = S.bank()
            mm_group(ps[:, b, 0:n], [(wfn(k), src(k, c0, n)) for k in range(kc)],
                     reads=[wkey] + skeys, bank_keys=[("ps", b)])
            return b

        class V:
            pass

        def make_views(tc):
            v = V()
            v.tc = tc
            v.nts = ntile_list(tc)
            v.x = lambda j, c0, n: xT[:, j, c0:c0 + n]
            v.xn = lambda j, c0, n: xnT[:, j, c0:c0 + n]
            v.q = lambda j, c0, n: R[:, j, c0:c0 + n]
            v.m = lambda j, c0, n: R[:, 8 + j, c0:c0 + n]
            v.s = lambda j, c0, n: R[:, 16 + j, c0:c0 + n]
            v.hg = lambda j, c0, n: R[:, j, c0:c0 + n]
            v.a = lambda j, c0, n: aT[:, j, c0:c0 + n]
            return v

        def kx(i):
            return [("xT", j, i) for j in range(8)]

        def kxn(i):
            return [("xnT", j, i) for j in range(8)]

        def kR(base, i, cnt=8):
            return [("R", base + j, i) for j in range(cnt)]

        def ka(i):
            return [("aT", j, i) for j in range(8)]

        def nts_of(v, c0, c1):
            return [i for i, (a, n) in enumerate(v.nts) if a < c1 and a + n > c0]

        def rms_stats(src_fn, src_keys, c0, n):
            S.op("scalar", lambda e: e.activation(out=sq[:, :, 0:n], in_=src_fn(c0, n), func=AF.Square),
                 reads=src_keys, writes=["sq"])
            b = S.bank()
            mm_group(ps[:, b, 0:n], [(onesb[:], sq[:, j, 0:n]) for j in range(8)], reads=["sq", "onesb"], bank_keys=[("ps", b)])
            sl = rot("st", 2)
            S.op("scalar", lambda e: e.activation(out=st1[:, sl, 0:n], in_=ps[:, b, 0:n], func=AF.Sqrt, bias=epsc[:, 0:1], scale=1.0),
                 reads=[("ps", b), "epsc"], writes=[("st1", sl)])
            S.op("vector", lambda e: e.reciprocal(st1[:, sl, 0:n], st1[:, sl, 0:n]), reads=[("st1", sl)], writes=[("st1", sl)])
            return sl

        def rmsnorm_to_xn(v, l, goff):
            for i, (c0, n) in enumerate(v.nts):
                sl = rms_stats(lambda c0, n: xT[:, :, c0:c0 + n], kx(i), c0, n)
                for j in range(8):
                    S.op("vector", lambda e, j=j, c0=c0, n=n, sl=sl: e.scalar_tensor_tensor(
                        out=xnT[:, j, c0:c0 + n], in0=xT[:, j, c0:c0 + n], scalar=vcol(l, goff, j) if l < 2 else vec[:, 2 * LV + j:2 * LV + j + 1],
                        in1=st1[:, sl, 0:n], op0=ALU.mult, op1=ALU.mult),
                        reads=[("xT", j, i), ("st1", sl), "vec"], writes=[("xnT", j, i)])

        def load_x(v, t0, ntl, sample):
            if sample:
                sl = rot("xs", 2)
                S.dma("sync", xstage[0:NS, sl, :], xs[:, :], "xs%d" % sl, writes=[("xstage", sl)])
                b = S.bank()
                fns = [lambda e, j=j: e.transpose(ps[:, b, j * NS:(j + 1) * NS], xstage[0:NS, sl, j * 128:(j + 1) * 128], idf[0:NS, 0:NS])
                       for j in range(8)]
                S.op("tensor", fns, reads=[("xstage", sl), "idf"], writes=[("ps", b)])
                S.op("vector", lambda e: e.tensor_copy(xT[:, :, 0:NS], ps[:, b, 0:8 * NS].rearrange("p (j t) -> p j t", j=8)),
                     reads=[("ps", b)], writes=kx(0))
                return
            for t in range(ntl):
                sl = rot("xs", 2)
                S.dma("sync", xstage[:, sl, :], xin[(t0 + t) * 128:(t0 + t + 1) * 128, :], "xs%d" % sl, writes=[("xstage", sl)])
                for half in range(2):
                    b = S.bank()
                    fns = [lambda e, j=j, b=b, sl=sl: e.transpose(ps[:, b, (j % 4) * 128:(j % 4 + 1) * 128],
                                                           xstage[:, sl, j * 128:(j + 1) * 128], idf[:])
                           for j in range(half * 4, half * 4 + 4)]
                    S.op("tensor", fns, reads=[("xstage", sl), "idf"], writes=[("ps", b)])
                    its = nts_of(v, t * 128, t * 128 + 128)
                    copy_op(evac_eng(), xT[:, half * 4:half * 4 + 4, t * 128:(t + 1) * 128],
                            ps[:, b, :].rearrange("p (j t) -> p j t", j=4),
                            reads=[("ps", b)], writes=[("xT", j, i) for j in range(half * 4, half * 4 + 4) for i in its])

        def conv_branch(v, l, sc_first, sc_last, sample, part=None):
            tc = v.tc
            if part != 1:
              conv_part(v, l, sc_first, sc_last, sample)
            if part != 0:
              ln_part(v, l)

        def conv_part(v, l, sc_first, sc_last, sample):
            tc = v.tc
            import os as _os
            if sample:
                for g in range(int(_os.environ.get("KG", "4"))):
                    sl = rot("xs", 2)
                    S.dma("sync", xstage[0:120, sl, :], sconv[l, 4 * g:4 * g + 4, :, :].rearrange("b k c -> (b k) c"),
                          "xs%d" % sl, writes=[("xstage", sl)])
                    for half in range(2):
                        b = S.bank()
                        kvar = _os.environ.get("KVAR", "")
                        W_ = 128 if "A" in kvar else 120
                        fns = [lambda e, j=j, b=b, sl=sl, W_=W_: e.transpose(ps[:, b, (j % 4) * W_:(j % 4) * W_ + 120],
                                                               xstage[0:120, sl, j * 128:(j + 1) * 128], idf[0:120, 0:120])
                               for j in range(half * 4, half * 4 + 4)]
                        S.op("tensor", fns, reads=[("xstage", sl), "idf"], writes=[("ps", b)])
                        eng_ = evac_eng()
                        for j in range(half * 4, half * 4 + 4):
                            copy_op(eng_, uTs[:, j, 0:30, 4 * g:4 * g + 4],
                                    ps[:, b, (j % 4) * W_:(j % 4) * W_ + 120].rearrange("p (b k) -> p k b", b=4),
                                    reads=[("ps", b)], writes=[("uTs", j)])
            import os as _os
            ksub = int(_os.environ.get("KSUB", "99"))
            if ksub <= 1:
                return
            for j in range(8):
                if ksub <= 2 and j >= 1:
                    return
                wa, fa = load_w(w_in[l, :, j * 128:(j + 1) * 128])
                wb, fb = load_w(w_in[l, :, D + j * 128:D + (j + 1) * 128])
                S.op("vector", lambda e, j=j: e.tensor_tensor(
                    dg[:], idb[:, None, :].broadcast_to([128, 31, 128]),
                    vec[:, l * LV + O_CDW + j * 31:l * LV + O_CDW + (j + 1) * 31, None].broadcast_to([128, 31, 128]), ALU.mult),
                    reads=["idb", "vec"], writes=["dg"])
                ub = j % 2
                if not sample:
                    S.op("vector", lambda e, j=j, ub=ub: e.tensor_copy(uT[:, ub, 0:30], uhist[:, l, j, :]),
                         reads=[("uhist", l, j)], writes=[("uT", ub, "h")])
                for i, (c0, n) in enumerate(v.nts):
                    ba = proj(wa, fa, v.xn, kxn(i), c0, n)
                    bb = proj(wb, fb, v.xn, kxn(i), c0, n)
                    ts = rot("tmp", 4)
                    S.op("scalar", lambda e, bb=bb, ts=ts, n=n: e.activation(out=tmp[:, ts, 0:n], in_=ps[:, bb, 0:n], func=AF.Sigmoid),
                         reads=[("ps", bb)], writes=[("tmp", ts)])
                    if sample:
                        S.op("vector", lambda e, ba=ba, ts=ts, j=j: e.tensor_tensor(uf32[:, j, 0:NS], ps[:, ba, 0:NS], tmp[:, ts, 0:NS], ALU.mult),
                             reads=[("ps", ba), ("tmp", ts)], writes=[("uf32", j)])
                        S.op("vector", lambda e, j=j: e.tensor_copy(uTs[:, j, 30, :], uf32[:, j, 0:NS]),
                             reads=[("uf32", j)], writes=[("uTs", j)])
                    else:
                        S.op("vector", lambda e, ba=ba, ts=ts, c0=c0, n=n, ub=ub: e.tensor_tensor(
                            uT[:, ub, 30 + c0:30 + c0 + n], ps[:, ba, 0:n], tmp[:, ts, 0:n], ALU.mult),
                            reads=[("ps", ba), ("tmp", ts)], writes=[("uT", ub, i)])
                        if sc_last and c0 + n == tc:
                            S.op("vector", lambda e, ba=ba, ts=ts, n=n, j=j: e.tensor_tensor(
                                uf32[:, j, :], ps[:, ba, n - 30:n], tmp[:, ts, n - 30:n], ALU.mult),
                                reads=[("ps", ba), ("tmp", ts)], writes=[("uf32", j)])
                if not sample:
                    nn = len(v.nts)
                    if sc_first:
                        S.op("vector", lambda e, ub=ub: e.tensor_tensor(uT[:, ub, 30:30 + 384], uT[:, ub, 30:30 + 384], validb[:], ALU.mult),
                             reads=[("uT", ub, i) for i in range(nn)] + ["validb"], writes=[("uT", ub, i) for i in range(nn)])
                    S.op("vector", lambda e, j=j, ub=ub: e.tensor_copy(uhist[:, l, j, :], uT[:, ub, tc:tc + 30]),
                         reads=[("uT", ub, i) for i in range(nn)] + [("uT", ub, "h")], writes=[("uhist", l, j)])
                for i, (c0, n) in enumerate(v.nts):
                    b = S.bank()
                    if sample:
                        pairs = [(dg[:, k, :], uTs[:, j, k, :]) for k in range(31)]
                        rd = ["dg", ("uTs", j)]
                    else:
                        pairs = [(dg[:, k, :], uT[:, ub, c0 + k:c0 + k + n]) for k in range(31)]
                        rd = ["dg", ("uT", ub, "h")] + [("uT", ub, ii) for ii in range(max(0, i - 1), i + 1)]
                    mm_group(ps[:, b, 0:n], pairs, reads=rd, bank_keys=[("ps", b)])
                    S.op("scalar", lambda e, b=b, j=j, c0=c0, n=n: e.activation(out=v.s(j, c0, n), in_=ps[:, b, 0:n], func=AF.Identity,
                                                                                 bias=vcol(l, O_CDB, j), scale=1.0),
                         reads=[("ps", b), "vec"], writes=[("R", 16 + j, i)])
        def ln_part(v, l):
            for i, (c0, n) in enumerate(v.nts):
                bm = S.bank()
                mm_group(ps[:, bm, 0:n], [(onesb[:], v.s(j, c0, n)) for j in range(8)], reads=kR(16, i) + ["onesb"], bank_keys=[("ps", bm)])
                S.op("scalar", lambda e, c0=c0, n=n: e.activation(out=sq[:, :, 0:n], in_=R[:, 16:24, c0:c0 + n], func=AF.Square),
                     reads=kR(16, i), writes=["sq"])
                b2 = S.bank()
                mm_group(ps[:, b2, 0:n], [(onesb[:], sq[:, j, 0:n]) for j in range(8)], reads=["sq", "onesb"], bank_keys=[("ps", b2)])
                sl = rot("st", 2)
                S.op("vector", lambda e, sl=sl, bm=bm, n=n: e.tensor_copy(st2[:, sl, 0:n], ps[:, bm, 0:n]), reads=[("ps", bm)], writes=[("st2", sl)])
                S.op("vector", lambda e, sl=sl, n=n: e.tensor_tensor(st1[:, sl, 0:n], st2[:, sl, 0:n], st2[:, sl, 0:n], ALU.mult),
                     reads=[("st2", sl)], writes=[("st1", sl)])
                S.op("vector", lambda e, sl=sl, b2=b2, n=n: e.tensor_tensor(st1[:, sl, 0:n], ps[:, b2, 0:n], st1[:, sl, 0:n], ALU.subtract),
                     reads=[("ps", b2), ("st1", sl)], writes=[("st1", sl)])
                S.op("vector", lambda e, sl=sl, n=n: e.tensor_scalar(st1[:, sl, 0:n], st1[:, sl, 0:n], 0.0, None, ALU.max),
                     reads=[("st1", sl)], writes=[("st1", sl)])
                S.op("scalar", lambda e, sl=sl, n=n: e.activation(out=st1[:, sl, 0:n], in_=st1[:, sl, 0:n], func=AF.Sqrt, bias=epsc[:, 0:1], scale=1.0),
                     reads=[("st1", sl), "epsc"], writes=[("st1", sl)])
                S.op("vector", lambda e, sl=sl, n=n: e.reciprocal(st1[:, sl, 0:n], st1[:, sl, 0:n]), reads=[("st1", sl)], writes=[("st1", sl)])
                for j in range(8):
                    ts = rot("tmp", 4)
                    S.op("vector", lambda e, j=j, ts=ts, sl=sl, c0=c0, n=n: e.tensor_tensor(tmp[:, ts, 0:n], v.s(j, c0, n), st2[:, sl, 0:n], ALU.subtract),
                         reads=[("R", 16 + j, i), ("st2", sl)], writes=[("tmp", ts)])
                    S.op("vector", lambda e, ts=ts, sl=sl, n=n: e.tensor_tensor(tmp[:, ts, 0:n], tmp[:, ts, 0:n], st1[:, sl, 0:n], ALU.mult),
                         reads=[("tmp", ts), ("st1", sl)], writes=[("tmp", ts)])
                    S.op("scalar", lambda e, j=j, ts=ts, c0=c0, n=n: e.activation(out=v.s(j, c0, n), in_=tmp[:, ts, 0:n], func=AF.Silu,
                                                                                  bias=vcol(l, O_LB, j), scale=vcol(l, O_LG, j)),
                         reads=[("tmp", ts), "vec"], writes=[("R", 16 + j, i)])

        def emit_rows(l, src_fn, ncols, nchunks, dst_fn, keys):
            for g0 in range(0, nchunks, 4):
                g1 = min(nchunks, g0 + 4)
                b = S.bank()
                fns = [lambda e, j=j, b=b, g0=g0: e.transpose(ps[0:ncols, b, (j - g0) * 128:(j - g0 + 1) * 128], src_fn(j), idf[:])
                       for j in range(g0, g1)]
                S.op("tensor", fns, reads=keys + ["idf"], writes=[("ps", b)])
                sl = rot("os", 2)
                w = (g1 - g0) * 128
                copy_op(evac_eng(), ostage[0:ncols, sl, 0:w], ps[0:ncols, b, 0:w], reads=[("ps", b)], writes=[("ostage", sl)])
                S.dma("sync", dst_fn(g0 * 128, w), ostage[0:ncols, sl, 0:w], "os%d" % sl, reads=[("ostage", sl)], writes=["OUT"])

        def q_proj(v, l):
            for j in range(8):
                wq, fq = load_w(w_in[l, :, 2 * D + j * 128:2 * D + (j + 1) * 128])
                for i, (c0, n) in enumerate(v.nts):
                    b = proj(wq, fq, v.xn, kxn(i), c0, n)
                    copy_op(evac_eng(), v.q(j, c0, n), ps[:, b, 0:n], reads=[("ps", b)], writes=[("R", j, i)])

        def qkv(v, l, sc_last, sample):
            tc = v.tc
            for kh in range(4):
                s = rot("w", NWS)
                src = w_in[l, :, 3 * D + kh * 64:3 * D + (kh + 1) * 64].rearrange("(k p) m -> p k m", p=128)
                S.dma("gpsimd", wsl[:, s, :, 0:64], src, "w%d" % s, writes=[("w", s)])
                S.dma("gpsimd", wsl[:, s, :, 64:128], src, "w%d" % s, writes=[("w", s)])
                fk = lambda k, s=s: wsl[:, s, k, :]
                for i, (c0, n) in enumerate(v.nts):
                    b = proj(("w", s), fk, v.xn, kxn(i), c0, n)
                    copy_op(evac_eng(), kT[:, kh, 128 + c0:128 + c0 + n], ps[:, b, 0:n], reads=[("ps", b)], writes=[("kT", kh, i)])
            S.dma("gpsimd", wv[:], w_in[l, :, 3 * D + 256:3 * D + 512].rearrange("(k p) m -> p k m", p=128), "wv", writes=["wv"])
            S.dma("gpsimd", wk[:], w_in[l, :, 3 * D:3 * D + 256].rearrange("(k p) m -> p k m", p=128), "wk", writes=["wk"])
            if sample:
                b = S.bank()
                mm_group(ps[0:NS, b, 0:256], [(xnT[:, k, 0:NS], wv[:, k, :]) for k in range(8)], reads=kxn(0) + ["wv"], bank_keys=[("ps", b)])
                S.op("vector", lambda e, b=b: e.tensor_copy(Vnew[:].rearrange("p k (u d) -> p k u d", u=2),
                                                          ps[0:NS, b, 0:256].rearrange("p (k d) -> p k d", k=4)[:, :, None, :].broadcast_to([NS, 4, 2, 64])),
                     reads=[("ps", b)], writes=["Vnew"])
                sl = rot("os", 2)
                S.op("vector", lambda e, b=b, sl=sl: e.tensor_copy(ostage[0:NS, sl, 0:256], ps[0:NS, b, 0:256]),
                     reads=[("ps", b)], writes=[("ostage", sl)])
                S.dma("sync", v_s[l, :, 127, :], ostage[0:NS, sl, 0:256], "os%d" % sl, reads=[("ostage", sl)], writes=["OUT"])
                b = S.bank()
                mm_group(ps[0:NS, b, 0:256], [(xnT[:, k, 0:NS], wk[:, k, :]) for k in range(8)], reads=kxn(0) + ["wk"], bank_keys=[("ps", b)])
                sl = rot("os", 2)
                S.op("scalar", lambda e, b=b, sl=sl: e.activation(out=ostage[0:NS, sl, 0:256], in_=ps[0:NS, b, 0:256], func=AF.Copy),
                     reads=[("ps", b)], writes=[("ostage", sl)])
                S.dma("sync", k_s[l, :, 127, :], ostage[0:NS, sl, 0:256], "os%d" % sl, reads=[("ostage", sl)], writes=["OUT"])
                return
            ntl = tc // 128
            for t in range(ntl):
                its = nts_of(v, t * 128, t * 128 + 128)
                rk = [("xnT", j, i) for j in range(8) for i in its]
                b = S.bank()
                mm_group(ps[:, b, 0:256], [(xnT[:, k, t * 128:(t + 1) * 128], wv[:, k, :]) for k in range(8)], reads=rk + ["wv"], bank_keys=[("ps", b)])
                S.op("vector", lambda e, b=b, t=t: e.tensor_copy(Vt[:, 1 + t].rearrange("p k (u d) -> p k u d", u=2),
                                                                 ps[:, b, 0:256].rearrange("p (k d) -> p k d", k=4)[:, :, None, :].broadcast_to([128, 4, 2, 64])),
                     reads=[("ps", b)], writes=[("Vt", 1 + t)])
                if sc_last and t == ntl - 1:
                    sl = rot("os", 2)
                    S.op("vector", lambda e, b=b, sl=sl: e.tensor_copy(ostage[:, sl, 0:256], ps[:, b, 0:256]),
                         reads=[("ps", b)], writes=[("ostage", sl)])
                    S.dma("sync", kv_p[l, 1, :, :], ostage[:, sl, 0:256], "os%d" % sl, reads=[("ostage", sl)], writes=["OUT"])
                    b = S.bank()
                    mm_group(ps[:, b, 0:256], [(xnT[:, k, t * 128:(t + 1) * 128], wk[:, k, :]) for k in range(8)], reads=rk + ["wk"], bank_keys=[("ps", b)])
                    sl = rot("os", 2)
                    S.op("scalar", lambda e, b=b, sl=sl: e.activation(out=ostage[:, sl, 0:256], in_=ps[:, b, 0:256], func=AF.Copy),
                         reads=[("ps", b)], writes=[("ostage", sl)])
                    S.dma("sync", kv_p[l, 0, :, :], ostage[:, sl, 0:256], "os%d" % sl, reads=[("ostage", sl)], writes=["OUT"])

        def attention(v, l, t0):
            ntl = v.tc // 128
            nn = len(v.nts)
            tc = v.tc
            S.op("vector", lambda e: e.tensor_copy(kT[:, :, 0:128], kprev[:, l]), reads=[("kprev", l)], writes=[("kT", kh, "prev") for kh in range(4)])
            S.op("vector", lambda e: e.tensor_copy(Vt[:, 0], vprev[:, l]), reads=[("vprev", l)], writes=[("Vt", 0)])
            units = [(t, kh) for t in range(ntl) for kh in range(4)]
            st_ = {}

            def stage_a(u):
                t, kh = u
                gi = t0 + t
                mi = gi if gi < 4 else 4
                its = nts_of(v, t * 128, t * 128 + 128)
                b = S.bank(2)
                sc = ps[:, b:b + 2, :].rearrange("p a (g c) -> p (a g) c", g=2)
                fns = []
                import os as _os
                oldmask = "M" in _os.environ.get("KATT", "")
                for g in range(4):
                    h = 4 * kh + g
                    jq, pb = h // 2, (h % 2) * 64
                    if oldmask:
                        fns.append(lambda e, g=g, jq=jq, pb=pb: e.matmul(sc[:, g, :], R[pb:pb + 64, jq, t * 128:(t + 1) * 128],
                                                                         kT[pb:pb + 64, kh, t * 128:t * 128 + 256], start=True, stop=False))
                        fns.append(lambda e, g=g: e.matmul(sc[:, g, :], idb[:], maskb[:, mi, :], start=False, stop=True))
                    else:
                        i4 = (g % 2) * 2 + g // 2
                        fns.append(lambda e, i4=i4, g=g, jq=jq, pb=pb: e.matmul(sc[:, i4, :], R[pb:pb + 64, jq, t * 128:(t + 1) * 128],
                                                                         kT[pb:pb + 64, kh, t * 128:t * 128 + 256],
                                                                         start=(g < 2), stop=False, skip_group_check=True))
                if not oldmask:
                    for a_ in range(2):
                        fns.append(lambda e, a_=a_: e.matmul(ps[:, b + a_, :], idb[:], maskb2[:, mi, :], start=False, stop=True, skip_group_check=True))
                rk = [("R", jq, i) for jq in (2 * kh, 2 * kh + 1) for i in its] + [("kT", kh, i) for i in range(nn)] + [("kT", kh, "prev"), "idb", "maskb"]
                S.op("tensor", fns, reads=rk, writes=[("ps", b), ("ps", b + 1)])
                S.reserved.update((b, b + 1))
                st_[u] = (b, sc, its)

            def stage_b1(u):
                t, kh = u
                b, sc, its = st_[u]
                s0 = rot("sm", 8)
                mx, ng, ssum, dd = sm[:, s0, 0, :], sm[:, s0, 1, :], sm[:, s0, 2, :], sm[:, s0, 3, :]
                S.op("vector", lambda e: e.tensor_reduce(mx, sc, AX.X, ALU.max), reads=[("ps", b), ("ps", b + 1)], writes=[("sm", s0, 0)])
                S.op("vector", lambda e: e.scalar_tensor_tensor(out=ng, in0=mx, scalar=-SCALE, in1=nsnk[:, l, 4 * kh:4 * kh + 4],
                                                                op0=ALU.mult, op1=ALU.min),
                     reads=[("sm", s0, 0), "nsnk"], writes=[("sm", s0, 1)])
                S.op("vector", lambda e: e.memset(ssum, 0.0), writes=[("sm", s0, 2)])
                S.op("scalar", lambda e: e.activation(out=dd, in_=ng, func=AF.Exp), reads=[("sm", s0, 1)], writes=[("sm", s0, 3)])
                S.op("vector", lambda e: e.tensor_tensor(dd, dd, esnk[:, l, 4 * kh:4 * kh + 4], ALU.mult), reads=[("sm", s0, 3), "esnk"], writes=[("sm", s0, 3)])
                st_[u] = (b, sc, its, s0)

            def stage_b2(u):
                t, kh = u
                b, sc, its, s0 = st_[u]
                ng, ssum = sm[:, s0, 1, :], sm[:, s0, 2, :]
                eb = rot("es", 3)
                for g in range(4):
                    S.op("scalar", lambda e, g=g: e.activation(
                        out=esb[:, eb, g, :], in_=sc[:, g, :], func=AF.Exp, bias=ng[:, g:g + 1], scale=SCALE, accum_out=ssum[:, g:g + 1]),
                        reads=[("ps", b), ("ps", b + 1), ("sm", s0, 1), ("sm", s0, 2)], writes=[("esb", eb, g), ("sm", s0, 2)])
                S.reserved.difference_update((b, b + 1))
                st_[u] = (b, sc, its, s0, eb)

            def stage_b3(u):
                t, kh = u
                b, sc, its, s0, eb = st_[u]
                ssum, dd = sm[:, s0, 2, :], sm[:, s0, 3, :]
                S.op("vector", lambda e: e.tensor_tensor(dd, dd, ssum, ALU.add), reads=[("sm", s0, 3), ("sm", s0, 2)], writes=[("sm", s0, 3)])
                S.op("vector", lambda e: e.reciprocal(dd, dd), reads=[("sm", s0, 3)], writes=[("sm", s0, 3)])
                S.op("gpsimd", lambda e: e.tensor_tensor(pbf[:, eb], esb[:, eb], dd[:, :, None].broadcast_to([128, 4, 256]), ALU.mult),
                     reads=[("esb", eb, g) for g in range(4)] + [("sm", s0, 3)], writes=[("pbf", eb)])
                st_[u] = (b, sc, its, eb)

            def stage_c(u):
                t, kh = u
                b, sc, its, eb = st_.pop(u)
                bt = S.bank()
                ptv = ps[:, bt, :].bitcast(BF16).rearrange("p (g k q) -> p g k q", g=4, k=2)
                fns = []
                for g in range(4):
                    for blk in range(2):
                        fns.append(lambda e, g=g, blk=blk: e.transpose(ptv[:, g, blk, :], pbf[:, eb, g, blk * 128:(blk + 1) * 128], idb[:]))
                S.op("tensor", fns, reads=[("pbf", eb), "idb"], writes=[("ps", bt)])
                pb_ = rot("kv", 2)
                copy_op("vector", pT[:, pb_], ptv, reads=[("ps", bt)], writes=[("pT", pb_)])
                bo = S.bank()
                fns = []
                for g in range(4):
                    for blk in range(2):
                        fns.append(lambda e, g=g, blk=blk: e.matmul(ps[:, bo, g * 128:(g + 1) * 128], Vt[:, t + blk, kh, :], pT[:, pb_, g, blk, :],
                                                                    start=(blk == 0), stop=(blk == 1)))
                S.op("tensor", fns, reads=[("pT", pb_), ("Vt", t), ("Vt", t + 1)], writes=[("ps", bo)])
                eng_ = evac_eng()
                for hh in range(2):
                    pb = hh * 64
                    copy_op(eng_, aT[pb:pb + 64, 2 * kh:2 * kh + 2, t * 128:(t + 1) * 128],
                            ps[pb:pb + 64, bo, :].rearrange("p (u j q) -> p u j q", u=2, j=2)[:, hh, :, :],
                            reads=[("ps", bo)], writes=[("aT", 2 * kh, i) for i in its] + [("aT", 2 * kh + 1, i) for i in its])

            nu = len(units)
            for i in range(min(2, nu)):
                stage_a(units[i])
            for i in range(min(2, nu)):
                stage_b1(units[i])
            stage_b2(units[0])
            for i, u in enumerate(units):
                if i + 2 < nu:
                    stage_a(units[i + 2])
                stage_b3(u)
                if i + 1 < nu:
                    stage_b2(units[i + 1])
                if i + 2 < nu:
                    stage_b1(units[i + 2])
                stage_c(u)
            S.op("vector", lambda e: e.tensor_copy(kprev[:, l], kT[:, :, tc:tc + 128]),
                 reads=[("kT", kh, i) for kh in range(4) for i in range(nn)], writes=[("kprev", l)])
            S.op("vector", lambda e: e.tensor_copy(vprev[:, l], Vt[:, ntl]), reads=[("Vt", ntl)], writes=[("vprev", l)])

        def attention_sample(v, l):
            bo = S.bank()
            S.reserved.add(bo)
            S.op("vector", lambda e: e.memset(R[:, 0:8, NS:32], 0.0), writes=[("R", j, 0) for j in range(8)], reads=[("R", j, 0) for j in range(8)])
            for bq in range(NS):
              def unit(bq=bq):
                st = rot("kv", 2)
                S.dma("sync", kvst[:, st, 0, :], ck[l, bq, :, :], "kvk%d" % st, writes=[("kvst", st, 0)])
                S.dma("sync", kvst[:, st, 1, :], cv[l, bq, :, :], "kvv%d" % st, writes=[("kvst", st, 1)])
                S.dma("sync", k_s[l, bq, 0:127, :], kvst[1:128, st, 0, :], "ksok%d" % st, reads=[("kvst", st, 0)], writes=["OUT"])
                S.dma("sync", v_s[l, bq, 0:127, :], kvst[1:128, st, 1, :], "ksov%d" % st, reads=[("kvst", st, 1)], writes=["OUT"])
                S.op("vector", lambda e, st=st: e.tensor_copy(kdup[:, st].rearrange("p k (u d) -> p k u d", u=2),
                                                            kvst[:, st, 0, :].rearrange("p (k d) -> p k d", k=4)[:, :, None, :].broadcast_to([128, 4, 2, 64])),
                     reads=[("kvst", st, 0)], writes=[("kdup", st)])
                S.op("vector", lambda e, st=st: e.tensor_copy(Vb[:, st].rearrange("p k (u d) -> p k u d", u=2),
                                                            kvst[:, st, 1, :].rearrange("p (k d) -> p k d", k=4)[:, :, None, :].broadcast_to([128, 4, 2, 64])),
                     reads=[("kvst", st, 1)], writes=[("Vb", st)])
                bt = S.bank()
                S.op("tensor", [lambda e, kh=kh: e.transpose(ps[:, bt, kh * 128:(kh + 1) * 128], kdup[:, st, kh, :], idf[:]) for kh in range(4)],
                     reads=[("kdup", st), "idf"], writes=[("ps", bt)])
                copy_op(evac_eng(), KTb[:, st], ps[:, bt, :].rearrange("p (k s) -> p k s", k=4), reads=[("ps", bt)], writes=[("KTb", st)])
                b = S.bank(2)
                scv = ps[:, b:b + 2, :].rearrange("p a (k c) -> p (a k) c", k=2)
                fns = []
                for pbsel in range(2):
                    for kh in range(4):
                        for g in (pbsel, pbsel + 2):
                            h = 4 * kh + g
                            jq, pb = h // 2, (h % 2) * 64
                            fns.append(lambda e, kh=kh, g=g, jq=jq, pb=pb: e.matmul(
                                scv[32 * g:32 * g + 32, kh, 0:128], R[pb:pb + 64, jq, 0:32], KTb[pb:pb + 64, st, kh, :],
                                start=True, stop=True, tile_position=(pb, 32 * g)))
                            fns.append(lambda e, kh=kh, g=g, jq=jq, pb=pb: e.matmul(
                                scv[32 * g:32 * g + 32, kh, 128:144], R[pb:pb + 64, jq, 0:32], kT[pb:pb + 64, kh, 128:128 + NS],
                                start=True, stop=True, tile_position=(pb, 32 * g)))
                S.op("tensor", fns, reads=[("R", j, 0) for j in range(8)] + [("KTb", st)] + [("kT", kh, 0) for kh in range(4)],
                     writes=[("ps", b), ("ps", b + 1)])
                eb = rot("es", 2)
                S.op("vector", lambda e, eb=eb: e.tensor_tensor(ess[:, eb, :, 128:144], scv[:, :, 128:144], masksb[:], ALU.add),
                     reads=[("ps", b), ("ps", b + 1), "masksb"], writes=[("essn", eb)])
                s0 = rot("sm", 8)
                mx, ng, ssum, dd = sm[:, s0, 0, :], sm[:, s0, 1, :], sm[:, s0, 2, :], sm[:, s0, 3, :]
                m2, s2 = sm[:, s0, 4, :], sm[:, s0, 5, :]
                S.op("vector", lambda e, mx=mx: e.tensor_reduce(mx, scv[:, :, 0:128], AX.X, ALU.max), reads=[("ps", b), ("ps", b + 1)], writes=[("sm", s0, 0)])
                S.op("vector", lambda e, m2=m2, eb=eb: e.tensor_reduce(m2, ess[:, eb, :, 128:144], AX.X, ALU.max), reads=[("essn", eb)], writes=[("sm", s0, 4)])
                S.op("vector", lambda e, mx=mx, m2=m2: e.tensor_tensor(mx, mx, m2, ALU.max), reads=[("sm", s0, 0), ("sm", s0, 4)], writes=[("sm", s0, 0)])
                S.op("vector", lambda e, mx=mx, ng=ng: e.scalar_tensor_tensor(out=ng, in0=mx, scalar=-SCALE, in1=nsnkc[:, l, :], op0=ALU.mult, op1=ALU.min),
                     reads=[("sm", s0, 0), "nsnkc"], writes=[("sm", s0, 1)])
                S.op("vector", lambda e, ssum=ssum, s2=s2: e.memset(sm[:, s0, 2:4, :], 0.0), writes=[("sm", s0, 2), ("sm", s0, 3)])
                S.op("vector", lambda e, s2=s2: e.memset(s2, 0.0), writes=[("sm", s0, 5)])
                for kh in range(4):
                    S.op("scalar", lambda e, kh=kh, eb=eb, ng=ng, ssum=ssum: e.activation(
                        out=ess[:, eb, kh, 0:128], in_=scv[:, kh, 0:128], func=AF.Exp, bias=ng[:, kh:kh + 1], scale=SCALE, accum_out=ssum[:, kh:kh + 1]),
                        reads=[("ps", b), ("ps", b + 1), ("sm", s0, 1), ("sm", s0, 2)], writes=[("essc", eb, kh), ("sm", s0, 2)])
                    S.op("scalar", lambda e, kh=kh, eb=eb, ng=ng, s2=s2: e.activation(
                        out=ess[:, eb, kh, 128:144], in_=ess[:, eb, kh, 128:144], func=AF.Exp, bias=ng[:, kh:kh + 1], scale=SCALE, accum_out=s2[:, kh:kh + 1]),
                        reads=[("essn", eb), ("sm", s0, 1), ("sm", s0, 5)], writes=[("essn", eb), ("sm", s0, 5)])
                S.op("vector", lambda e, dd=dd, ng=ng: e.tensor_tensor(dd, snkc[:, l, :], ng, ALU.add), reads=["snkc", ("sm", s0, 1)], writes=[("sm", s0, 3)])
                S.op("scalar", lambda e, dd=dd: e.activation(out=dd, in_=dd, func=AF.Exp), reads=[("sm", s0, 3)], writes=[("sm", s0, 3)])
                S.op("vector", lambda e, dd=dd, ssum=ssum: e.tensor_tensor(dd, dd, ssum, ALU.add), reads=[("sm", s0, 3), ("sm", s0, 2)], writes=[("sm", s0, 3)])
                S.op("vector", lambda e, dd=dd, s2=s2: e.tensor_tensor(dd, dd, s2, ALU.add), reads=[("sm", s0, 3), ("sm", s0, 5)], writes=[("sm", s0, 3)])
                S.op("vector", lambda e, dd=dd: e.reciprocal(dd, dd), reads=[("sm", s0, 3)], writes=[("sm", s0, 3)])
                S.op("vector", lambda e, dd=dd, bq=bq: e.tensor_scalar(dd, dd, rowselt[:, bq:bq + 1], None, ALU.mult), reads=[("sm", s0, 3), "rowsel"], writes=[("sm", s0, 3)])
                S.op("vector", lambda e, dd=dd, eb=eb: e.tensor_tensor(ess[:, eb], ess[:, eb], dd[:, :, None].broadcast_to([128, 4, 144]), ALU.mult),
                     reads=[("essc", eb, kh) for kh in range(4)] + [("essn", eb), ("sm", s0, 3)], writes=[("essc", eb, kh) for kh in range(4)] + [("essn", eb)])
                btp = S.bank()
                S.op("tensor", [lambda e, kh=kh: e.transpose(ps[:, btp, kh * 128:(kh + 1) * 128], ess[:, eb, kh, 0:128], idf[:]) for kh in range(4)],
                     reads=[("essc", eb, kh) for kh in range(4)] + ["idf"], writes=[("ps", btp)])
                copy_op(evac_eng(), pTs[:, st], ps[:, btp, :].rearrange("p (k s) -> p k s", k=4), reads=[("ps", btp)], writes=[("pTs", st)])
                bt2 = S.bank()
                S.op("tensor", [lambda e, kh=kh: e.transpose(ps[0:NS, bt2, kh * 128:(kh + 1) * 128], ess[:, eb, kh, 128:144], idf[:]) for kh in range(4)],
                     reads=[("essn", eb), "idf"], writes=[("ps", bt2)])
                copy_op(evac_eng(), pTn[:, st], ps[0:NS, bt2, :].rearrange("p (k s) -> p k s", k=4), reads=[("ps", bt2)], writes=[("pTn", st)])
                fns = []
                for kh in range(4):
                    fns.append(lambda e, kh=kh: e.matmul(ps[:, bo, kh * 128:(kh + 1) * 128], Vb[:, st, kh, :], pTs[:, st, kh, :],
                                                         start=(bq == 0 and kh == 0), stop=False, skip_group_check=True))
                    fns.append(lambda e, kh=kh: e.matmul(ps[:, bo, kh * 128:(kh + 1) * 128], Vnew[:, kh, :], pTn[:, st, kh, :],
                                                         start=False, stop=(bq == NS - 1 and kh == 3), skip_group_check=True))
                S.op("tensor", fns, reads=[("Vb", st), ("pTs", st), ("pTn", st), "Vnew"], writes=[("ps", bo)])
              unit()
            S.reserved.discard(bo)
            ov = ps[:, bo, :].rearrange("p (k g s) -> p k g s", k=4, g=4)
            for kh in range(4):
                for g in range(4):
                    h = 4 * kh + g
                    jq, pb = h // 2, (h % 2) * 64
                    copy_op("vector", aT[pb:pb + 64, jq, 0:NS], ov[pb:pb + 64, kh, g, 0:NS], reads=[("ps", bo)], writes=[("aT", jq, 0)])

        def gates_out(v, l):
            for j in range(8):
                w1, f1 = load_w(w_pw[l, :, j * 128:(j + 1) * 128])
                w2, f2 = load_w(w_in[l, :, 3 * D + 512 + j * 128:3 * D + 512 + (j + 1) * 128])
                w3, f3 = load_w(w_ao[l, :, j * 128:(j + 1) * 128])
                w4, f4 = load_w(w_in[l, :, 4 * D + 512 + j * 128:4 * D + 512 + (j + 1) * 128])
                for i, (c0, n) in enumerate(v.nts):
                    b1 = proj(w1, f1, v.s, kR(16, i), c0, n)
                    b2 = proj(w2, f2, v.xn, kxn(i), c0, n)
                    t1 = rot("tmp", 4)
                    S.op("scalar", lambda e, b2=b2, t1=t1, n=n: e.activation(out=tmp[:, t1, 0:n], in_=ps[:, b2, 0:n], func=AF.Sigmoid),
                         reads=[("ps", b2)], writes=[("tmp", t1)])
                    S.op("vector", lambda e, b1=b1, t1=t1, n=n: e.tensor_tensor(tmp[:, t1, 0:n], ps[:, b1, 0:n], tmp[:, t1, 0:n], ALU.mult),
                         reads=[("ps", b1), ("tmp", t1)], writes=[("tmp", t1)])
                    b3 = proj(w3, f3, v.a, ka(i), c0, n)
                    b4 = proj(w4, f4, v.xn, kxn(i), c0, n)
                    t2 = rot("tmp", 4)
                    S.op("scalar", lambda e, b4=b4, t2=t2, n=n: e.activation(out=tmp[:, t2, 0:n], in_=ps[:, b4, 0:n], func=AF.Sigmoid),
                         reads=[("ps", b4)], writes=[("tmp", t2)])
                    S.op("vector", lambda e, b3=b3, t2=t2, n=n: e.tensor_tensor(tmp[:, t2, 0:n], ps[:, b3, 0:n], tmp[:, t2, 0:n], ALU.mult),
                         reads=[("ps", b3), ("tmp", t2)], writes=[("tmp", t2)])
                    S.op("vector", lambda e, t1=t1, t2=t2, j=j, c0=c0, n=n: e.tensor_tensor(v.m(j, c0, n), tmp[:, t1, 0:n], tmp[:, t2, 0:n], ALU.add),
                         reads=[("tmp", t1), ("tmp", t2)], writes=[("R", 8 + j, i)])
            for j in range(8):
                wo, fo = load_w(w_out[l, :, j * 128:(j + 1) * 128])
                for i, (c0, n) in enumerate(v.nts):
                    b = proj(wo, fo, v.m, kR(8, i), c0, n)
                    S.op("vector", lambda e, b=b, j=j, c0=c0, n=n: e.tensor_tensor(xT[:, j, c0:c0 + n], xT[:, j, c0:c0 + n], ps[:, b, 0:n], ALU.add),
                         reads=[("ps", b), ("xT", j, i)], writes=[("xT", j, i)])

        def ffn(v, l, sc_first, sc_last, sample):
            tc = v.tc
            nn = len(v.nts)
            if sample:
                sl = rot("xs", 2)
                for k in range(2):
                    S.dma("sync", xstage[k * NS:(k + 1) * NS, sl, 0:1024], sffn[l, :, k, 0:1024], "xs%d" % sl, writes=[("xstage", sl)])
                sl2 = rot("xs", 2)
                for k in range(2):
                    S.dma("sync", xstage[k * NS:(k + 1) * NS, sl2, 0:1024], sffn[l, :, k, 1024:2048], "xs%d" % sl2, writes=[("xstage", sl2)])
                S.dma("sync", ffn_s[l, :, 0, :], sffn[l, :, 1, :], "ffs", writes=["OUT"])
                for part, slp in ((0, sl), (1, sl2)):
                    for g0 in range(0, 8, 4):
                        b = S.bank()
                        S.op("tensor", [lambda e, jj=jj, b=b, slp=slp: e.transpose(ps[:, b, (jj % 4) * 32:(jj % 4 + 1) * 32], xstage[0:32, slp, jj * 128:(jj + 1) * 128], idf[0:32, 0:32])
                                        for jj in range(g0, g0 + 4)], reads=[("xstage", slp), "idf"], writes=[("ps", b)])
                        eng_ = evac_eng()
                        for jj in range(g0, g0 + 4):
                            copy_op(eng_, hTs[:, part * 8 + jj, 0:2, :], ps[:, b, (jj % 4) * 32:(jj % 4 + 1) * 32].rearrange("p (k b) -> p k b", k=2),
                                    reads=[("ps", b)], writes=[("hTs", part * 8 + jj)])
                sl3 = rot("xs", 2)
                for k in range(2):
                    S.dma("sync", xstage[k * NS:(k + 1) * NS, sl3, 0:768], sffn[l, :, k, 2048:2816], "xs%d" % sl3, writes=[("xstage", sl3)])
                for g0 in range(0, 6, 3):
                    b = S.bank()
                    S.op("tensor", [lambda e, jj=jj, b=b: e.transpose(ps[:, b, (jj % 3) * 32:(jj % 3 + 1) * 32], xstage[0:32, sl3, jj * 128:(jj + 1) * 128], idf[0:32, 0:32])
                                    for jj in range(g0, g0 + 3)], reads=[("xstage", sl3), "idf"], writes=[("ps", b)])
                    eng_ = evac_eng()
                    for jj in range(g0, g0 + 3):
                        copy_op(eng_, hTs[:, 16 + jj, 0:2, :], ps[:, b, (jj % 3) * 32:(jj % 3 + 1) * 32].rearrange("p (k b) -> p k b", k=2),
                                reads=[("ps", b)], writes=[("hTs", 16 + jj)])
            rmsnorm_to_xn(v, l, O_NF)
            for jf in range(NFC):
                wh, fh = load_w(w_up[l, :, jf * 128:(jf + 1) * 128])
                wg, fg = load_w(w_up[l, :, DFF + jf * 128:DFF + (jf + 1) * 128])
                db_ = jf % 2
                S.op("vector", lambda e, jf=jf, db_=db_: e.tensor_tensor(
                    dgf[:, db_], idb[:, None, :].broadcast_to([128, 3, 128]),
                    vec[:, l * LV + O_FDW + jf * 3:l * LV + O_FDW + (jf + 1) * 3, None].broadcast_to([128, 3, 128]), ALU.mult),
                    reads=["idb", "vec"], writes=[("dgf", db_)])
                hb = jf % 2
                if not sample:
                    S.op("vector", lambda e, jf=jf, hb=hb: e.tensor_copy(hT[:, hb, 0:2], hhist[:, l, jf, :]),
                         reads=[("hhist", l, jf)], writes=[("hT", hb, "h")])
                bgs = []
                for i, (c0, n) in enumerate(v.nts):
                    bh = proj(wh, fh, v.xn, kxn(i), c0, n)
                    if sample:
                        S.op("scalar", lambda e, bh=bh, jf=jf: e.activation(out=hf32[:, jf, 0:NS], in_=ps[:, bh, 0:NS], func=AF.Copy),
                             reads=[("ps", bh)], writes=[("hf32", jf)])
                        S.op("vector", lambda e, jf=jf: e.tensor_copy(hTs[:, jf, 2, :], hf32[:, jf, 0:NS]), reads=[("hf32", jf)], writes=[("hTs", jf)])
                    else:
                        S.op("scalar", lambda e, bh=bh, hb=hb, c0=c0, n=n: e.activation(out=hT[:, hb, 2 + c0:2 + c0 + n], in_=ps[:, bh, 0:n], func=AF.Copy),
                             reads=[("ps", bh)], writes=[("hT", hb, i)])
                        if sc_last and c0 + n == tc:
                            S.op("scalar", lambda e, bh=bh, jf=jf, n=n: e.activation(out=hf32[:, jf, 0:2], in_=ps[:, bh, n - 2:n], func=AF.Copy), reads=[("ps", bh)], writes=[("hf32", jf)])
                if not sample:
                    if sc_first:
                        S.op("vector", lambda e, hb=hb: e.tensor_tensor(hT[:, hb, 2:2 + 384], hT[:, hb, 2:2 + 384], validb[:], ALU.mult),
                             reads=[("hT", hb, i) for i in range(nn)] + ["validb"], writes=[("hT", hb, i) for i in range(nn)])
                    S.op("vector", lambda e, jf=jf, hb=hb: e.tensor_copy(hhist[:, l, jf, :], hT[:, hb, tc:tc + 2]),
                         reads=[("hT", hb, i) for i in range(nn)] + [("hT", hb, "h")], writes=[("hhist", l, jf)])
                for i, (c0, n) in enumerate(v.nts):
                    bg = proj(wg, fg, v.xn, kxn(i), c0, n)
                    bc = S.bank()
                    if sample:
                        pairs = [(dgf[:, db_, k, :], hTs[:, jf, k, :]) for k in range(3)]
                        rd = [("dgf", db_), ("hTs", jf)]
                    else:
                        pairs = [(dgf[:, db_, k, :], hT[:, hb, c0 + k:c0 + k + n]) for k in range(3)]
                        rd = [("dgf", db_), ("hT", hb, "h")] + [("hT", hb, ii) for ii in range(max(0, i - 1), i + 1)]
                    mm_group(ps[:, bc, 0:n], pairs, reads=rd, bank_keys=[("ps", bc)])
                    ts = rot("tmp", 4)
                    S.op("scalar", lambda e, bc=bc, ts=ts, jf=jf, n=n: e.activation(out=tmp[:, ts, 0:n], in_=ps[:, bc, 0:n], func=AF.Gelu,
                                                                                  bias=vcol(l, O_FDB, jf), scale=1.0),
                         reads=[("ps", bc), "vec"], writes=[("tmp", ts)])
                    S.op("vector", lambda e, bg=bg, ts=ts, jf=jf, c0=c0, n=n: e.tensor_tensor(v.hg(jf, c0, n), tmp[:, ts, 0:n], ps[:, bg, 0:n], ALU.mult),
                         reads=[("ps", bg), ("tmp", ts)], writes=[("R", jf, i)])
            for j in range(8):
                wd_, fd = load_w(w_dn[l, :, j * 128:(j + 1) * 128], k_chunks=NFC)
                for i, (c0, n) in enumerate(v.nts):
                    b = proj(wd_, fd, v.hg, kR(0, i, NFC), c0, n, kc=NFC)
                    S.op("vector", lambda e, b=b, j=j, c0=c0, n=n: e.tensor_tensor(xT[:, j, c0:c0 + n], xT[:, j, c0:c0 + n], ps[:, b, 0:n], ALU.add),
                         reads=[("ps", b), ("xT", j, i)], writes=[("xT", j, i)])

        def final_out(v, t0, ntl, sample):
            cols = [(0, NS, NOWN * 128)] if sample else [(t * 128, 128, (t0 + t - 3) * 128) for t in range(ntl) if t0 + t >= 3]
            for (c0, n, row0) in cols:
                its = nts_of(v, c0, c0 + n)
                kk = [("xT", j, i) for j in range(8) for i in its]
                sl = rms_stats(lambda c0_, n_: xT[:, :, c0_:c0_ + n_], kk, c0, n)
                for j in range(8):
                    S.op("vector", lambda e, j=j, c0=c0, n=n, sl=sl: e.scalar_tensor_tensor(
                        out=ytmp[:, j, 0:n], in0=xT[:, j, c0:c0 + n], scalar=vec[:, 2 * LV + j:2 * LV + j + 1],
                        in1=st1[:, sl, 0:n], op0=ALU.mult, op1=ALU.mult),
                        reads=kk + [("st1", sl), "vec"], writes=[("ytmp", j)])
                xsl = rot("xs", 2)
                for half in range(2):
                    b = S.bank()
                    S.op("tensor", [lambda e, j=j, b=b, n=n: e.transpose(ps[0:n, b, (j % 4) * 128:(j % 4 + 1) * 128], ytmp[:, j, 0:n], idf[:])
                                    for j in range(half * 4, half * 4 + 4)], reads=[("ytmp", j) for j in range(8)] + ["idf"], writes=[("ps", b)])
                    copy_op(evac_eng(), xstage[0:n, xsl, half * 512:(half + 1) * 512], ps[0:n, b, :], reads=[("ps", b)], writes=[("xstage", xsl)])
                S.dma("sync", y[row0:row0 + n, :], xstage[0:n, xsl, :], "xs%d" % xsl, reads=[("xstage", xsl)], writes=["OUT"])

        for l in range(2):
            S.dma("sync", conv_s[l, :, 0:29, :], sconv[l, :, 1:30, :], "cvs", writes=["OUT"])

        sc_list = [(t0, ntl, False) for (t0, ntl) in SCS] + [(0, 0, True)]
        if dbg == "sample_l0":
            sc_list = [(0, 0, True)]
        if dbg == "sc0":
            sc_list = sc_list[:1]
        for si, (t0, ntl, sample) in enumerate(sc_list):
            if sample:
                S.barrier()
            S.new_phase()
            tc = NS if sample else ntl * 128
            v = make_views(tc)
            sc_first = (si == 0)
            sc_last = (si == len(SCS) - 1)
            import os as _os
            kstop = int(_os.environ.get("KSTOP", "99"))
            if kstop <= 0:
                break
            load_x(v, t0, ntl, sample)
            for l in range(2):
                if kstop <= 1:
                    break
                rmsnorm_to_xn(v, l, O_NM)
                if kstop <= 2:
                    break
                conv_branch(v, l, sc_first, sc_last, sample, part=0)
                q_proj(v, l)
                conv_branch(v, l, sc_first, sc_last, sample, part=1)
                if kstop <= 3:
                    break
                if sc_last or sample:
                    if sample:
                        emit_rows(l, lambda j: uf32[:, j, 0:NS], NS, 8, lambda c, w: conv_s[l, :, 29, c:c + w], [("uf32", j) for j in range(8)])
                    else:
                        emit_rows(l, lambda j: uf32[:, j, :], 30, 8, lambda c, w: conv_p[l, :, c:c + w], [("uf32", j) for j in range(8)])
                if kstop <= 4:
                    break
                qkv(v, l, sc_last, sample)
                if kstop <= 5:
                    break
                if sample:
                    attention_sample(v, l)
                else:
                    attention(v, l, t0)
                if kstop <= 6:
                    break
                gates_out(v, l)
                if dbg == "sample_l0":
                    break
                ffn(v, l, sc_first, sc_last, sample)
                if sc_last or sample:
                    if sample:
                        emit_rows(l, lambda j: hf32[:, j, 0:NS], NS, NFC, lambda c, w: ffn_s[l, :, 1, c:c + w], [("hf32", j) for j in range(NFC)])
                    else:
                        emit_rows(l, lambda j: hf32[:, j, 0:2], 2, NFC, lambda c, w: ffn_p[l, :, c:c + w], [("hf32", j) for j in range(NFC)])
            if dbg is None:
                final_out(v, t0, ntl, sample)

        waits = S._deps("sync", ["OUT"], [])
        allw = []
        for c in S.chan.values():
            allw.append((c[0], c[1]))
        S.ops["sync"].append((allw, [], None, ""))
        S.emit()
    return nc


_NC_CACHE = {}


def _host_layout(inputs):
    f = lambda a: np.ascontiguousarray(np.asarray(a, dtype=np.float32))
    xp = f(inputs["x_prompt"])[0]
    xsamp = f(inputs["x_sample"])[:, 0, :]
    meta = f(inputs["meta_tokens"])
    seq = np.concatenate([np.zeros((256 + 112, D), np.float32), meta, xp], axis=0)
    vecs = np.zeros((128, NV), np.float32)

    def colmaj(a):
        return a.reshape(-1, 128).T

    for l in range(2):
        o = l * LV
        vecs[:, o + O_NM:o + O_NM + 8] = colmaj(f(inputs["norm_mix"])[l])
        vecs[:, o + O_CDB:o + O_CDB + 8] = colmaj(f(inputs["conv_db"])[l])
        vecs[:, o + O_LG:o + O_LG + 8] = colmaj(f(inputs["conv_ln_g"])[l])
        vecs[:, o + O_LB:o + O_LB + 8] = colmaj(f(inputs["conv_ln_b"])[l])
        vecs[:, o + O_NF:o + O_NF + 8] = colmaj(f(inputs["norm_ffn"])[l])
        vecs[:, o + O_FDB:o + O_FDB + 22] = colmaj(f(inputs["ffn_db"])[l])
        cdw = f(inputs["conv_dw"])[l]
        vecs[:, o + O_CDW:o + O_CDW + 248] = cdw.T.reshape(8, 128, 31).transpose(1, 0, 2).reshape(128, 248)
        fdw = f(inputs["ffn_dw"])[l]
        vecs[:, o + O_FDW:o + O_FDW + 66] = fdw.T.reshape(22, 128, 3).transpose(1, 0, 2).reshape(128, 66)
    vecs[:, 2 * LV:2 * LV + 8] = colmaj(f(inputs["norm_final"]))
    sinks = f(inputs["attn_sinks"])
    perm = np.array([4 * k + g for k in range(4) for g in (0, 2, 1, 3)])
    sinkrow = np.ascontiguousarray(np.broadcast_to(sinks[None][:, :, perm], (128, 2, 16)))
    sinkcol = np.zeros((128, 2, 4), np.float32)
    for r in range(128):
        for kh in range(4):
            sinkcol[r, :, kh] = sinks[:, 4 * kh + r // 32]
    masks_s = np.full((128, 16), NEGM, np.float32)
    rowsel = np.zeros((128, 16), np.float32)
    for r in range(128):
        s = r % 32
        if s < 16:
            masks_s[r, s] = 0.0
            rowsel[r, s] = 1.0
    qi = np.arange(128)[:, None]
    kj = np.arange(256)[None, :]
    in_maps = []
    for c in range(NCORES):
        b0 = 16 * c - 2
        xin = seq[(b0 + 2) * 128:(b0 + 2 + NT) * 128]
        m = np.zeros((128, 5, 256), np.float32)
        for mi in range(5):
            gi = mi if mi < 4 else 8
            qpos = (b0 + gi) * 128 + qi
            kpos = (b0 + gi - 1) * 128 + kj
            rel = qpos - kpos
            ok = (rel >= 0) & (rel <= 128) & (kpos >= 112)
            if mi == 0:
                ok = ok & (kj >= 128)
            m[:, mi, :] = np.where(ok, 0.0, NEGM)
        pos = (b0 * 128 + np.arange(384))
        valid = np.ascontiguousarray(np.broadcast_to((pos >= 112).astype(np.float32)[None], (128, 384)))
        sl = slice(NS * c, NS * (c + 1))
        in_maps.append({
            "xin": np.ascontiguousarray(xin), "xs": np.ascontiguousarray(xsamp[sl]),
            "masks": m, "masks_s": masks_s, "rowsel": rowsel, "valid": valid, "vecs": vecs,
            "sinkrow": sinkrow, "sinkcol": sinkcol,
            "w_in": f(inputs["w_in"]), "w_pw": f(inputs["w_conv_pw"]), "w_ao": f(inputs["w_attn_o"]),
            "w_out": f(inputs["w_out"]), "w_up": f(inputs["w_ffn_up"]), "w_dn": f(inputs["w_ffn_down"]),
            "ck": np.ascontiguousarray(f(inputs["cache_swa_k"])[:, sl].reshape(2, NS, 128, 256)),
            "cv": np.ascontiguousarray(f(inputs["cache_swa_v"])[:, sl].reshape(2, NS, 128, 256)),
            "sconv": np.ascontiguousarray(f(inputs["state_conv"])[:, sl]),
            "sffn": np.ascontiguousarray(f(inputs["state_ffn_conv"])[:, sl]),
        })
    return in_maps


def kernel(**inputs):
    in_maps = _host_layout(inputs)
    if "nc" not in _NC_CACHE:
        _NC_CACHE["nc"] = build()
    nc = _NC_CACHE["nc"]
    res = run_bass_kernel_spmd(nc, in_maps, core_ids=list(range(NCORES)))
    r = res.results
    y_prompt = np.concatenate([r[c]["y"][:NOWN * 128] for c in range(NCORES)], axis=0)[None]
    y_sample = np.concatenate([r[c]["y"][NOWN * 128:] for c in range(NCORES)], axis=0)[:, None, :]
    last = r[NCORES - 1]
    k_p = last["kv_p"][:, 0].reshape(2, 1, 128, 4, 64)
    v_p = last["kv_p"][:, 1].reshape(2, 1, 128, 4, 64)
    c_p = last["conv_p"].reshape(2, 1, 30, D)
    f_p = last["ffn_p"].reshape(2, 1, 2, DFF)
    k_s = np.concatenate([r[c]["k_s"] for c in range(NCORES)], axis=1).reshape(2, 128, 128, 4, 64)
    v_s = np.concatenate([r[c]["v_s"] for c in range(NCORES)], axis=1).reshape(2, 128, 128, 4, 64)
    c_s = np.concatenate([r[c]["conv_s"] for c in range(NCORES)], axis=1)
    f_s = np.concatenate([r[c]["ffn_s"] for c in range(NCORES)], axis=1)
    f32 = lambda a: np.ascontiguousarray(a, dtype=np.float32)
    return (f32(y_prompt), f32(y_sample), f32(k_p), f32(v_p), f32(c_p), f32(f_p), f32(k_s), f32(v_s), f32(c_s), f32(f_s))
```

```python
import numpy as np
from contextlib import ExitStack
import concourse.bass as bass
import concourse.mybir as mybir
from concourse.bass_utils import run_bass_kernel_spmd

F32 = mybir.dt.float32
BF16 = mybir.dt.bfloat16
AF = mybir.ActivationFunctionType
ALU = mybir.AluOpType
AX = mybir.AxisListType

NCORES = 8
NT = 19
NOWN = 16
SCS = [(0, 5), (5, 5), (10, 5), (15, 4)]
TCM = 640
NS = 16
D = 1024
DFF = 2816
NFC = 22
DIN = 5632
EPS = 1e-6
SCALE = 0.125
LV = 8 * 5 + 22 + 8 * 31 + 22 * 3
NV = 2 * LV + 8
O_NM, O_CDB, O_LG, O_LB, O_NF, O_FDB, O_CDW, O_FDW = 0, 8, 16, 24, 32, 40, 62, 62 + 248
NEGM = -30000.0

ENGS = ("tensor", "vector", "scalar", "gpsimd", "sync")


class Sched:
    def __init__(self, nc, stack):
        self.nc = nc
        self.stack = stack
        self.ops = {e: [] for e in ENGS}
        self.sem = {}
        self.cnt = {}
        self.seen = {e: {} for e in ENGS}
        self.res = {}
        self.chan = {}
        self.nsem = 0
        self.pb = 0
        self.reserved = set()
        self.new_phase()

    debug_tags = False
    names = []

    def _tag(self):
        if not self.debug_tags:
            return ""
        import traceback
        st = traceback.extract_stack(limit=6)
        return ">".join(str(f.lineno) for f in st[:-2])

    def _newsem(self, name):
        self.nsem += 1
        return self.stack.enter_context(self.nc.semaphore(name))

    def new_phase(self):
        for e in ("tensor", "vector", "scalar", "gpsimd"):
            self.sem[e] = self._newsem("s_%s_%d" % (e, self.nsem))
            self.cnt[e] = 0

    def bank(self, n=1):
        while True:
            if self.pb % 8 + n > 8:
                self.pb += 8 - self.pb % 8
            b = self.pb % 8
            if any((b + i) in self.reserved for i in range(n)):
                self.pb += 1
                continue
            self.pb += n
            return b

    def _deps(self, eng, reads, writes):
        evs = []
        for r in reads:
            st = self.res.get(r)
            if st is not None and st[0] is not None:
                evs.append(st[0])
            if st is not None and isinstance(r, tuple) and r[0] == "ps":
                evs.extend(ev for ev in st[1] if ev[2] != eng)
        for w in writes:
            st = self.res.get(w)
            if st is not None:
                if st[0] is not None:
                    evs.append(st[0])
                evs.extend(st[1])
        need = {}
        for (s, v, e) in evs:
            if e == "tensor" and eng == "tensor":
                continue
            k = id(s)
            if self.seen[eng].get(k, 0) >= v:
                continue
            if k not in need or need[k][1] < v:
                need[k] = (s, v)
        waits = []
        for k, (s, v) in need.items():
            self.seen[eng][k] = v
            waits.append((s, v))
        return waits

    def _commit(self, ev, reads, writes):
        for r in reads:
            st = self.res.setdefault(r, [None, []])
            st[1].append(ev)
        for w in writes:
            self.res[w] = [ev, []]

    def op(self, eng, fns, reads=(), writes=()):
        if callable(fns):
            fns = [fns]
        waits = self._deps(eng, reads, writes)
        self.cnt[eng] += 1
        ev = (self.sem[eng], self.cnt[eng], eng)
        self.ops[eng].append((waits, fns, (self.sem[eng], 1), self._tag()))
        self._commit(ev, reads, writes)

    def dma(self, queue, out, in_, chan, reads=(), writes=()):
        self.nout = getattr(self, "nout", 0) + 1
        writes = [("OUT", self.nout) if w == "OUT" else w for w in writes]
        if chan not in self.chan:
            self.chan[chan] = [self._newsem("d_%s" % chan), 0]
        c = self.chan[chan]
        waits = self._deps(queue, reads, writes)
        c[1] += 16
        ev = (c[0], c[1], "dma")
        self.ops[queue].append((waits, [lambda e: e.dma_start(out=out, in_=in_)], (c[0], 16), self._tag()))
        self._commit(ev, reads, writes)

    def barrier(self):
        evs = []
        for e in ("tensor", "vector", "scalar", "gpsimd"):
            if self.cnt[e] > 0:
                evs.append((self.sem[e], self.cnt[e]))
        for c in self.chan.values():
            evs.append((c[0], c[1]))
        for eng in ENGS:
            waits = []
            for (s, v) in evs:
                if self.seen[eng].get(id(s), 0) < v:
                    self.seen[eng][id(s)] = v
                    waits.append((s, v))
            self.ops[eng].append((waits, [], None, ""))

    def emit(self):
        nc = self.nc
        with nc.Block() as block:
            for ename in ENGS:
                lst = self.ops[ename]

                def body(e, lst=lst):
                    for waits, fns, inc, tag in lst:
                        for (s, v) in waits:
                            e.wait_ge(s, v)
                        ins = None
                        for f in fns:
                            ins = f(e)
                            if self.debug_tags:
                                ins.annotate(tag)
                                self.names.append((ins.ins.name, ename, tag, str(ins)[:300]))
                        if inc is not None and ins is not None:
                            ins.then_inc(inc[0], inc[1])

                getattr(block, ename)(body)


def ntile_list(tc):
    if tc <= 320:
        return [(0, tc)]
    h = tc // 2
    return [(0, h), (h, tc - h)]


def build(dbg=None):
    nc = bass.Bass("TRN2", target_bir_lowering=False)

    def din(name, shape):
        return nc.dram_tensor(name, shape, F32, kind="ExternalInput").ap()

    def dout(name, shape):
        return nc.dram_tensor(name, shape, F32, kind="ExternalOutput").ap()

    xin = din("xin", [NT * 128, D])
    xs = din("xs", [NS, D])
    masks = din("masks", [128, 5, 256])
    masks_s = din("masks_s", [128, 16])
    rowsel = din("rowsel", [128, 16])
    valid = din("valid", [128, 384])
    vecs = din("vecs", [128, NV])
    sinkrow = din("sinkrow", [128, 2, 16])
    sinkcol = din("sinkcol", [128, 2, 4])
    w_in = din("w_in", [2, D, DIN])
    w_pw = din("w_pw", [2, D, D])
    w_ao = din("w_ao", [2, D, D])
    w_out = din("w_out", [2, D, D])
    w_up = din("w_up", [2, D, 2 * DFF])
    w_dn = din("w_dn", [2, DFF, D])
    ck = din("ck", [2, NS, 128, 256])
    cv = din("cv", [2, NS, 128, 256])
    sconv = din("sconv", [2, NS, 30, D])
    sffn = din("sffn", [2, NS, 2, DFF])

    y = dout("y", [NOWN * 128 + NS, D])
    kv_p = dout("kv_p", [2, 2, 128, 256])
    conv_p = dout("conv_p", [2, 30, D])
    ffn_p = dout("ffn_p", [2, 2, DFF])
    k_s = dout("k_s", [2, NS, 128, 256])
    v_s = dout("v_s", [2, NS, 128, 256])
    conv_s = dout("conv_s", [2, NS, 30, D])
    ffn_s = dout("ffn_s", [2, NS, 2, DFF])

    with ExitStack() as es:
        def sb(name, shape, dt):
            return es.enter_context(nc.sbuf_tensor(name, shape, dt))

        idf = sb("idf", [128, 128], F32)
        idb = sb("idb", [128, 128], BF16)
        onesb = sb("onesb", [128, 128], BF16)
        epsc = sb("epsc", [128, 1], F32)
        vec = sb("vec", [128, NV], F32)
        masksb = sb("masksb", [128, 4, 16], BF16)
        rowselt = sb("rowselt", [128, 16], F32)
        validb = sb("validb", [128, 384], BF16)
        snk = sb("snk", [128, 2, 16], F32)
        nsnk = sb("nsnk", [128, 2, 16], F32)
        snkc = sb("snkc", [128, 2, 4], F32)
        nsnkc = sb("nsnkc", [128, 2, 4], F32)

        xT = sb("xT", [128, 8, TCM], F32)
        xnT = sb("xnT", [128, 8, TCM], BF16)
        R = sb("R", [128, 24, TCM], BF16)
        aT = sb("aT", [128, 8, TCM], BF16)
        kT = sb("kT", [128, 4, 128 + TCM], BF16)
        Vt = sb("Vt", [128, 6, 4, 128], BF16)
        kprev = sb("kprev", [128, 2, 4, 128], BF16)
        vprev = sb("vprev", [128, 2, 4, 128], BF16)
        uT = sb("uT", [128, 2, 30 + TCM], BF16)
        uhist = sb("uhist", [128, 2, 8, 30], BF16)
        hT = sb("hT", [128, 2, 2 + TCM], BF16)
        hhist = sb("hhist", [128, 2, 22, 2], BF16)
        NWS = 6
        wsl = sb("wsl", [128, NWS, 8, 128], BF16)
        wdsl = sb("wdsl", [128, 2, 22, 128], BF16)
        wv = sb("wv", [128, 8, 256], BF16)
        wk = sb("wk", [128, 8, 256], BF16)
        dg = sb("dg", [128, 31, 128], BF16)
        dgf = sb("dgf", [128, 2, 3, 128], BF16)
        xstage = sb("xstage", [128, 2, 1024], F32)
        ostage = sb("ostage", [128, 2, 512], F32)
        sq = sb("sq", [128, 8, 320], BF16)
        st1 = sb("st1", [128, 2, 320], F32)
        st2 = sb("st2", [128, 2, 320], F32)
        tmp = sb("tmp", [128, 4, 320], F32)
        esb = sb("esb", [128, 2, 4, 256], F32)
        esnk = sb("esnk", [128, 2, 16], F32)
        pT = sb("pT", [128, 2, 4, 2, 128], BF16)
        pbf = sb("pbf", [128, 2, 4, 256], BF16)
        maskb2 = sb("maskb2", [128, 5, 512], BF16)
        sm = sb("sm", [128, 8, 8, 4], F32)
        ytmp = esb[:, 0].rearrange("p g (u c) -> p (g u) c", u=2)
        uf32 = sb("uf32", [128, 8, 30], F32)
        hf32 = sb("hf32", [128, 22, 16], F32)
        hf32p = sb("hf32p", [128, 22, 2], F32)
        uf32s = sb("uf32s", [128, 8, NS], F32)
        uTs = sb("uTs", [128, 8, 31, NS], BF16)
        hTs = R[:, 0:22, 544:544 + 3 * NS].rearrange("p j (k b) -> p j k b", k=3)
        kvst = sb("kvst", [128, 2, 2, 256], F32)
        kdup = sb("kdup", [128, 1, 4, 128], F32)
        ess = sb("ess", [128, 1, 4, 144], F32)
        KTb = pbf[:, :, :, 0:128]
        Vb = pbf[:, :, :, 128:256]
        pTs = pT[:, :, :, 0, :]
        pTn = pT[0:NS, :, :, 1, :]
        Vnew = Vt[0:NS, 5]

        ps = es.enter_context(nc.psum_tensor("ps", [128, 8, 512], F32))

        S = Sched(nc, es)
        ctr = {"w": 0, "wd": 0, "xs": 0, "os": 0, "tmp": 0, "st": 0, "es": 0, "sm": 0, "kv": 0, "ev": 0}

        def rot(name, n):
            v = ctr[name] % n
            ctr[name] += 1
            return v

        def evac_eng():
            return "vector" if rot("ev", 2) == 0 else "scalar"

        def copy_op(eng, out, in_, reads, writes):
            if eng == "scalar":
                S.op("scalar", lambda e: e.activation(out=out, in_=in_, func=AF.Copy), reads=reads, writes=writes)
            else:
                S.op(eng, lambda e: e.tensor_copy(out, in_), reads=reads, writes=writes)

        S.dma("sync", vec[:], vecs[:, :], "c_vec", writes=["vec"])
        S.dma("sync", snk[:], sinkrow[:, :, :], "c_snk", writes=["snk"])
        S.dma("sync", snkc[:], sinkcol[:, :, :], "c_snkc", writes=["snkc"])
        S.dma("sync", rowselt[:], rowsel[:, :], "c_rs", writes=["rowsel"])
        S.dma("gpsimd", maskb2[:, :, 0:256], masks[:, :, :], "c_mask", writes=["maskb"])
        S.dma("gpsimd", maskb2[:, :, 256:512], masks[:, :, :], "c_mask", writes=["maskb"])
        S.dma("gpsimd", validb[:], valid[:, :], "c_valid", writes=["validb"])
        for k4 in range(4):
            S.dma("gpsimd", masksb[:, k4, :], masks_s[:, :], "c_masks", writes=["masksb"])
        S.op("gpsimd", lambda e: e.memset(idf[:], 1.0), writes=["idf"])
        S.op("gpsimd", lambda e: e.affine_select(idf[:], idf[:], pattern=[[-1, 128]], compare_op=ALU.is_equal,
                                                 fill=0.0, base=0, channel_multiplier=1), reads=["idf"], writes=["idf"])
        S.op("vector", lambda e: e.tensor_copy(idb[:], idf[:]), reads=["idf"], writes=["idb"])
        S.op("vector", lambda e: e.memset(onesb[:], 1.0 / 1024.0), writes=["onesb"])
        S.op("vector", lambda e: e.memset(epsc[:], EPS), writes=["epsc"])
        S.op("vector", lambda e: e.memset(kprev[:], 0.0), writes=[("kprev", 0), ("kprev", 1)])
        S.op("vector", lambda e: e.memset(vprev[:], 0.0), writes=[("vprev", 0), ("vprev", 1)])
        S.op("vector", lambda e: e.memset(uhist[:], 0.0), writes=[("uhist", l, j) for l in range(2) for j in range(8)])
        S.op("vector", lambda e: e.memset(hhist[:], 0.0), writes=[("hhist", l, j) for l in range(2) for j in range(22)])
        S.op("vector", lambda e: e.tensor_scalar(nsnk[:], snk[:], -1.0, None, ALU.mult), reads=["snk"], writes=["nsnk"])
        S.op("scalar", lambda e: e.activation(out=esnk[:], in_=snk[:], func=AF.Exp), reads=["snk"], writes=["esnk"])
        S.op("vector", lambda e: e.tensor_scalar(nsnkc[:], snkc[:], -1.0, None, ALU.mult), reads=["snkc"], writes=["nsnkc"])

        def vcol(l, off, j):
            c = l * LV + off + j
            return vec[:, c:c + 1]

        def load_w(src, k_chunks=8):
            if k_chunks == 8:
                s = rot("w", NWS)
                S.dma("gpsimd", wsl[:, s, :, :], src.rearrange("(k p) m -> p k m", p=128), "w%d" % s, writes=[("w", s)])
                return ("w", s), (lambda k, s=s: wsl[:, s, k, :])
            s = rot("wd", 2)
            S.dma("gpsimd", wdsl[:, s, :, :], src.rearrange("(k p) m -> p k m", p=128), "wd%d" % s, writes=[("wd", s)])
            return ("wd", s), (lambda k, s=s: wdsl[:, s, k, :])

        def mm_group(out_ap, pairs, reads, bank_keys):
            n = len(pairs)
            fns = []
            for i, (l_, r_) in enumerate(pairs):
                fns.append(lambda e, l_=l_, r_=r_, i=i: e.matmul(out_ap, l_, r_, start=(i == 0), stop=(i == n - 1)))
            S.op("tensor", fns, reads=reads, writes=bank_keys)

        def proj(wkey, wfn, src, skeys, c0, n, kc=8):
            b = S.bank()
            mm_group(ps[:, b, 0:n], [(wfn(k), src(k, c0, n)) for k in range(kc)],
                     reads=[wkey] + skeys, bank_keys=[("ps", b)])
            return b

        class V:
            pass

        def make_views(pc, has_sample):
            v = V()
            v.pc = pc
            v.so = pc if has_sample else None
            v.tc = tc = pc + (NS if has_sample else 0)
            v.nts = ntile_list(tc)
            v.si = len(v.nts) - 1
            v.x = lambda j, c0, n: xT[:, j, c0:c0 + n]
            v.xn = lambda j, c0, n: xnT[:, j, c0:c0 + n]
            v.q = lambda j, c0, n: R[:, j, c0:c0 + n]
            v.m = lambda j, c0, n: R[:, 8 + j, c0:c0 + n]
            v.s = lambda j, c0, n: R[:, 16 + j, c0:c0 + n]
            v.hg = lambda j, c0, n: R[:, j, c0:c0 + n]
            v.a = lambda j, c0, n: aT[:, j, c0:c0 + n]
            return v

        def kx(i):
            return [("xT", j, i) for j in range(8)]

        def kxn(i):
            return [("xnT", j, i) for j in range(8)]

        def kR(base, i, cnt=8):
            return [("R", base + j, i) for j in range(cnt)]

        def ka(i):
            return [("aT", j, i) for j in range(8)]

        def nts_of(v, c0, c1):
            return [i for i, (a, n) in enumerate(v.nts) if a < c1 and a + n > c0]

        def rms_stats(src_fn, src_keys, c0, n):
            S.op("scalar", lambda e: e.activation(out=sq[:, :, 0:n], in_=src_fn(c0, n), func=AF.Square),
                 reads=src_keys, writes=["sq"])
            b = S.bank()
            mm_group(ps[:, b, 0:n], [(onesb[:], sq[:, j, 0:n]) for j in range(8)], reads=["sq", "onesb"], bank_keys=[("ps", b)])
            sl = rot("st", 2)
            S.op("scalar", lambda e: e.activation(out=st1[:, sl, 0:n], in_=ps[:, b, 0:n], func=AF.Sqrt, bias=epsc[:, 0:1], scale=1.0),
                 reads=[("ps", b), "epsc"], writes=[("st1", sl)])
            S.op("vector", lambda e: e.reciprocal(st1[:, sl, 0:n], st1[:, sl, 0:n]), reads=[("st1", sl)], writes=[("st1", sl)])
            return sl

        def rmsnorm_to_xn(v, l, goff):
            for i, (c0, n) in enumerate(v.nts):
                sl = rms_stats(lambda c0, n: xT[:, :, c0:c0 + n], kx(i), c0, n)
                for j in range(8):
                    S.op("vector", lambda e, j=j, c0=c0, n=n, sl=sl: e.scalar_tensor_tensor(
                        out=xnT[:, j, c0:c0 + n], in0=xT[:, j, c0:c0 + n], scalar=vcol(l, goff, j) if l < 2 else vec[:, 2 * LV + j:2 * LV + j + 1],
                        in1=st1[:, sl, 0:n], op0=ALU.mult, op1=ALU.mult),
                        reads=[("xT", j, i), ("st1", sl), "vec"], writes=[("xnT", j, i)])

        def load_x(v, t0, ntl):
            if v.so is not None:
                so = v.so
                sl = rot("xs", 2)
                S.dma("sync", xstage[0:NS, sl, :], xs[:, :], "xs%d" % sl, writes=[("xstage", sl)])
                b = S.bank()
                fns = [lambda e, j=j: e.transpose(ps[:, b, j * NS:(j + 1) * NS], xstage[0:NS, sl, j * 128:(j + 1) * 128], idf[0:NS, 0:NS])
                       for j in range(8)]
                S.op("tensor", fns, reads=[("xstage", sl), "idf"], writes=[("ps", b)])
                S.op("vector", lambda e: e.tensor_copy(xT[:, :, so:so + NS], ps[:, b, 0:8 * NS].rearrange("p (j t) -> p j t", j=8)),
                     reads=[("ps", b)], writes=kx(v.si))
            for t in range(ntl):
                sl = rot("xs", 2)
                S.dma("sync", xstage[:, sl, :], xin[(t0 + t) * 128:(t0 + t + 1) * 128, :], "xs%d" % sl, writes=[("xstage", sl)])
                for half in range(2):
                    b = S.bank()
                    fns = [lambda e, j=j, b=b, sl=sl: e.transpose(ps[:, b, (j % 4) * 128:(j % 4 + 1) * 128],
                                                           xstage[:, sl, j * 128:(j + 1) * 128], idf[:])
                           for j in range(half * 4, half * 4 + 4)]
                    S.op("tensor", fns, reads=[("xstage", sl), "idf"], writes=[("ps", b)])
                    its = nts_of(v, t * 128, t * 128 + 128)
                    copy_op(evac_eng(), xT[:, half * 4:half * 4 + 4, t * 128:(t + 1) * 128],
                            ps[:, b, :].rearrange("p (j t) -> p j t", j=4),
                            reads=[("ps", b)], writes=[("xT", j, i) for j in range(half * 4, half * 4 + 4) for i in its])

        def conv_branch(v, l, sc_first, sc_last, part=None):
            tc = v.tc
            if part != 1:
              conv_part(v, l, sc_first, sc_last)
            if part != 0:
              ln_part(v, l)

        def conv_part(v, l, sc_first, sc_last):
            tc, pc, so = v.tc, v.pc, v.so
            import os as _os
            if so is not None:
                for g in range(int(_os.environ.get("KG", "4"))):
                    sl = rot("xs", 2)
                    S.dma("sync", xstage[0:120, sl, :], sconv[l, 4 * g:4 * g + 4, :, :].rearrange("b k c -> (b k) c"),
                          "xs%d" % sl, writes=[("xstage", sl)])
                    for half in range(2):
                        b = S.bank()
                        kvar = _os.environ.get("KVAR", "")
                        W_ = 128 if "A" in kvar else 120
                        fns = [lambda e, j=j, b=b, sl=sl, W_=W_: e.transpose(ps[:, b, (j % 4) * W_:(j % 4) * W_ + 120],
                                                               xstage[0:120, sl, j * 128:(j + 1) * 128], idf[0:120, 0:120])
                               for j in range(half * 4, half * 4 + 4)]
                        S.op("tensor", fns, reads=[("xstage", sl), "idf"], writes=[("ps", b)])
                        eng_ = evac_eng()
                        for j in range(half * 4, half * 4 + 4):
                            copy_op(eng_, uTs[:, j, 0:30, 4 * g:4 * g + 4],
                                    ps[:, b, (j % 4) * W_:(j % 4) * W_ + 120].rearrange("p (b k) -> p k b", b=4),
                                    reads=[("ps", b)], writes=[("uTs", j)])
            import os as _os
            ksub = int(_os.environ.get("KSUB", "99"))
            if ksub <= 1:
                return
            for j in range(8):
                if ksub <= 2 and j >= 1:
                    return
                wa, fa = load_w(w_in[l, :, j * 128:(j + 1) * 128])
                wb, fb = load_w(w_in[l, :, D + j * 128:D + (j + 1) * 128])
                S.op("vector", lambda e, j=j: e.tensor_tensor(
                    dg[:], idb[:, None, :].broadcast_to([128, 31, 128]),
                    vec[:, l * LV + O_CDW + j * 31:l * LV + O_CDW + (j + 1) * 31, None].broadcast_to([128, 31, 128]), ALU.mult),
                    reads=["idb", "vec"], writes=["dg"])
                ub = j % 2
                S.op("vector", lambda e, j=j, ub=ub: e.tensor_copy(uT[:, ub, 0:30], uhist[:, l, j, :]),
                     reads=[("uhist", l, j)], writes=[("uT", ub, "h")])
                for i, (c0, n) in enumerate(v.nts):
                    ba = proj(wa, fa, v.xn, kxn(i), c0, n)
                    bb = proj(wb, fb, v.xn, kxn(i), c0, n)
                    ts = rot("tmp", 4)
                    S.op("scalar", lambda e, bb=bb, ts=ts, n=n: e.activation(out=tmp[:, ts, 0:n], in_=ps[:, bb, 0:n], func=AF.Sigmoid),
                         reads=[("ps", bb)], writes=[("tmp", ts)])
                    S.op("vector", lambda e, ba=ba, ts=ts, c0=c0, n=n, ub=ub: e.tensor_tensor(
                        uT[:, ub, 30 + c0:30 + c0 + n], ps[:, ba, 0:n], tmp[:, ts, 0:n], ALU.mult),
                        reads=[("ps", ba), ("tmp", ts)], writes=[("uT", ub, i)])
                    if sc_last and c0 <= pc - 30 and c0 + n >= pc:
                        lo = pc - 30 - c0
                        S.op("vector", lambda e, ba=ba, ts=ts, lo=lo, j=j: e.tensor_tensor(
                            uf32[:, j, :], ps[:, ba, lo:lo + 30], tmp[:, ts, lo:lo + 30], ALU.mult),
                            reads=[("ps", ba), ("tmp", ts)], writes=[("uf32", j)])
                    if so is not None and i == v.si:
                        lo = so - c0
                        S.op("vector", lambda e, ba=ba, ts=ts, lo=lo, j=j: e.tensor_tensor(uf32s[:, j, :], ps[:, ba, lo:lo + NS], tmp[:, ts, lo:lo + NS], ALU.mult),
                             reads=[("ps", ba), ("tmp", ts)], writes=[("uf32s", j)])
                        S.op("vector", lambda e, j=j: e.tensor_copy(uTs[:, j, 30, :], uf32s[:, j, :]),
                             reads=[("uf32s", j)], writes=[("uTs", j)])
                nn = len(v.nts)
                if sc_first:
                    S.op("vector", lambda e, ub=ub: e.tensor_tensor(uT[:, ub, 30:30 + 384], uT[:, ub, 30:30 + 384], validb[:], ALU.mult),
                         reads=[("uT", ub, i) for i in range(nn)] + ["validb"], writes=[("uT", ub, i) for i in range(nn)])
                S.op("vector", lambda e, j=j, ub=ub: e.tensor_copy(uhist[:, l, j, :], uT[:, ub, pc:pc + 30]),
                     reads=[("uT", ub, i) for i in range(nn)] + [("uT", ub, "h")], writes=[("uhist", l, j)])
                for i, (c0, n) in enumerate(v.nts):
                    b = S.bank()
                    pairs = [(dg[:, k, :], uT[:, ub, c0 + k:c0 + k + n]) for k in range(31)]
                    rd = ["dg", ("uT", ub, "h")] + [("uT", ub, ii) for ii in range(max(0, i - 1), i + 1)]
                    mm_group(ps[:, b, 0:n], pairs, reads=rd, bank_keys=[("ps", b)])
                    S.op("scalar", lambda e, b=b, j=j, c0=c0, n=n: e.activation(out=v.s(j, c0, n), in_=ps[:, b, 0:n], func=AF.Identity,
                                                                                 bias=vcol(l, O_CDB, j), scale=1.0),
                         reads=[("ps", b), "vec"], writes=[("R", 16 + j, i)])
                    if so is not None and i == v.si:
                        b = S.bank()
                        mm_group(ps[:, b, 0:NS], [(dg[:, k, :], uTs[:, j, k, :]) for k in range(31)], reads=["dg", ("uTs", j)], bank_keys=[("ps", b)])
                        S.op("scalar", lambda e, b=b, j=j: e.activation(out=v.s(j, so, NS), in_=ps[:, b, 0:NS], func=AF.Identity,
                                                                        bias=vcol(l, O_CDB, j), scale=1.0),
                             reads=[("ps", b), "vec"], writes=[("R", 16 + j, i)])
        def ln_part(v, l):
            for i, (c0, n) in enumerate(v.nts):
                bm = S.bank()
                mm_group(ps[:, bm, 0:n], [(onesb[:], v.s(j, c0, n)) for j in range(8)], reads=kR(16, i) + ["onesb"], bank_keys=[("ps", bm)])
                S.op("scalar", lambda e, c0=c0, n=n: e.activation(out=sq[:, :, 0:n], in_=R[:, 16:24, c0:c0 + n], func=AF.Square),
                     reads=kR(16, i), writes=["sq"])
                b2 = S.bank()
                mm_group(ps[:, b2, 0:n], [(onesb[:], sq[:, j, 0:n]) for j in range(8)], reads=["sq", "onesb"], bank_keys=[("ps", b2)])
                sl = rot("st", 2)
                S.op("vector", lambda e, sl=sl, bm=bm, n=n: e.tensor_copy(st2[:, sl, 0:n], ps[:, bm, 0:n]), reads=[("ps", bm)], writes=[("st2", sl)])
                S.op("vector", lambda e, sl=sl, n=n: e.tensor_tensor(st1[:, sl, 0:n], st2[:, sl, 0:n], st2[:, sl, 0:n], ALU.mult),
                     reads=[("st2", sl)], writes=[("st1", sl)])
                S.op("vector", lambda e, sl=sl, b2=b2, n=n: e.tensor_tensor(st1[:, sl, 0:n], ps[:, b2, 0:n], st1[:, sl, 0:n], ALU.subtract),
                     reads=[("ps", b2), ("st1", sl)], writes=[("st1", sl)])
                S.op("vector", lambda e, sl=sl, n=n: e.tensor_scalar(st1[:, sl, 0:n], st1[:, sl, 0:n], 0.0, None, ALU.max),
                     reads=[("st1", sl)], writes=[("st1", sl)])
                S.op("scalar", lambda e, sl=sl, n=n: e.activation(out=st1[:, sl, 0:n], in_=st1[:, sl, 0:n], func=AF.Sqrt, bias=epsc[:, 0:1], scale=1.0),
                     reads=[("st1", sl), "epsc"], writes=[("st1", sl)])
                S.op("vector", lambda e, sl=sl, n=n: e.reciprocal(st1[:, sl, 0:n], st1[:, sl, 0:n]), reads=[("st1", sl)], writes=[("st1", sl)])
                for j in range(8):
                    ts = rot("tmp", 4)
                    S.op("vector", lambda e, j=j, ts=ts, sl=sl, c0=c0, n=n: e.tensor_tensor(tmp[:, ts, 0:n], v.s(j, c0, n), st2[:, sl, 0:n], ALU.subtract),
                         reads=[("R", 16 + j, i), ("st2", sl)], writes=[("tmp", ts)])
                    S.op("vector", lambda e, ts=ts, sl=sl, n=n: e.tensor_tensor(tmp[:, ts, 0:n], tmp[:, ts, 0:n], st1[:, sl, 0:n], ALU.mult),
                         reads=[("tmp", ts), ("st1", sl)], writes=[("tmp", ts)])
                    S.op("scalar", lambda e, j=j, ts=ts, c0=c0, n=n: e.activation(out=v.s(j, c0, n), in_=tmp[:, ts, 0:n], func=AF.Silu,
                                                                                  bias=vcol(l, O_LB, j), scale=vcol(l, O_LG, j)),
                         reads=[("tmp", ts), "vec"], writes=[("R", 16 + j, i)])

        def emit_rows(l, src_fn, ncols, nchunks, dst_fn, keys):
            for g0 in range(0, nchunks, 4):
                g1 = min(nchunks, g0 + 4)
                b = S.bank()
                fns = [lambda e, j=j, b=b, g0=g0: e.transpose(ps[0:ncols, b, (j - g0) * 128:(j - g0 + 1) * 128], src_fn(j), idf[:])
                       for j in range(g0, g1)]
                S.op("tensor", fns, reads=keys + ["idf"], writes=[("ps", b)])
                sl = rot("os", 2)
                w = (g1 - g0) * 128
                copy_op(evac_eng(), ostage[0:ncols, sl, 0:w], ps[0:ncols, b, 0:w], reads=[("ps", b)], writes=[("ostage", sl)])
                S.dma("sync", dst_fn(g0 * 128, w), ostage[0:ncols, sl, 0:w], "os%d" % sl, reads=[("ostage", sl)], writes=["OUT"])

        def q_proj(v, l):
            for j in range(8):
                wq, fq = load_w(w_in[l, :, 2 * D + j * 128:2 * D + (j + 1) * 128])
                for i, (c0, n) in enumerate(v.nts):
                    b = proj(wq, fq, v.xn, kxn(i), c0, n)
                    copy_op(evac_eng(), v.q(j, c0, n), ps[:, b, 0:n], reads=[("ps", b)], writes=[("R", j, i)])

        def qkv(v, l, sc_last):
            tc, pc, so = v.tc, v.pc, v.so
            for kh in range(4):
                s = rot("w", NWS)
                src = w_in[l, :, 3 * D + kh * 64:3 * D + (kh + 1) * 64].rearrange("(k p) m -> p k m", p=128)
                S.dma("gpsimd", wsl[:, s, :, 0:64], src, "w%d" % s, writes=[("w", s)])
                S.dma("gpsimd", wsl[:, s, :, 64:128], src, "w%d" % s, writes=[("w", s)])
                fk = lambda k, s=s: wsl[:, s, k, :]
                for i, (c0, n) in enumerate(v.nts):
                    b = proj(("w", s), fk, v.xn, kxn(i), c0, n)
                    copy_op(evac_eng(), kT[:, kh, 128 + c0:128 + c0 + n], ps[:, b, 0:n], reads=[("ps", b)], writes=[("kT", kh, i)])
            S.dma("gpsimd", wv[:], w_in[l, :, 3 * D + 256:3 * D + 512].rearrange("(k p) m -> p k m", p=128), "wv", writes=["wv"])
            S.dma("gpsimd", wk[:], w_in[l, :, 3 * D:3 * D + 256].rearrange("(k p) m -> p k m", p=128), "wk", writes=["wk"])
            ntl = pc // 128
            for t in range(ntl):
                its = nts_of(v, t * 128, t * 128 + 128)
                rk = [("xnT", j, i) for j in range(8) for i in its]
                b = S.bank()
                mm_group(ps[:, b, 0:256], [(xnT[:, k, t * 128:(t + 1) * 128], wv[:, k, :]) for k in range(8)], reads=rk + ["wv"], bank_keys=[("ps", b)])
                S.op("vector", lambda e, b=b, t=t: e.tensor_copy(Vt[:, 1 + t].rearrange("p k (u d) -> p k u d", u=2),
                                                                 ps[:, b, 0:256].rearrange("p (k d) -> p k d", k=4)[:, :, None, :].broadcast_to([128, 4, 2, 64])),
                     reads=[("ps", b)], writes=[("Vt", 1 + t)])
                if sc_last and t == ntl - 1:
                    sl = rot("os", 2)
                    S.op("vector", lambda e, b=b, sl=sl: e.tensor_copy(ostage[:, sl, 0:256], ps[:, b, 0:256]),
                         reads=[("ps", b)], writes=[("ostage", sl)])
                    S.dma("sync", kv_p[l, 1, :, :], ostage[:, sl, 0:256], "os%d" % sl, reads=[("ostage", sl)], writes=["OUT"])
                    b = S.bank()
                    mm_group(ps[:, b, 0:256], [(xnT[:, k, t * 128:(t + 1) * 128], wk[:, k, :]) for k in range(8)], reads=rk + ["wk"], bank_keys=[("ps", b)])
                    sl = rot("os", 2)
                    S.op("scalar", lambda e, b=b, sl=sl: e.activation(out=ostage[:, sl, 0:256], in_=ps[:, b, 0:256], func=AF.Copy),
                         reads=[("ps", b)], writes=[("ostage", sl)])
                    S.dma("sync", kv_p[l, 0, :, :], ostage[:, sl, 0:256], "os%d" % sl, reads=[("ostage", sl)], writes=["OUT"])
            if so is not None:
                b = S.bank()
                mm_group(ps[0:NS, b, 0:256], [(xnT[:, k, so:so + NS], wv[:, k, :]) for k in range(8)], reads=kxn(v.si) + ["wv"], bank_keys=[("ps", b)])
                S.op("vector", lambda e, b=b: e.tensor_copy(Vnew[:].rearrange("p k (u d) -> p k u d", u=2),
                                                          ps[0:NS, b, 0:256].rearrange("p (k d) -> p k d", k=4)[:, :, None, :].broadcast_to([NS, 4, 2, 64])),
                     reads=[("ps", b)], writes=[("Vt", 5)])
                sl = rot("os", 2)
                S.op("vector", lambda e, b=b, sl=sl: e.tensor_copy(ostage[0:NS, sl, 0:256], ps[0:NS, b, 0:256]),
                     reads=[("ps", b)], writes=[("ostage", sl)])
                S.dma("sync", v_s[l, :, 127, :], ostage[0:NS, sl, 0:256], "os%d" % sl, reads=[("ostage", sl)], writes=["OUT"])
                b = S.bank()
                mm_group(ps[0:NS, b, 0:256], [(xnT[:, k, so:so + NS], wk[:, k, :]) for k in range(8)], reads=kxn(v.si) + ["wk"], bank_keys=[("ps", b)])
                sl = rot("os", 2)
                S.op("scalar", lambda e, b=b, sl=sl: e.activation(out=ostage[0:NS, sl, 0:256], in_=ps[0:NS, b, 0:256], func=AF.Copy),
                     reads=[("ps", b)], writes=[("ostage", sl)])
                S.dma("sync", k_s[l, :, 127, :], ostage[0:NS, sl, 0:256], "os%d" % sl, reads=[("ostage", sl)], writes=["OUT"])

        def attention(v, l, t0):
            ntl = v.pc // 128
            nn = len(v.nts)
            tc = v.pc
            S.op("vector", lambda e: e.tensor_copy(kT[:, :, 0:128], kprev[:, l]), reads=[("kprev", l)], writes=[("kT", kh, "prev") for kh in range(4)])
            S.op("vector", lambda e: e.tensor_copy(Vt[:, 0], vprev[:, l]), reads=[("vprev", l)], writes=[("Vt", 0)])
            units = [(t, kh) for t in range(ntl) for kh in range(4)]
            st_ = {}

            def stage_a(u):
                t, kh = u
                gi = t0 + t
                mi = gi if gi < 4 else 4
                its = nts_of(v, t * 128, t * 128 + 128)
                b = S.bank(2)
                sc = ps[:, b:b + 2, :].rearrange("p a (g c) -> p (a g) c", g=2)
                fns = []
                for g in range(4):
                    h = 4 * kh + g
                    jq, pb = h // 2, (h % 2) * 64
                    i4 = (g % 2) * 2 + g // 2
                    fns.append(lambda e, i4=i4, g=g, jq=jq, pb=pb: e.matmul(sc[:, i4, :], R[pb:pb + 64, jq, t * 128:(t + 1) * 128],
                                                                     kT[pb:pb + 64, kh, t * 128:t * 128 + 256],
                                                                     start=(g < 2), stop=False, skip_group_check=True))
                for a_ in range(2):
                    fns.append(lambda e, a_=a_: e.matmul(ps[:, b + a_, :], idb[:], maskb2[:, mi, :], start=False, stop=True, skip_group_check=True))
                rk = [("R", jq, i) for jq in (2 * kh, 2 * kh + 1) for i in its] + [("kT", kh, i) for i in range(nn)] + [("kT", kh, "prev"), "idb", "maskb"]
                S.op("tensor", fns, reads=rk, writes=[("ps", b), ("ps", b + 1)])
                S.reserved.update((b, b + 1))
                st_[u] = (b, sc, its)

            def stage_b1(u):
                t, kh = u
                b, sc, its = st_[u]
                s0 = rot("sm", 8)
                mx, ng, ssum, dd = sm[:, s0, 0, :], sm[:, s0, 1, :], sm[:, s0, 2, :], sm[:, s0, 3, :]
                S.op("vector", lambda e: e.tensor_reduce(mx, sc, AX.X, ALU.max), reads=[("ps", b), ("ps", b + 1)], writes=[("sm", s0, 0)])
                S.op("vector", lambda e: e.scalar_tensor_tensor(out=ng, in0=mx, scalar=-SCALE, in1=nsnk[:, l, 4 * kh:4 * kh + 4],
                                                                op0=ALU.mult, op1=ALU.min),
                     reads=[("sm", s0, 0), "nsnk"], writes=[("sm", s0, 1)])
                S.op("vector", lambda e: e.memset(ssum, 0.0), writes=[("sm", s0, 2)])
                st_[u] = (b, sc, its, s0)

            def stage_b1b(u):
                t, kh = u
                b, sc, its, s0 = st_[u][:4]
                ng, dd = sm[:, s0, 1, :], sm[:, s0, 3, :]
                S.op("scalar", lambda e: e.activation(out=dd, in_=ng, func=AF.Exp), reads=[("sm", s0, 1)], writes=[("sm", s0, 3)])

            def stage_b2(u):
                t, kh = u
                b, sc, its, s0 = st_[u]
                ng, ssum = sm[:, s0, 1, :], sm[:, s0, 2, :]
                eb = rot("es", 2)
                for g in range(4):
                    S.op("scalar", lambda e, g=g: e.activation(
                        out=esb[:, eb, g, :], in_=sc[:, g, :], func=AF.Exp, bias=ng[:, g:g + 1], scale=SCALE, accum_out=ssum[:, g:g + 1]),
                        reads=[("ps", b), ("ps", b + 1), ("sm", s0, 1), ("sm", s0, 2)], writes=[("esb", eb, g), ("sm", s0, 2)])
                S.reserved.difference_update((b, b + 1))
                st_[u] = (b, sc, its, s0, eb)

            def stage_b3(u):
                t, kh = u
                b, sc, its, s0, eb = st_[u]
                ssum, dd = sm[:, s0, 2, :], sm[:, s0, 3, :]
                S.op("vector", lambda e: e.tensor_tensor(dd, dd, esnk[:, l, 4 * kh:4 * kh + 4], ALU.mult), reads=[("sm", s0, 3), "esnk"], writes=[("sm", s0, 3)])
                S.op("vector", lambda e: e.tensor_tensor(dd, dd, ssum, ALU.add), reads=[("sm", s0, 3), ("sm", s0, 2)], writes=[("sm", s0, 3)])
                S.op("vector", lambda e: e.reciprocal(dd, dd), reads=[("sm", s0, 3)], writes=[("sm", s0, 3)])
                S.op("gpsimd", lambda e: e.tensor_tensor(pbf[:, eb], esb[:, eb], dd[:, :, None].broadcast_to([128, 4, 256]), ALU.mult),
                     reads=[("esb", eb, g) for g in range(4)] + [("sm", s0, 3)], writes=[("pbf", eb)])
                st_[u] = (b, sc, its, eb)

            def stage_c(u):
                t, kh = u
                b, sc, its, eb = st_[u]
                bt = S.bank()
                ptv = ps[:, bt, :].bitcast(BF16).rearrange("p (g k q) -> p g k q", g=4, k=2)
                fns = []
                for g in range(4):
                    for blk in range(2):
                        fns.append(lambda e, g=g, blk=blk: e.transpose(ptv[:, g, blk, :], pbf[:, eb, g, blk * 128:(blk + 1) * 128], idb[:]))
                S.op("tensor", fns, reads=[("pbf", eb), "idb"], writes=[("ps", bt)])
                pb_ = rot("kv", 2)
                copy_op("vector", pT[:, pb_], ptv, reads=[("ps", bt)], writes=[("pT", pb_)])
                st_[u] = (its, pb_)

            def stage_c2(u):
                t, kh = u
                its, pb_ = st_.pop(u)
                bo = S.bank()
                fns = []
                for g in range(4):
                    for blk in range(2):
                        fns.append(lambda e, g=g, blk=blk: e.matmul(ps[:, bo, g * 128:(g + 1) * 128], Vt[:, t + blk, kh, :], pT[:, pb_, g, blk, :],
                                                                    start=(blk == 0), stop=(blk == 1)))
                S.op("tensor", fns, reads=[("pT", pb_), ("Vt", t), ("Vt", t + 1)], writes=[("ps", bo)])
                eng_ = evac_eng()
                for hh in range(2):
                    pb = hh * 64
                    copy_op(eng_, aT[pb:pb + 64, 2 * kh:2 * kh + 2, t * 128:(t + 1) * 128],
                            ps[pb:pb + 64, bo, :].rearrange("p (u j q) -> p u j q", u=2, j=2)[:, hh, :, :],
                            reads=[("ps", bo)], writes=[("aT", 2 * kh, i) for i in its] + [("aT", 2 * kh + 1, i) for i in its])

            nu = len(units)
            def un(i):
                return units[i] if 0 <= i < nu else None
            for k in range(-5, nu):
                for off, fn in ((5, stage_a), (4, stage_b1), (2, stage_b3), (3, stage_b2), (4, stage_b1b), (1, stage_c), (0, stage_c2)):
                    u = un(k + off)
                    if u is not None:
                        fn(u)
            S.op("vector", lambda e: e.tensor_copy(kprev[:, l], kT[:, :, tc:tc + 128]),
                 reads=[("kT", kh, i) for kh in range(4) for i in range(nn)], writes=[("kprev", l)])
            S.op("vector", lambda e: e.tensor_copy(vprev[:, l], Vt[:, ntl]), reads=[("Vt", ntl)], writes=[("vprev", l)])

        def attention_sample(v, l):
            bo = S.bank()
            S.reserved.add(bo)
            so, si = v.so, v.si
            S.op("vector", lambda e: e.memset(R[:, 0:8, so + NS:so + 32], 0.0), writes=[("Rpad", 0)])
            for bq in range(NS):
              def unit(bq=bq):
                st = rot("kv", 2)
                S.dma("sync", kvst[:, st, 0, :], ck[l, bq, :, :], "kvk%d" % st, writes=[("kvst", st, 0)])
                S.dma("sync", kvst[:, st, 1, :], cv[l, bq, :, :], "kvv%d" % st, writes=[("kvst", st, 1)])
                S.dma("sync", k_s[l, bq, 0:127, :], kvst[1:128, st, 0, :], "ksok%d" % st, reads=[("kvst", st, 0)], writes=["OUT"])
                S.dma("sync", v_s[l, bq, 0:127, :], kvst[1:128, st, 1, :], "ksov%d" % st, reads=[("kvst", st, 1)], writes=["OUT"])
                S.op("vector", lambda e, st=st: e.tensor_copy(kdup[:, 0].rearrange("p k (u d) -> p k u d", u=2),
                                                            kvst[:, st, 0, :].rearrange("p (k d) -> p k d", k=4)[:, :, None, :].broadcast_to([128, 4, 2, 64])),
                     reads=[("kvst", st, 0)], writes=[("kdup", 0)])
                S.op("vector", lambda e, st=st: e.tensor_copy(Vb[:, st].rearrange("p k (u d) -> p k u d", u=2),
                                                            kvst[:, st, 1, :].rearrange("p (k d) -> p k d", k=4)[:, :, None, :].broadcast_to([128, 4, 2, 64])),
                     reads=[("kvst", st, 1)], writes=[("pbf", st)])
                bt = S.bank()
                S.op("tensor", [lambda e, kh=kh: e.transpose(ps[:, bt, kh * 128:(kh + 1) * 128], kdup[:, 0, kh, :], idf[:]) for kh in range(4)],
                     reads=[("kdup", 0), "idf"], writes=[("ps", bt)])
                copy_op(evac_eng(), KTb[:, st], ps[:, bt, :].rearrange("p (k s) -> p k s", k=4), reads=[("ps", bt)], writes=[("pbf", st)])
                b = S.bank(2)
                scv = ps[:, b:b + 2, :].rearrange("p a (k c) -> p (a k) c", k=2)
                fns = []
                for pbsel in range(2):
                    for kh in range(4):
                        for g in (pbsel, pbsel + 2):
                            h = 4 * kh + g
                            jq, pb = h // 2, (h % 2) * 64
                            fns.append(lambda e, kh=kh, g=g, jq=jq, pb=pb: e.matmul(
                                scv[32 * g:32 * g + 32, kh, 0:128], R[pb:pb + 64, jq, so:so + 32], KTb[pb:pb + 64, st, kh, :],
                                start=True, stop=True, tile_position=(pb, 32 * g)))
                            fns.append(lambda e, kh=kh, g=g, jq=jq, pb=pb: e.matmul(
                                scv[32 * g:32 * g + 32, kh, 128:144], R[pb:pb + 64, jq, so:so + 32], kT[pb:pb + 64, kh, 128 + so:128 + so + NS],
                                start=True, stop=True, tile_position=(pb, 32 * g)))
                S.op("tensor", fns, reads=[("R", j, si) for j in range(8)] + [("Rpad", 0), ("pbf", st)] + [("kT", kh, si) for kh in range(4)],
                     writes=[("ps", b), ("ps", b + 1)])
                eb = 0
                S.op("vector", lambda e, eb=eb: e.tensor_tensor(ess[:, eb, :, 128:144], scv[:, :, 128:144], masksb[:], ALU.add),
                     reads=[("ps", b), ("ps", b + 1), "masksb"], writes=[("essn", eb)])
                s0 = rot("sm", 8)
                mx, ng, ssum, dd = sm[:, s0, 0, :], sm[:, s0, 1, :], sm[:, s0, 2, :], sm[:, s0, 3, :]
                m2, s2 = sm[:, s0, 4, :], sm[:, s0, 5, :]
                S.op("vector", lambda e, mx=mx: e.tensor_reduce(mx, scv[:, :, 0:128], AX.X, ALU.max), reads=[("ps", b), ("ps", b + 1)], writes=[("sm", s0, 0)])
                S.op("vector", lambda e, m2=m2, eb=eb: e.tensor_reduce(m2, ess[:, eb, :, 128:144], AX.X, ALU.max), reads=[("essn", eb)], writes=[("sm", s0, 4)])
                S.op("vector", lambda e, mx=mx, m2=m2: e.tensor_tensor(mx, mx, m2, ALU.max), reads=[("sm", s0, 0), ("sm", s0, 4)], writes=[("sm", s0, 0)])
                S.op("vector", lambda e, mx=mx, ng=ng: e.scalar_tensor_tensor(out=ng, in0=mx, scalar=-SCALE, in1=nsnkc[:, l, :], op0=ALU.mult, op1=ALU.min),
                     reads=[("sm", s0, 0), "nsnkc"], writes=[("sm", s0, 1)])
                S.op("vector", lambda e, ssum=ssum, s2=s2: e.memset(sm[:, s0, 2:4, :], 0.0), writes=[("sm", s0, 2), ("sm", s0, 3)])
                S.op("vector", lambda e, s2=s2: e.memset(s2, 0.0), writes=[("sm", s0, 5)])
                for kh in range(4):
                    S.op("scalar", lambda e, kh=kh, eb=eb, ng=ng, ssum=ssum: e.activation(
                        out=ess[:, eb, kh, 0:128], in_=scv[:, kh, 0:128], func=AF.Exp, bias=ng[:, kh:kh + 1], scale=SCALE, accum_out=ssum[:, kh:kh + 1]),
                        reads=[("ps", b), ("ps", b + 1), ("sm", s0, 1), ("sm", s0, 2)], writes=[("essc", eb, kh), ("sm", s0, 2)])
                    S.op("scalar", lambda e, kh=kh, eb=eb, ng=ng, s2=s2: e.activation(
                        out=ess[:, eb, kh, 128:144], in_=ess[:, eb, kh, 128:144], func=AF.Exp, bias=ng[:, kh:kh + 1], scale=SCALE, accum_out=s2[:, kh:kh + 1]),
                        reads=[("essn", eb), ("sm", s0, 1), ("sm", s0, 5)], writes=[("essn", eb), ("sm", s0, 5)])
                S.op("vector", lambda e, dd=dd, ng=ng: e.tensor_tensor(dd, snkc[:, l, :], ng, ALU.add), reads=["snkc", ("sm", s0, 1)], writes=[("sm", s0, 3)])
                S.op("scalar", lambda e, dd=dd: e.activation(out=dd, in_=dd, func=AF.Exp), reads=[("sm", s0, 3)], writes=[("sm", s0, 3)])
                S.op("vector", lambda e, dd=dd, ssum=ssum: e.tensor_tensor(dd, dd, ssum, ALU.add), reads=[("sm", s0, 3), ("sm", s0, 2)], writes=[("sm", s0, 3)])
                S.op("vector", lambda e, dd=dd, s2=s2: e.tensor_tensor(dd, dd, s2, ALU.add), reads=[("sm", s0, 3), ("sm", s0, 5)], writes=[("sm", s0, 3)])
                S.op("vector", lambda e, dd=dd: e.reciprocal(dd, dd), reads=[("sm", s0, 3)], writes=[("sm", s0, 3)])
                S.op("vector", lambda e, dd=dd, bq=bq: e.tensor_scalar(dd, dd, rowselt[:, bq:bq + 1], None, ALU.mult), reads=[("sm", s0, 3), "rowsel"], writes=[("sm", s0, 3)])
                S.op("vector", lambda e, dd=dd, eb=eb: e.tensor_tensor(ess[:, eb], ess[:, eb], dd[:, :, None].broadcast_to([128, 4, 144]), ALU.mult),
                     reads=[("essc", eb, kh) for kh in range(4)] + [("essn", eb), ("sm", s0, 3)], writes=[("essc", eb, kh) for kh in range(4)] + [("essn", eb)])
                btp = S.bank()
                S.op("tensor", [lambda e, kh=kh: e.transpose(ps[:, btp, kh * 128:(kh + 1) * 128], ess[:, eb, kh, 0:128], idf[:]) for kh in range(4)],
                     reads=[("essc", eb, kh) for kh in range(4)] + ["idf"], writes=[("ps", btp)])
                copy_op(evac_eng(), pTs[:, st], ps[:, btp, :].rearrange("p (k s) -> p k s", k=4), reads=[("ps", btp)], writes=[("pT", st)])
                bt2 = S.bank()
                S.op("tensor", [lambda e, kh=kh: e.transpose(ps[0:NS, bt2, kh * 128:(kh + 1) * 128], ess[:, eb, kh, 128:144], idf[:]) for kh in range(4)],
                     reads=[("essn", eb), "idf"], writes=[("ps", bt2)])
                copy_op(evac_eng(), pTn[:, st], ps[0:NS, bt2, :].rearrange("p (k s) -> p k s", k=4), reads=[("ps", bt2)], writes=[("pT", st)])
                fns = []
                for kh in range(4):
                    fns.append(lambda e, kh=kh: e.matmul(ps[:, bo, kh * 128:(kh + 1) * 128], Vb[:, st, kh, :], pTs[:, st, kh, :],
                                                         start=(bq == 0 and kh == 0), stop=False, skip_group_check=True))
                    fns.append(lambda e, kh=kh: e.matmul(ps[:, bo, kh * 128:(kh + 1) * 128], Vnew[:, kh, :], pTn[:, st, kh, :],
                                                         start=False, stop=(bq == NS - 1 and kh == 3), skip_group_check=True))
                S.op("tensor", fns, reads=[("pbf", st), ("pT", st), ("Vt", 5)], writes=[("ps", bo)])
              unit()
            S.reserved.discard(bo)
            ov = ps[:, bo, :].rearrange("p (k g s) -> p k g s", k=4, g=4)
            for kh in range(4):
                for g in range(4):
                    h = 4 * kh + g
                    jq, pb = h // 2, (h % 2) * 64
                    copy_op("vector", aT[pb:pb + 64, jq, so:so + NS], ov[pb:pb + 64, kh, g, 0:NS], reads=[("ps", bo)], writes=[("aT", jq, si)])

        def gates_out(v, l):
            for j in range(8):
                w1, f1 = load_w(w_pw[l, :, j * 128:(j + 1) * 128])
                w2, f2 = load_w(w_in[l, :, 3 * D + 512 + j * 128:3 * D + 512 + (j + 1) * 128])
                w3, f3 = load_w(w_ao[l, :, j * 128:(j + 1) * 128])
                w4, f4 = load_w(w_in[l, :, 4 * D + 512 + j * 128:4 * D + 512 + (j + 1) * 128])
                for i, (c0, n) in enumerate(v.nts):
                    b1 = proj(w1, f1, v.s, kR(16, i), c0, n)
                    b2 = proj(w2, f2, v.xn, kxn(i), c0, n)
                    t1 = rot("tmp", 4)
                    S.op("scalar", lambda e, b2=b2, t1=t1, n=n: e.activation(out=tmp[:, t1, 0:n], in_=ps[:, b2, 0:n], func=AF.Sigmoid),
                         reads=[("ps", b2)], writes=[("tmp", t1)])
                    S.op("vector", lambda e, b1=b1, t1=t1, n=n: e.tensor_tensor(tmp[:, t1, 0:n], ps[:, b1, 0:n], tmp[:, t1, 0:n], ALU.mult),
                         reads=[("ps", b1), ("tmp", t1)], writes=[("tmp", t1)])
                    b3 = proj(w3, f3, v.a, ka(i), c0, n)
                    b4 = proj(w4, f4, v.xn, kxn(i), c0, n)
                    t2 = rot("tmp", 4)
                    S.op("scalar", lambda e, b4=b4, t2=t2, n=n: e.activation(out=tmp[:, t2, 0:n], in_=ps[:, b4, 0:n], func=AF.Sigmoid),
                         reads=[("ps", b4)], writes=[("tmp", t2)])
                    S.op("vector", lambda e, b3=b3, t2=t2, n=n: e.tensor_tensor(tmp[:, t2, 0:n], ps[:, b3, 0:n], tmp[:, t2, 0:n], ALU.mult),
                         reads=[("ps", b3), ("tmp", t2)], writes=[("tmp", t2)])
                    S.op("vector", lambda e, t1=t1, t2=t2, j=j, c0=c0, n=n: e.tensor_tensor(v.m(j, c0, n), tmp[:, t1, 0:n], tmp[:, t2, 0:n], ALU.add),
                         reads=[("tmp", t1), ("tmp", t2)], writes=[("R", 8 + j, i)])
            for j in range(8):
                wo, fo = load_w(w_out[l, :, j * 128:(j + 1) * 128])
                for i, (c0, n) in enumerate(v.nts):
                    b = proj(wo, fo, v.m, kR(8, i), c0, n)
                    S.op("vector", lambda e, b=b, j=j, c0=c0, n=n: e.tensor_tensor(xT[:, j, c0:c0 + n], xT[:, j, c0:c0 + n], ps[:, b, 0:n], ALU.add),
                         reads=[("ps", b), ("xT", j, i)], writes=[("xT", j, i)])

        def ffn(v, l, sc_first, sc_last):
            tc, pc, so = v.tc, v.pc, v.so
            nn = len(v.nts)
            if so is not None:
                sl = rot("xs", 2)
                for k in range(2):
                    S.dma("sync", xstage[k * NS:(k + 1) * NS, sl, 0:1024], sffn[l, :, k, 0:1024], "xs%d" % sl, writes=[("xstage", sl)])
                sl2 = rot("xs", 2)
                for k in range(2):
                    S.dma("sync", xstage[k * NS:(k + 1) * NS, sl2, 0:1024], sffn[l, :, k, 1024:2048], "xs%d" % sl2, writes=[("xstage", sl2)])
                S.dma("sync", ffn_s[l, :, 0, :], sffn[l, :, 1, :], "ffs", writes=["OUT"])
                for part, slp in ((0, sl), (1, sl2)):
                    for g0 in range(0, 8, 4):
                        b = S.bank()
                        S.op("tensor", [lambda e, jj=jj, b=b, slp=slp: e.transpose(ps[:, b, (jj % 4) * 32:(jj % 4 + 1) * 32], xstage[0:32, slp, jj * 128:(jj + 1) * 128], idf[0:32, 0:32])
                                        for jj in range(g0, g0 + 4)], reads=[("xstage", slp), "idf"], writes=[("ps", b)])
                        eng_ = evac_eng()
                        for jj in range(g0, g0 + 4):
                            copy_op(eng_, hTs[:, part * 8 + jj, 0:2, :], ps[:, b, (jj % 4) * 32:(jj % 4 + 1) * 32].rearrange("p (k b) -> p k b", k=2),
                                    reads=[("ps", b)], writes=[("hTs", part * 8 + jj)])
                sl3 = rot("xs", 2)
                for k in range(2):
                    S.dma("sync", xstage[k * NS:(k + 1) * NS, sl3, 0:768], sffn[l, :, k, 2048:2816], "xs%d" % sl3, writes=[("xstage", sl3)])
                for g0 in range(0, 6, 3):
                    b = S.bank()
                    S.op("tensor", [lambda e, jj=jj, b=b: e.transpose(ps[:, b, (jj % 3) * 32:(jj % 3 + 1) * 32], xstage[0:32, sl3, jj * 128:(jj + 1) * 128], idf[0:32, 0:32])
                                    for jj in range(g0, g0 + 3)], reads=[("xstage", sl3), "idf"], writes=[("ps", b)])
                    eng_ = evac_eng()
                    for jj in range(g0, g0 + 3):
                        copy_op(eng_, hTs[:, 16 + jj, 0:2, :], ps[:, b, (jj % 3) * 32:(jj % 3 + 1) * 32].rearrange("p (k b) -> p k b", k=2),
                                reads=[("ps", b)], writes=[("hTs", 16 + jj)])
            rmsnorm_to_xn(v, l, O_NF)
            for jf in range(NFC):
                wh, fh = load_w(w_up[l, :, jf * 128:(jf + 1) * 128])
                wg, fg = load_w(w_up[l, :, DFF + jf * 128:DFF + (jf + 1) * 128])
                db_ = jf % 2
                S.op("vector", lambda e, jf=jf, db_=db_: e.tensor_tensor(
                    dgf[:, db_], idb[:, None, :].broadcast_to([128, 3, 128]),
                    vec[:, l * LV + O_FDW + jf * 3:l * LV + O_FDW + (jf + 1) * 3, None].broadcast_to([128, 3, 128]), ALU.mult),
                    reads=["idb", "vec"], writes=[("dgf", db_)])
                hb = jf % 2
                S.op("vector", lambda e, jf=jf, hb=hb: e.tensor_copy(hT[:, hb, 0:2], hhist[:, l, jf, :]),
                     reads=[("hhist", l, jf)], writes=[("hT", hb, "h")])
                for i, (c0, n) in enumerate(v.nts):
                    bh = proj(wh, fh, v.xn, kxn(i), c0, n)
                    S.op("scalar", lambda e, bh=bh, hb=hb, c0=c0, n=n: e.activation(out=hT[:, hb, 2 + c0:2 + c0 + n], in_=ps[:, bh, 0:n], func=AF.Copy),
                         reads=[("ps", bh)], writes=[("hT", hb, i)])
                    if sc_last and c0 <= pc - 2 and c0 + n >= pc:
                        lo = pc - 2 - c0
                        S.op("scalar", lambda e, bh=bh, jf=jf, lo=lo: e.activation(out=hf32p[:, jf, :], in_=ps[:, bh, lo:lo + 2], func=AF.Copy), reads=[("ps", bh)], writes=[("hf32p", jf)])
                    if so is not None and i == v.si:
                        lo = so - c0
                        S.op("scalar", lambda e, bh=bh, jf=jf, lo=lo: e.activation(out=hf32[:, jf, 0:NS], in_=ps[:, bh, lo:lo + NS], func=AF.Copy),
                             reads=[("ps", bh)], writes=[("hf32", jf)])
                        S.op("vector", lambda e, jf=jf: e.tensor_copy(hTs[:, jf, 2, :], hf32[:, jf, 0:NS]), reads=[("hf32", jf)], writes=[("hTs", jf)])
                if sc_first:
                    S.op("vector", lambda e, hb=hb: e.tensor_tensor(hT[:, hb, 2:2 + 384], hT[:, hb, 2:2 + 384], validb[:], ALU.mult),
                         reads=[("hT", hb, i) for i in range(nn)] + ["validb"], writes=[("hT", hb, i) for i in range(nn)])
                S.op("vector", lambda e, jf=jf, hb=hb: e.tensor_copy(hhist[:, l, jf, :], hT[:, hb, pc:pc + 2]),
                     reads=[("hT", hb, i) for i in range(nn)] + [("hT", hb, "h")], writes=[("hhist", l, jf)])
                for i, (c0, n) in enumerate(v.nts):
                    bg = proj(wg, fg, v.xn, kxn(i), c0, n)
                    bc = S.bank()
                    pairs = [(dgf[:, db_, k, :], hT[:, hb, c0 + k:c0 + k + n]) for k in range(3)]
                    rd = [("dgf", db_), ("hT", hb, "h")] + [("hT", hb, ii) for ii in range(max(0, i - 1), i + 1)]
                    mm_group(ps[:, bc, 0:n], pairs, reads=rd, bank_keys=[("ps", bc)])
                    ts = rot("tmp", 4)
                    S.op("scalar", lambda e, bc=bc, ts=ts, jf=jf, n=n: e.activation(out=tmp[:, ts, 0:n], in_=ps[:, bc, 0:n], func=AF.Gelu,
                                                                                  bias=vcol(l, O_FDB, jf), scale=1.0),
                         reads=[("ps", bc), "vec"], writes=[("tmp", ts)])
                    S.op("vector", lambda e, bg=bg, ts=ts, jf=jf, c0=c0, n=n: e.tensor_tensor(v.hg(jf, c0, n), tmp[:, ts, 0:n], ps[:, bg, 0:n], ALU.mult),
                         reads=[("ps", bg), ("tmp", ts)], writes=[("R", jf, i)])
                    if so is not None and i == v.si:
                        lo = so - c0
                        bcs = S.bank()
                        mm_group(ps[:, bcs, 0:NS], [(dgf[:, db_, k, :], hTs[:, jf, k, :]) for k in range(3)], reads=[("dgf", db_), ("hTs", jf)], bank_keys=[("ps", bcs)])
                        ts = rot("tmp", 4)
                        S.op("scalar", lambda e, bcs=bcs, ts=ts, jf=jf: e.activation(out=tmp[:, ts, 0:NS], in_=ps[:, bcs, 0:NS], func=AF.Gelu,
                                                                                   bias=vcol(l, O_FDB, jf), scale=1.0),
                             reads=[("ps", bcs), "vec"], writes=[("tmp", ts)])
                        S.op("vector", lambda e, bg=bg, ts=ts, jf=jf, lo=lo: e.tensor_tensor(v.hg(jf, so, NS), tmp[:, ts, 0:NS], ps[:, bg, lo:lo + NS], ALU.mult),
                             reads=[("ps", bg), ("tmp", ts)], writes=[("R", jf, i)])
            for j in range(8):
                wd_, fd = load_w(w_dn[l, :, j * 128:(j + 1) * 128], k_chunks=NFC)
                for i, (c0, n) in enumerate(v.nts):
                    b = proj(wd_, fd, v.hg, kR(0, i, NFC), c0, n, kc=NFC)
                    S.op("vector", lambda e, b=b, j=j, c0=c0, n=n: e.tensor_tensor(xT[:, j, c0:c0 + n], xT[:, j, c0:c0 + n], ps[:, b, 0:n], ALU.add),
                         reads=[("ps", b), ("xT", j, i)], writes=[("xT", j, i)])

        def final_out(v, t0, ntl):
            cols = [(t * 128, 128, (t0 + t - 3) * 128) for t in range(ntl) if t0 + t >= 3]
            if v.so is not None:
                cols.append((v.so, NS, NOWN * 128))
            for (c0, n, row0) in cols:
                its = nts_of(v, c0, c0 + n)
                kk = [("xT", j, i) for j in range(8) for i in its]
                sl = rms_stats(lambda c0_, n_: xT[:, :, c0_:c0_ + n_], kk, c0, n)
                for j in range(8):
                    S.op("vector", lambda e, j=j, c0=c0, n=n, sl=sl: e.scalar_tensor_tensor(
                        out=ytmp[:, j, 0:n], in0=xT[:, j, c0:c0 + n], scalar=vec[:, 2 * LV + j:2 * LV + j + 1],
                        in1=st1[:, sl, 0:n], op0=ALU.mult, op1=ALU.mult),
                        reads=kk + [("st1", sl), "vec"], writes=[("ytmp", j), ("esb", 0, j % 4)])
                xsl = rot("xs", 2)
                for half in range(2):
                    b = S.bank()
                    S.op("tensor", [lambda e, j=j, b=b, n=n: e.transpose(ps[0:n, b, (j % 4) * 128:(j % 4 + 1) * 128], ytmp[:, j, 0:n], idf[:])
                                    for j in range(half * 4, half * 4 + 4)], reads=[("ytmp", j) for j in range(8)] + [("esb", 0, g) for g in range(4)] + ["idf"], writes=[("ps", b)])
                    copy_op(evac_eng(), xstage[0:n, xsl, half * 512:(half + 1) * 512], ps[0:n, b, :], reads=[("ps", b)], writes=[("xstage", xsl)])
                S.dma("sync", y[row0:row0 + n, :], xstage[0:n, xsl, :], "xs%d" % xsl, reads=[("xstage", xsl)], writes=["OUT"])

        for l in range(2):
            S.dma("sync", conv_s[l, :, 0:29, :], sconv[l, :, 1:30, :], "cvs", writes=["OUT"])

        sc_list = [(t0, ntl, i == len(SCS) - 1) for i, (t0, ntl) in enumerate(SCS)]
        for si, (t0, ntl, has_s) in enumerate(sc_list):
            S.new_phase()
            v = make_views(ntl * 128, has_s)
            sc_first = (si == 0)
            sc_last = (si == len(SCS) - 1)
            load_x(v, t0, ntl)
            for l in range(2):
                rmsnorm_to_xn(v, l, O_NM)
                conv_branch(v, l, sc_first, sc_last, part=0)
                q_proj(v, l)
                conv_branch(v, l, sc_first, sc_last, part=1)
                if sc_last:
                    emit_rows(l, lambda j: uf32[:, j, :], 30, 8, lambda c, w: conv_p[l, :, c:c + w], [("uf32", j) for j in range(8)])
                    emit_rows(l, lambda j: uf32s[:, j, :], NS, 8, lambda c, w: conv_s[l, :, 29, c:c + w], [("uf32s", j) for j in range(8)])
                qkv(v, l, sc_last)
                attention(v, l, t0)
                if has_s:
                    attention_sample(v, l)
                gates_out(v, l)
                ffn(v, l, sc_first, sc_last)
                if sc_last:
                    emit_rows(l, lambda j: hf32p[:, j, :], 2, NFC, lambda c, w: ffn_p[l, :, c:c + w], [("hf32p", j) for j in range(NFC)])
                    emit_rows(l, lambda j: hf32[:, j, 0:NS], NS, NFC, lambda c, w: ffn_s[l, :, 1, c:c + w], [("hf32", j) for j in range(NFC)])
            final_out(v, t0, ntl)

        waits = S._deps("sync", ["OUT"], [])
        allw = []
        for c in S.chan.values():
            allw.append((c[0], c[1]))
        S.ops["sync"].append((allw, [], None, ""))
        S.emit()
    return nc


_NC_CACHE = {}


def _host_layout(inputs):
    f = lambda a: np.ascontiguousarray(np.asarray(a, dtype=np.float32))
    xp = f(inputs["x_prompt"])[0]
    xsamp = f(inputs["x_sample"])[:, 0, :]
    meta = f(inputs["meta_tokens"])
    seq = np.concatenate([np.zeros((256 + 112, D), np.float32), meta, xp], axis=0)
    vecs = np.zeros((128, NV), np.float32)

    def colmaj(a):
        return a.reshape(-1, 128).T

    for l in range(2):
        o = l * LV
        vecs[:, o + O_NM:o + O_NM + 8] = colmaj(f(inputs["norm_mix"])[l])
        vecs[:, o + O_CDB:o + O_CDB + 8] = colmaj(f(inputs["conv_db"])[l])
        vecs[:, o + O_LG:o + O_LG + 8] = colmaj(f(inputs["conv_ln_g"])[l])
        vecs[:, o + O_LB:o + O_LB + 8] = colmaj(f(inputs["conv_ln_b"])[l])
        vecs[:, o + O_NF:o + O_NF + 8] = colmaj(f(inputs["norm_ffn"])[l])
        vecs[:, o + O_FDB:o + O_FDB + 22] = colmaj(f(inputs["ffn_db"])[l])
        cdw = f(inputs["conv_dw"])[l]
        vecs[:, o + O_CDW:o + O_CDW + 248] = cdw.T.reshape(8, 128, 31).transpose(1, 0, 2).reshape(128, 248)
        fdw = f(inputs["ffn_dw"])[l]
        vecs[:, o + O_FDW:o + O_FDW + 66] = fdw.T.reshape(22, 128, 3).transpose(1, 0, 2).reshape(128, 66)
    vecs[:, 2 * LV:2 * LV + 8] = colmaj(f(inputs["norm_final"]))
    sinks = f(inputs["attn_sinks"])
    perm = np.array([4 * k + g for k in range(4) for g in (0, 2, 1, 3)])
    sinkrow = np.ascontiguousarray(np.broadcast_to(sinks[None][:, :, perm], (128, 2, 16)))
    sinkcol = np.zeros((128, 2, 4), np.float32)
    for r in range(128):
        for kh in range(4):
            sinkcol[r, :, kh] = sinks[:, 4 * kh + r // 32]
    masks_s = np.full((128, 16), NEGM, np.float32)
    rowsel = np.zeros((128, 16), np.float32)
    for r in range(128):
        s = r % 32
        if s < 16:
            masks_s[r, s] = 0.0
            rowsel[r, s] = 1.0
    qi = np.arange(128)[:, None]
    kj = np.arange(256)[None, :]
    in_maps = []
    for c in range(NCORES):
        b0 = 16 * c - 2
        xin = seq[(b0 + 2) * 128:(b0 + 2 + NT) * 128]
        m = np.zeros((128, 5, 256), np.float32)
        for mi in range(5):
            gi = mi if mi < 4 else 8
            qpos = (b0 + gi) * 128 + qi
            kpos = (b0 + gi - 1) * 128 + kj
            rel = qpos - kpos
            ok = (rel >= 0) & (rel <= 128) & (kpos >= 112)
            if mi == 0:
                ok = ok & (kj >= 128)
            m[:, mi, :] = np.where(ok, 0.0, NEGM)
        pos = (b0 * 128 + np.arange(384))
        valid = np.ascontiguousarray(np.broadcast_to((pos >= 112).astype(np.float32)[None], (128, 384)))
        sl = slice(NS * c, NS * (c + 1))
        in_maps.append({
            "xin": np.ascontiguousarray(xin), "xs": np.ascontiguousarray(xsamp[sl]),
            "masks": m, "masks_s": masks_s, "rowsel": rowsel, "valid": valid, "vecs": vecs,
            "sinkrow": sinkrow, "sinkcol": sinkcol,
            "w_in": f(inputs["w_in"]), "w_pw": f(inputs["w_conv_pw"]), "w_ao": f(inputs["w_attn_o"]),
            "w_out": f(inputs["w_out"]), "w_up": f(inputs["w_ffn_up"]), "w_dn": f(inputs["w_ffn_down"]),
            "ck": np.ascontiguousarray(f(inputs["cache_swa_k"])[:, sl].reshape(2, NS, 128, 256)),
            "cv": np.ascontiguousarray(f(inputs["cache_swa_v"])[:, sl].reshape(2, NS, 128, 256)),
            "sconv": np.ascontiguousarray(f(inputs["state_conv"])[:, sl]),
            "sffn": np.ascontiguousarray(f(inputs["state_ffn_conv"])[:, sl]),
        })
    return in_maps


def kernel(**inputs):
    in_maps = _host_layout(inputs)
    if "nc" not in _NC_CACHE:
        _NC_CACHE["nc"] = build()
    nc = _NC_CACHE["nc"]
    res = run_bass_kernel_spmd(nc, in_maps, core_ids=list(range(NCORES)))
    r = res.results
    y_prompt = np.concatenate([r[c]["y"][:NOWN * 128] for c in range(NCORES)], axis=0)[None]
    y_sample = np.concatenate([r[c]["y"][NOWN * 128:] for c in range(NCORES)], axis=0)[:, None, :]
    last = r[NCORES - 1]
    k_p = last["kv_p"][:, 0].reshape(2, 1, 128, 4, 64)
    v_p = last["kv_p"][:, 1].reshape(2, 1, 128, 4, 64)
    c_p = last["conv_p"].reshape(2, 1, 30, D)
    f_p = last["ffn_p"].reshape(2, 1, 2, DFF)
    k_s = np.concatenate([r[c]["k_s"] for c in range(NCORES)], axis=1).reshape(2, 128, 128, 4, 64)
    v_s = np.concatenate([r[c]["v_s"] for c in range(NCORES)], axis=1).reshape(2, 128, 128, 4, 64)
    c_s = np.concatenate([r[c]["conv_s"] for c in range(NCORES)], axis=1)
    f_s = np.concatenate([r[c]["ffn_s"] for c in range(NCORES)], axis=1)
    f32 = lambda a: np.ascontiguousarray(a, dtype=np.float32)
    return (f32(y_prompt), f32(y_sample), f32(k_p), f32(v_p), f32(c_p), f32(f_p), f32(k_s), f32(v_s), f32(c_s), f32(f_s))
```

```python
import numpy as np
from contextlib import ExitStack
import concourse.bass as bass
import concourse.mybir as mybir
from concourse.bass_utils import run_bass_kernel_spmd

F32 = mybir.dt.float32
BF16 = mybir.dt.bfloat16
AF = mybir.ActivationFunctionType
ALU = mybir.AluOpType
AX = mybir.AxisListType

NCORES = 8
NT = 19
NOWN = 16
SCS = [(0, 5), (5, 5), (10, 5), (15, 4)]
TCM = 640
NS = 16
D = 1024
DFF = 2816
NFC = 22
DIN = 5632
EPS = 1e-6
SCALE = 0.125
LV = 8 * 5 + 22 + 8 * 31 + 22 * 3
NV = 2 * LV + 8
O_NM, O_CDB, O_LG, O_LB, O_NF, O_FDB, O_CDW, O_FDW = 0, 8, 16, 24, 32, 40, 62, 62 + 248
NEGM = -30000.0

ENGS = ("tensor", "vector", "scalar", "gpsimd", "sync")


class Sched:
    def __init__(self, nc, stack):
        self.nc = nc
        self.stack = stack
        self.ops = {e: [] for e in ENGS}
        self.sem = {}
        self.cnt = {}
        self.seen = {e: {} for e in ENGS}
        self.res = {}
        self.chan = {}
        self.nsem = 0
        self.pb = 0
        self.reserved = set()
        self.past = []
        self.new_phase()

    debug_tags = False
    names = []

    def _tag(self):
        if not self.debug_tags:
            return ""
        import traceback
        st = traceback.extract_stack(limit=6)
        return ">".join(str(f.lineno) for f in st[:-2])

    def _newsem(self, name):
        self.nsem += 1
        return self.stack.enter_context(self.nc.semaphore(name))

    def new_phase(self):
        for e in ("tensor", "vector", "scalar", "gpsimd"):
            if e in self.sem and self.cnt[e] > 0:
                self.past.append((self.sem[e], self.cnt[e]))
            self.sem[e] = self._newsem("s_%s_%d" % (e, self.nsem))
            self.cnt[e] = 0

    def bank(self, n=1):
        while True:
            if self.pb % 8 + n > 8:
                self.pb += 8 - self.pb % 8
            b = self.pb % 8
            if any((b + i) in self.reserved for i in range(n)):
                self.pb += 1
                continue
            self.pb += n
            return b

    def _deps(self, eng, reads, writes):
        evs = []
        for r in reads:
            st = self.res.get(r)
            if st is not None and st[0] is not None:
                evs.append(st[0])
            if st is not None and isinstance(r, tuple) and r[0] == "ps":
                evs.extend(ev for ev in st[1] if ev[2] != eng)
        for w in writes:
            st = self.res.get(w)
            if st is not None:
                if st[0] is not None:
                    evs.append(st[0])
                evs.extend(st[1])
        need = {}
        for (s, v, e) in evs:
            if e == "tensor" and eng == "tensor":
                continue
            k = id(s)
            if self.seen[eng].get(k, 0) >= v:
                continue
            if k not in need or need[k][1] < v:
                need[k] = (s, v)
        waits = []
        for k, (s, v) in need.items():
            self.seen[eng][k] = v
            waits.append((s, v))
        return waits

    def _commit(self, ev, reads, writes):
        for r in reads:
            st = self.res.setdefault(r, [None, []])
            st[1].append(ev)
        for w in writes:
            self.res[w] = [ev, []]

    def op(self, eng, fns, reads=(), writes=()):
        if callable(fns):
            fns = [fns]
        waits = self._deps(eng, reads, writes)
        self.cnt[eng] += 1
        ev = (self.sem[eng], self.cnt[eng], eng)
        self.ops[eng].append((waits, fns, (self.sem[eng], 1), self._tag()))
        self._commit(ev, reads, writes)

    def dma(self, queue, out, in_, chan, reads=(), writes=()):
        self.nout = getattr(self, "nout", 0) + 1
        writes = [("OUT", self.nout) if w == "OUT" else w for w in writes]
        if chan not in self.chan:
            self.chan[chan] = [self._newsem("d_%s" % chan), 0]
        c = self.chan[chan]
        waits = self._deps(queue, reads, writes)
        c[1] += 16
        ev = (c[0], c[1], "dma")
        self.ops[queue].append((waits, [lambda e: e.dma_start(out=out, in_=in_)], (c[0], 16), self._tag()))
        self._commit(ev, reads, writes)

    def barrier(self):
        evs = list(self.past)
        for e in ("tensor", "vector", "scalar", "gpsimd"):
            if self.cnt[e] > 0:
                evs.append((self.sem[e], self.cnt[e]))
        for c in self.chan.values():
            evs.append((c[0], c[1]))
        for eng in ENGS:
            waits = []
            for (s, v) in evs:
                if self.seen[eng].get(id(s), 0) < v:
                    self.seen[eng][id(s)] = v
                    waits.append((s, v))
            self.ops[eng].append((waits, [], None, ""))

    def emit(self):
        nc = self.nc
        with nc.Block() as block:
            for ename in ENGS:
                lst = self.ops[ename]

                def body(e, lst=lst):
                    for waits, fns, inc, tag in lst:
                        for (s, v) in waits:
                            e.wait_ge(s, v)
                        ins = None
                        for f in fns:
                            ins = f(e)
                            if self.debug_tags:
                                ins.annotate(tag)
                                self.names.append((ins.ins.name, ename, tag, str(ins)[:300]))
                        if inc is not None and ins is not None:
                            ins.then_inc(inc[0], inc[1])

                getattr(block, ename)(body)


def ntile_list(tc):
    if tc <= 320:
        return [(0, tc)]
    h = tc // 2
    return [(0, h), (h, tc - h)]


def build(dbg=None):
    nc = bass.Bass("TRN2", target_bir_lowering=False)

    def din(name, shape):
        return nc.dram_tensor(name, shape, F32, kind="ExternalInput").ap()

    def dout(name, shape):
        return nc.dram_tensor(name, shape, F32, kind="ExternalOutput").ap()

    xin = din("xin", [NT * 128, D])
    xs = din("xs", [NS, D])
    masks = din("masks", [128, 5, 256])
    masks_s = din("masks_s", [128, 16])
    rowsel = din("rowsel", [128, 16])
    valid = din("valid", [128, 384])
    vecs = din("vecs", [128, NV])
    sinkrow = din("sinkrow", [128, 2, 16])
    sinkcol = din("sinkcol", [128, 2, 4])
    w_in = din("w_in", [2, D, DIN])
    w_pw = din("w_pw", [2, D, D])
    w_ao = din("w_ao", [2, D, D])
    w_out = din("w_out", [2, D, D])
    w_up = din("w_up", [2, D, 2 * DFF])
    w_dn = din("w_dn", [2, DFF, D])
    ck = din("ck", [2, NS, 128, 256])
    cv = din("cv", [2, NS, 128, 256])
    sconv = din("sconv", [2, NS, 30, D])
    sffn = din("sffn", [2, NS, 2, DFF])

    y = dout("y", [NOWN * 128 + NS, D])
    kv_p = dout("kv_p", [2, 2, 128, 256])
    conv_p = dout("conv_p", [2, 30, D])
    ffn_p = dout("ffn_p", [2, 2, DFF])
    k_s = dout("k_s", [2, NS, 128, 256])
    v_s = dout("v_s", [2, NS, 128, 256])
    conv_s = dout("conv_s", [2, NS, 30, D])
    ffn_s = dout("ffn_s", [2, NS, 2, DFF])

    with ExitStack() as es:
        def sb(name, shape, dt):
            return es.enter_context(nc.sbuf_tensor(name, shape, dt))

        idf = sb("idf", [128, 128], F32)
        idb = sb("idb", [128, 128], BF16)
        onesb = sb("onesb", [128, 128], BF16)
        epsc = sb("epsc", [128, 1], F32)
        vec = sb("vec", [128, NV], F32)
        masksb = sb("masksb", [128, 4, 16], BF16)
        rowselt = sb("rowselt", [128, 16], F32)
        validb = sb("validb", [128, 384], BF16)
        snk = sb("snk", [128, 2, 16], F32)
        nsnk = sb("nsnk", [128, 2, 16], F32)
        snkc = sb("snkc", [128, 2, 4], F32)
        nsnkc = sb("nsnkc", [128, 2, 4], F32)

        xT = sb("xT", [128, 8, TCM], F32)
        xnT = sb("xnT", [128, 8, TCM], BF16)
        R = sb("R", [128, 24, TCM], BF16)
        aT = sb("aT", [128, 8, TCM], BF16)
        kT = sb("kT", [128, 4, 128 + TCM], BF16)
        Vt = sb("Vt", [128, 6, 4, 128], BF16)
        kprev = sb("kprev", [128, 2, 4, 128], BF16)
        vprev = sb("vprev", [128, 2, 4, 128], BF16)
        uT = sb("uT", [128, 2, 30 + TCM], BF16)
        uhist = sb("uhist", [128, 2, 8, 30], BF16)
        hT = sb("hT", [128, 2, 2 + TCM], BF16)
        hhist = sb("hhist", [128, 2, 22, 2], BF16)
        NWS = 6
        wsl = sb("wsl", [128, NWS, 8, 128], BF16)
        wdsl = sb("wdsl", [128, 2, 22, 128], BF16)
        wv = sb("wv", [128, 8, 256], BF16)
        wk = sb("wk", [128, 8, 256], BF16)
        dg = sb("dg", [128, 31, 128], BF16)
        dgf = sb("dgf", [128, 2, 3, 128], BF16)
        xstage = sb("xstage", [128, 2, 1024], F32)
        ostage = sb("ostage", [128, 2, 512], F32)
        sq = sb("sq", [128, 8, 320], BF16)
        st1 = sb("st1", [128, 2, 320], F32)
        st2 = sb("st2", [128, 2, 320], F32)
        tmp = sb("tmp", [128, 4, 320], F32)
        esb = sb("esb", [128, 2, 4, 256], F32)
        esnk = sb("esnk", [128, 2, 16], F32)
        pT = sb("pT", [128, 2, 4, 2, 128], BF16)
        pbf = sb("pbf", [128, 2, 4, 256], BF16)
        maskb2 = sb("maskb2", [128, 5, 512], BF16)
        sm = sb("sm", [128, 8, 8, 4], F32)
        ytmp = esb[:, 0].rearrange("p g (u c) -> p (g u) c", u=2)
        uf32 = sb("uf32", [128, 8, 30], F32)
        hf32 = sb("hf32", [128, 22, 16], F32)
        hf32p = sb("hf32p", [128, 22, 2], F32)
        uf32s = sb("uf32s", [128, 8, NS], F32)
        uTs = sb("uTs", [128, 8, 31, NS], BF16)
        hTs = R[:, 0:22, 544:544 + 3 * NS].rearrange("p j (k b) -> p j k b", k=3)
        kvst = sb("kvst", [128, 2, 2, 256], F32)
        kdup = sb("kdup", [128, 1, 4, 128], F32)
        ess = sb("ess", [128, 1, 4, 144], F32)
        KTb = pbf[:, :, :, 0:128]
        Vb = pbf[:, :, :, 128:256]
        pTs = pT[:, :, :, 0, :]
        pTn = pT[0:NS, :, :, 1, :]
        Vnew = Vt[0:NS, 5]

        ps = es.enter_context(nc.psum_tensor("ps", [128, 8, 512], F32))

        S = Sched(nc, es)
        ctr = {"w": 0, "wd": 0, "xs": 0, "os": 0, "tmp": 0, "st": 0, "es": 0, "sm": 0, "kv": 0, "ev": 0}

        def rot(name, n):
            v = ctr[name] % n
            ctr[name] += 1
            return v

        def evac_eng():
            return "vector" if rot("ev", 2) == 0 else "scalar"

        def copy_op(eng, out, in_, reads, writes):
            if eng == "scalar":
                S.op("scalar", lambda e: e.activation(out=out, in_=in_, func=AF.Copy), reads=reads, writes=writes)
            else:
                S.op(eng, lambda e: e.tensor_copy(out, in_), reads=reads, writes=writes)

        S.dma("sync", vec[:], vecs[:, :], "c_vec", writes=["vec"])
        S.dma("sync", snk[:], sinkrow[:, :, :], "c_snk", writes=["snk"])
        S.dma("sync", snkc[:], sinkcol[:, :, :], "c_snkc", writes=["snkc"])
        S.dma("sync", rowselt[:], rowsel[:, :], "c_rs", writes=["rowsel"])
        S.dma("gpsimd", maskb2[:, :, 0:256], masks[:, :, :], "c_mask", writes=["maskb"])
        S.dma("gpsimd", maskb2[:, :, 256:512], masks[:, :, :], "c_mask", writes=["maskb"])
        S.dma("gpsimd", validb[:], valid[:, :], "c_valid", writes=["validb"])
        for k4 in range(4):
            S.dma("gpsimd", masksb[:, k4, :], masks_s[:, :], "c_masks", writes=["masksb"])
        S.op("gpsimd", lambda e: e.memset(idf[:], 1.0), writes=["idf"])
        S.op("gpsimd", lambda e: e.affine_select(idf[:], idf[:], pattern=[[-1, 128]], compare_op=ALU.is_equal,
                                                 fill=0.0, base=0, channel_multiplier=1), reads=["idf"], writes=["idf"])
        S.op("vector", lambda e: e.tensor_copy(idb[:], idf[:]), reads=["idf"], writes=["idb"])
        S.op("vector", lambda e: e.memset(onesb[:], 1.0 / 1024.0), writes=["onesb"])
        S.op("vector", lambda e: e.memset(epsc[:], EPS), writes=["epsc"])
        S.op("gpsimd", lambda e: e.memset(uT[:], 0.0), writes=["init_uT"])
        S.op("gpsimd", lambda e: e.memset(hT[:], 0.0), writes=["init_hT"])
        S.op("gpsimd", lambda e: e.memset(xnT[:], 0.0), writes=["init_xnT"])
        S.op("gpsimd", lambda e: e.memset(kT[:], 0.0), writes=["init_kT"])
        S.op("gpsimd", lambda e: e.memset(Vt[:], 0.0), writes=["init_Vt"])
        S.op("gpsimd", lambda e: e.memset(aT[:], 0.0), writes=["init_aT"])
        S.op("vector", lambda e: e.memset(kprev[:], 0.0), writes=[("kprev", 0), ("kprev", 1)])
        S.op("vector", lambda e: e.memset(vprev[:], 0.0), writes=[("vprev", 0), ("vprev", 1)])
        S.op("vector", lambda e: e.memset(uhist[:], 0.0), writes=[("uhist", l, j) for l in range(2) for j in range(8)])
        S.op("vector", lambda e: e.memset(hhist[:], 0.0), writes=[("hhist", l, j) for l in range(2) for j in range(22)])
        S.op("vector", lambda e: e.tensor_scalar(nsnk[:], snk[:], -1.0, None, ALU.mult), reads=["snk"], writes=["nsnk"])
        S.op("scalar", lambda e: e.activation(out=esnk[:], in_=snk[:], func=AF.Exp), reads=["snk"], writes=["esnk"])
        S.op("vector", lambda e: e.tensor_scalar(nsnkc[:], snkc[:], -1.0, None, ALU.mult), reads=["snkc"], writes=["nsnkc"])

        def vcol(l, off, j):
            c = l * LV + off + j
            return vec[:, c:c + 1]

        def load_w(src, k_chunks=8):
            if k_chunks == 8:
                s = rot("w", NWS)
                S.dma("gpsimd", wsl[:, s, :, :], src.rearrange("(k p) m -> p k m", p=128), "w%d" % s, writes=[("w", s)])
                return ("w", s), (lambda k, s=s: wsl[:, s, k, :])
            s = rot("wd", 2)
            S.dma("gpsimd", wdsl[:, s, :, :], src.rearrange("(k p) m -> p k m", p=128), "wd%d" % s, writes=[("wd", s)])
            return ("wd", s), (lambda k, s=s: wdsl[:, s, k, :])

        def mm_group(out_ap, pairs, reads, bank_keys):
            n = len(pairs)
            fns = []
            for i, (l_, r_) in enumerate(pairs):
                fns.append(lambda e, l_=l_, r_=r_, i=i: e.matmul(out_ap, l_, r_, start=(i == 0), stop=(i == n - 1)))
            S.op("tensor", fns, reads=reads, writes=bank_keys)

        def proj(wkey, wfn, src, skeys, c0, n, kc=8):
            b = S.bank()
            mm_group(ps[:, b, 0:n], [(wfn(k), src(k, c0, n)) for k in range(kc)],
                     reads=[wkey] + skeys, bank_keys=[("ps", b)])
            return b

        class V:
            pass

        def make_views(pc, has_sample):
            v = V()
            v.pc = pc
            v.so = pc if has_sample else None
            v.tc = tc = pc + (NS if has_sample else 0)
            v.nts = ntile_list(tc)
            v.si = len(v.nts) - 1
            v.t_first = 0
            v.x = lambda j, c0, n: xT[:, j, c0:c0 + n]
            v.xn = lambda j, c0, n: xnT[:, j, c0:c0 + n]
            v.q = lambda j, c0, n: R[:, j, c0:c0 + n]
            v.m = lambda j, c0, n: R[:, 8 + j, c0:c0 + n]
            v.s = lambda j, c0, n: R[:, 16 + j, c0:c0 + n]
            v.hg = lambda j, c0, n: R[:, j, c0:c0 + n]
            v.a = lambda j, c0, n: aT[:, j, c0:c0 + n]
            return v

        def kx(i):
            return [("xT", j, i) for j in range(8)]

        def kxn(i):
            return [("xnT", j, i) for j in range(8)]

        def kR(base, i, cnt=8):
            return [("R", base + j, i) for j in range(cnt)]

        def ka(i):
            return [("aT", j, i) for j in range(8)]

        def nts_of(v, c0, c1):
            return [i for i, (a, n) in enumerate(v.nts) if a < c1 and a + n > c0]

        def rms_stats(src_fn, src_keys, c0, n):
            S.op("scalar", lambda e: e.activation(out=sq[:, :, 0:n], in_=src_fn(c0, n), func=AF.Square),
                 reads=src_keys, writes=["sq"])
            b = S.bank()
            mm_group(ps[:, b, 0:n], [(onesb[:], sq[:, j, 0:n]) for j in range(8)], reads=["sq", "onesb"], bank_keys=[("ps", b)])
            sl = rot("st", 2)
            S.op("scalar", lambda e: e.activation(out=st1[:, sl, 0:n], in_=ps[:, b, 0:n], func=AF.Sqrt, bias=epsc[:, 0:1], scale=1.0),
                 reads=[("ps", b), "epsc"], writes=[("st1", sl)])
            S.op("vector", lambda e: e.reciprocal(st1[:, sl, 0:n], st1[:, sl, 0:n]), reads=[("st1", sl)], writes=[("st1", sl)])
            return sl

        def rmsnorm_to_xn(v, l, goff):
            for i, (c0, n) in enumerate(v.nts):
                sl = rms_stats(lambda c0, n: xT[:, :, c0:c0 + n], kx(i), c0, n)
                for j in range(8):
                    S.op("vector", lambda e, j=j, c0=c0, n=n, sl=sl: e.scalar_tensor_tensor(
                        out=xnT[:, j, c0:c0 + n], in0=xT[:, j, c0:c0 + n], scalar=vcol(l, goff, j) if l < 2 else vec[:, 2 * LV + j:2 * LV + j + 1],
                        in1=st1[:, sl, 0:n], op0=ALU.mult, op1=ALU.mult),
                        reads=[("xT", j, i), ("st1", sl), "vec"], writes=[("xnT", j, i)])

        def load_x(v, t0, ntl):
            if v.so is not None:
                so = v.so
                sl = rot("xs", 2)
                S.dma("sync", xstage[0:NS, sl, :], xs[:, :], "xs%d" % sl, writes=[("xstage", sl)])
                b = S.bank()
                fns = [lambda e, j=j: e.transpose(ps[:, b, j * NS:(j + 1) * NS], xstage[0:NS, sl, j * 128:(j + 1) * 128], idf[0:NS, 0:NS])
                       for j in range(8)]
                S.op("tensor", fns, reads=[("xstage", sl), "idf"], writes=[("ps", b)])
                S.op("vector", lambda e: e.tensor_copy(xT[:, :, so:so + NS], ps[:, b, 0:8 * NS].rearrange("p (j t) -> p j t", j=8)),
                     reads=[("ps", b)], writes=kx(v.si))
            for t in range(ntl):
                sl = rot("xs", 2)
                S.dma("sync", xstage[:, sl, :], xin[(t0 + t) * 128:(t0 + t + 1) * 128, :], "xs%d" % sl, writes=[("xstage", sl)])
                for half in range(2):
                    b = S.bank()
                    fns = [lambda e, j=j, b=b, sl=sl: e.transpose(ps[:, b, (j % 4) * 128:(j % 4 + 1) * 128],
                                                           xstage[:, sl, j * 128:(j + 1) * 128], idf[:])
                           for j in range(half * 4, half * 4 + 4)]
                    S.op("tensor", fns, reads=[("xstage", sl), "idf"], writes=[("ps", b)])
                    its = nts_of(v, t * 128, t * 128 + 128)
                    copy_op(evac_eng(), xT[:, half * 4:half * 4 + 4, t * 128:(t + 1) * 128],
                            ps[:, b, :].rearrange("p (j t) -> p j t", j=4),
                            reads=[("ps", b)], writes=[("xT", j, i) for j in range(half * 4, half * 4 + 4) for i in its])

        def conv_branch(v, l, sc_first, sc_last, part=None):
            tc = v.tc
            if part != 1:
              conv_part(v, l, sc_first, sc_last)
            if part != 0:
              ln_part(v, l)

        def conv_part(v, l, sc_first, sc_last):
            tc, pc, so = v.tc, v.pc, v.so
            import os as _os
            if so is not None:
                for g in range(int(_os.environ.get("KG", "4"))):
                    sl = rot("xs", 2)
                    S.dma("sync", xstage[0:120, sl, :], sconv[l, 4 * g:4 * g + 4, :, :].rearrange("b k c -> (b k) c"),
                          "xs%d" % sl, writes=[("xstage", sl)])
                    for half in range(2):
                        b = S.bank()
                        kvar = _os.environ.get("KVAR", "")
                        W_ = 128 if "A" in kvar else 120
                        fns = [lambda e, j=j, b=b, sl=sl, W_=W_: e.transpose(ps[:, b, (j % 4) * W_:(j % 4) * W_ + 120],
                                                               xstage[0:120, sl, j * 128:(j + 1) * 128], idf[0:120, 0:120])
                               for j in range(half * 4, half * 4 + 4)]
                        S.op("tensor", fns, reads=[("xstage", sl), "idf"], writes=[("ps", b)])
                        eng_ = evac_eng()
                        for j in range(half * 4, half * 4 + 4):
                            copy_op(eng_, uTs[:, j, 0:30, 4 * g:4 * g + 4],
                                    ps[:, b, (j % 4) * W_:(j % 4) * W_ + 120].rearrange("p (b k) -> p k b", b=4),
                                    reads=[("ps", b)], writes=[("uTs", j)])
            import os as _os
            ksub = int(_os.environ.get("KSUB", "99"))
            if ksub <= 1:
                return
            for j in range(8):
                if ksub <= 2 and j >= 1:
                    return
                wa, fa = load_w(w_in[l, :, j * 128:(j + 1) * 128])
                wb, fb = load_w(w_in[l, :, D + j * 128:D + (j + 1) * 128])
                S.op("vector", lambda e, j=j: e.tensor_tensor(
                    dg[:], idb[:, None, :].broadcast_to([128, 31, 128]),
                    vec[:, l * LV + O_CDW + j * 31:l * LV + O_CDW + (j + 1) * 31, None].broadcast_to([128, 31, 128]), ALU.mult),
                    reads=["idb", "vec"], writes=["dg"])
                ub = j % 2
                S.op("vector", lambda e, j=j, ub=ub: e.tensor_copy(uT[:, ub, 0:30], uhist[:, l, j, :]),
                     reads=[("uhist", l, j)], writes=[("uT", ub, "h")])
                for i, (c0, n) in enumerate(v.nts):
                    ba = proj(wa, fa, v.xn, kxn(i), c0, n)
                    bb = proj(wb, fb, v.xn, kxn(i), c0, n)
                    ts = rot("tmp", 4)
                    S.op("scalar", lambda e, bb=bb, ts=ts, n=n: e.activation(out=tmp[:, ts, 0:n], in_=ps[:, bb, 0:n], func=AF.Sigmoid),
                         reads=[("ps", bb)], writes=[("tmp", ts)])
                    S.op("vector", lambda e, ba=ba, ts=ts, c0=c0, n=n, ub=ub: e.tensor_tensor(
                        uT[:, ub, 30 + c0:30 + c0 + n], ps[:, ba, 0:n], tmp[:, ts, 0:n], ALU.mult),
                        reads=[("ps", ba), ("tmp", ts)], writes=[("uT", ub, i)])
                    if sc_last and c0 <= pc - 30 and c0 + n >= pc:
                        lo = pc - 30 - c0
                        S.op("vector", lambda e, ba=ba, ts=ts, lo=lo, j=j: e.tensor_tensor(
                            uf32[:, j, :], ps[:, ba, lo:lo + 30], tmp[:, ts, lo:lo + 30], ALU.mult),
                            reads=[("ps", ba), ("tmp", ts)], writes=[("uf32", j)])
                    if so is not None and i == v.si:
                        lo = so - c0
                        S.op("vector", lambda e, ba=ba, ts=ts, lo=lo, j=j: e.tensor_tensor(uf32s[:, j, :], ps[:, ba, lo:lo + NS], tmp[:, ts, lo:lo + NS], ALU.mult),
                             reads=[("ps", ba), ("tmp", ts)], writes=[("uf32s", j)])
                        S.op("vector", lambda e, j=j: e.tensor_copy(uTs[:, j, 30, :], uf32s[:, j, :]),
                             reads=[("uf32s", j)], writes=[("uTs", j)])
                nn = len(v.nts)
                if sc_first:
                    S.op("vector", lambda e, ub=ub: e.tensor_tensor(uT[:, ub, 30:30 + 384], uT[:, ub, 30:30 + 384], validb[:], ALU.mult),
                         reads=[("uT", ub, i) for i in range(nn)] + ["validb"], writes=[("uT", ub, i) for i in range(nn)])
                S.op("vector", lambda e, j=j, ub=ub: e.tensor_copy(uhist[:, l, j, :], uT[:, ub, pc:pc + 30]),
                     reads=[("uT", ub, i) for i in range(nn)] + [("uT", ub, "h")], writes=[("uhist", l, j)])
                for i, (c0, n) in enumerate(v.nts):
                    b = S.bank()
                    pairs = [(dg[:, k, :], uT[:, ub, c0 + k:c0 + k + n]) for k in range(31)]
                    rd = ["dg", ("uT", ub, "h")] + [("uT", ub, ii) for ii in range(max(0, i - 1), i + 1)]
                    mm_group(ps[:, b, 0:n], pairs, reads=rd, bank_keys=[("ps", b)])
                    S.op("scalar", lambda e, b=b, j=j, c0=c0, n=n: e.activation(out=v.s(j, c0, n), in_=ps[:, b, 0:n], func=AF.Identity,
                                                                                 bias=vcol(l, O_CDB, j), scale=1.0),
                         reads=[("ps", b), "vec"], writes=[("R", 16 + j, i)])
                    if so is not None and i == v.si:
                        b = S.bank()
                        mm_group(ps[:, b, 0:NS], [(dg[:, k, :], uTs[:, j, k, :]) for k in range(31)], reads=["dg", ("uTs", j)], bank_keys=[("ps", b)])
                        S.op("scalar", lambda e, b=b, j=j: e.activation(out=v.s(j, so, NS), in_=ps[:, b, 0:NS], func=AF.Identity,
                                                                        bias=vcol(l, O_CDB, j), scale=1.0),
                             reads=[("ps", b), "vec"], writes=[("R", 16 + j, i)])
        def ln_part(v, l):
            for i, (c0, n) in enumerate(v.nts):
                bm = S.bank()
                mm_group(ps[:, bm, 0:n], [(onesb[:], v.s(j, c0, n)) for j in range(8)], reads=kR(16, i) + ["onesb"], bank_keys=[("ps", bm)])
                S.op("scalar", lambda e, c0=c0, n=n: e.activation(out=sq[:, :, 0:n], in_=R[:, 16:24, c0:c0 + n], func=AF.Square),
                     reads=kR(16, i), writes=["sq"])
                b2 = S.bank()
                mm_group(ps[:, b2, 0:n], [(onesb[:], sq[:, j, 0:n]) for j in range(8)], reads=["sq", "onesb"], bank_keys=[("ps", b2)])
                sl = rot("st", 2)
                S.op("vector", lambda e, sl=sl, bm=bm, n=n: e.tensor_copy(st2[:, sl, 0:n], ps[:, bm, 0:n]), reads=[("ps", bm)], writes=[("st2", sl)])
                S.op("vector", lambda e, sl=sl, n=n: e.tensor_tensor(st1[:, sl, 0:n], st2[:, sl, 0:n], st2[:, sl, 0:n], ALU.mult),
                     reads=[("st2", sl)], writes=[("st1", sl)])
                S.op("vector", lambda e, sl=sl, b2=b2, n=n: e.tensor_tensor(st1[:, sl, 0:n], ps[:, b2, 0:n], st1[:, sl, 0:n], ALU.subtract),
                     reads=[("ps", b2), ("st1", sl)], writes=[("st1", sl)])
                S.op("vector", lambda e, sl=sl, n=n: e.tensor_scalar(st1[:, sl, 0:n], st1[:, sl, 0:n], 0.0, None, ALU.max),
                     reads=[("st1", sl)], writes=[("st1", sl)])
                S.op("scalar", lambda e, sl=sl, n=n: e.activation(out=st1[:, sl, 0:n], in_=st1[:, sl, 0:n], func=AF.Sqrt, bias=epsc[:, 0:1], scale=1.0),
                     reads=[("st1", sl), "epsc"], writes=[("st1", sl)])
                S.op("vector", lambda e, sl=sl, n=n: e.reciprocal(st1[:, sl, 0:n], st1[:, sl, 0:n]), reads=[("st1", sl)], writes=[("st1", sl)])
                for j in range(8):
                    ts = rot("tmp", 4)
                    S.op("vector", lambda e, j=j, ts=ts, sl=sl, c0=c0, n=n: e.tensor_tensor(tmp[:, ts, 0:n], v.s(j, c0, n), st2[:, sl, 0:n], ALU.subtract),
                         reads=[("R", 16 + j, i), ("st2", sl)], writes=[("tmp", ts)])
                    S.op("vector", lambda e, ts=ts, sl=sl, n=n: e.tensor_tensor(tmp[:, ts, 0:n], tmp[:, ts, 0:n], st1[:, sl, 0:n], ALU.mult),
                         reads=[("tmp", ts), ("st1", sl)], writes=[("tmp", ts)])
                    S.op("scalar", lambda e, j=j, ts=ts, c0=c0, n=n: e.activation(out=v.s(j, c0, n), in_=tmp[:, ts, 0:n], func=AF.Silu,
                                                                                  bias=vcol(l, O_LB, j), scale=vcol(l, O_LG, j)),
                         reads=[("tmp", ts), "vec"], writes=[("R", 16 + j, i)])

        def emit_rows(l, src_fn, ncols, nchunks, dst_fn, keys):
            for g0 in range(0, nchunks, 4):
                g1 = min(nchunks, g0 + 4)
                b = S.bank()
                fns = [lambda e, j=j, b=b, g0=g0: e.transpose(ps[0:ncols, b, (j - g0) * 128:(j - g0 + 1) * 128], src_fn(j), idf[:])
                       for j in range(g0, g1)]
                S.op("tensor", fns, reads=keys + ["idf"], writes=[("ps", b)])
                sl = rot("os", 2)
                w = (g1 - g0) * 128
                copy_op(evac_eng(), ostage[0:ncols, sl, 0:w], ps[0:ncols, b, 0:w], reads=[("ps", b)], writes=[("ostage", sl)])
                S.dma("sync", dst_fn(g0 * 128, w), ostage[0:ncols, sl, 0:w], "os%d" % sl, reads=[("ostage", sl)], writes=["OUT"])

        def q_proj(v, l):
            for j in range(8):
                wq, fq = load_w(w_in[l, :, 2 * D + j * 128:2 * D + (j + 1) * 128])
                for i, (c0, n) in enumerate(v.nts):
                    b = proj(wq, fq, v.xn, kxn(i), c0, n)
                    copy_op(evac_eng(), v.q(j, c0, n), ps[:, b, 0:n], reads=[("ps", b)], writes=[("R", j, i)])

        def qkv(v, l, sc_last):
            tc, pc, so = v.tc, v.pc, v.so
            for kh in range(4):
                s = rot("w", NWS)
                src = w_in[l, :, 3 * D + kh * 64:3 * D + (kh + 1) * 64].rearrange("(k p) m -> p k m", p=128)
                S.dma("gpsimd", wsl[:, s, :, 0:64], src, "w%d" % s, writes=[("w", s)])
                S.dma("gpsimd", wsl[:, s, :, 64:128], src, "w%d" % s, writes=[("w", s)])
                fk = lambda k, s=s: wsl[:, s, k, :]
                for i, (c0, n) in enumerate(v.nts):
                    b = proj(("w", s), fk, v.xn, kxn(i), c0, n)
                    copy_op(evac_eng(), kT[:, kh, 128 + c0:128 + c0 + n], ps[:, b, 0:n], reads=[("ps", b)], writes=[("kT", kh, i)])
            S.dma("gpsimd", wv[:], w_in[l, :, 3 * D + 256:3 * D + 512].rearrange("(k p) m -> p k m", p=128), "wv", writes=["wv"])
            S.dma("gpsimd", wk[:], w_in[l, :, 3 * D:3 * D + 256].rearrange("(k p) m -> p k m", p=128), "wk", writes=["wk"])
            ntl = pc // 128
            for t in range(max(0, v.t_first - 1), ntl):
                its = nts_of(v, t * 128, t * 128 + 128)
                rk = [("xnT", j, i) for j in range(8) for i in its]
                b = S.bank()
                mm_group(ps[:, b, 0:256], [(xnT[:, k, t * 128:(t + 1) * 128], wv[:, k, :]) for k in range(8)], reads=rk + ["wv"], bank_keys=[("ps", b)])
                S.op("vector", lambda e, b=b, t=t: e.tensor_copy(Vt[:, 1 + t].rearrange("p k (u d) -> p k u d", u=2),
                                                                 ps[:, b, 0:256].rearrange("p (k d) -> p k d", k=4)[:, :, None, :].broadcast_to([128, 4, 2, 64])),
                     reads=[("ps", b)], writes=[("Vt", 1 + t)])
                if sc_last and t == ntl - 1:
                    sl = rot("os", 2)
                    S.op("vector", lambda e, b=b, sl=sl: e.tensor_copy(ostage[:, sl, 0:256], ps[:, b, 0:256]),
                         reads=[("ps", b)], writes=[("ostage", sl)])
                    S.dma("sync", kv_p[l, 1, :, :], ostage[:, sl, 0:256], "os%d" % sl, reads=[("ostage", sl)], writes=["OUT"])
                    b = S.bank()
                    mm_group(ps[:, b, 0:256], [(xnT[:, k, t * 128:(t + 1) * 128], wk[:, k, :]) for k in range(8)], reads=rk + ["wk"], bank_keys=[("ps", b)])
                    sl = rot("os", 2)
                    S.op("scalar", lambda e, b=b, sl=sl: e.activation(out=ostage[:, sl, 0:256], in_=ps[:, b, 0:256], func=AF.Copy),
                         reads=[("ps", b)], writes=[("ostage", sl)])
                    S.dma("sync", kv_p[l, 0, :, :], ostage[:, sl, 0:256], "os%d" % sl, reads=[("ostage", sl)], writes=["OUT"])
            if so is not None:
                b = S.bank()
                mm_group(ps[0:NS, b, 0:256], [(xnT[:, k, so:so + NS], wv[:, k, :]) for k in range(8)], reads=kxn(v.si) + ["wv"], bank_keys=[("ps", b)])
                S.op("vector", lambda e, b=b: e.tensor_copy(Vnew[:].rearrange("p k (u d) -> p k u d", u=2),
                                                          ps[0:NS, b, 0:256].rearrange("p (k d) -> p k d", k=4)[:, :, None, :].broadcast_to([NS, 4, 2, 64])),
                     reads=[("ps", b)], writes=[("Vt", 5)])
                sl = rot("os", 2)
                S.op("vector", lambda e, b=b, sl=sl: e.tensor_copy(ostage[0:NS, sl, 0:256], ps[0:NS, b, 0:256]),
                     reads=[("ps", b)], writes=[("ostage", sl)])
                S.dma("sync", v_s[l, :, 127, :], ostage[0:NS, sl, 0:256], "os%d" % sl, reads=[("ostage", sl)], writes=["OUT"])
                b = S.bank()
                mm_group(ps[0:NS, b, 0:256], [(xnT[:, k, so:so + NS], wk[:, k, :]) for k in range(8)], reads=kxn(v.si) + ["wk"], bank_keys=[("ps", b)])
                sl = rot("os", 2)
                S.op("scalar", lambda e, b=b, sl=sl: e.activation(out=ostage[0:NS, sl, 0:256], in_=ps[0:NS, b, 0:256], func=AF.Copy),
                     reads=[("ps", b)], writes=[("ostage", sl)])
                S.dma("sync", k_s[l, :, 127, :], ostage[0:NS, sl, 0:256], "os%d" % sl, reads=[("ostage", sl)], writes=["OUT"])

        def attention(v, l, t0):
            ntl = v.pc // 128
            nn = len(v.nts)
            tc = v.pc
            S.op("vector", lambda e: e.tensor_copy(kT[:, :, 0:128], kprev[:, l]), reads=[("kprev", l)], writes=[("kT", kh, "prev") for kh in range(4)])
            S.op("vector", lambda e: e.tensor_copy(Vt[:, 0], vprev[:, l]), reads=[("vprev", l)], writes=[("Vt", 0)])
            units = [(t, kh) for t in range(v.t_first, ntl) for kh in range(4)]
            st_ = {}

            def stage_a(u):
                t, kh = u
                gi = t0 + t
                mi = gi if gi < 4 else 4
                its = nts_of(v, t * 128, t * 128 + 128)
                b = S.bank(2)
                sc = ps[:, b:b + 2, :].rearrange("p a (g c) -> p (a g) c", g=2)
                fns = []
                for g in range(4):
                    h = 4 * kh + g
                    jq, pb = h // 2, (h % 2) * 64
                    i4 = (g % 2) * 2 + g // 2
                    fns.append(lambda e, i4=i4, g=g, jq=jq, pb=pb: e.matmul(sc[:, i4, :], R[pb:pb + 64, jq, t * 128:(t + 1) * 128],
                                                                     kT[pb:pb + 64, kh, t * 128:t * 128 + 256],
                                                                     start=(g < 2), stop=False, skip_group_check=True))
                for a_ in range(2):
                    fns.append(lambda e, a_=a_: e.matmul(ps[:, b + a_, :], idb[:], maskb2[:, mi, :], start=False, stop=True, skip_group_check=True))
                rk = [("R", jq, i) for jq in (2 * kh, 2 * kh + 1) for i in its] + [("kT", kh, i) for i in range(nn)] + [("kT", kh, "prev"), "idb", "maskb"]
                S.op("tensor", fns, reads=rk, writes=[("ps", b), ("ps", b + 1)])
                S.reserved.update((b, b + 1))
                st_[u] = (b, sc, its)

            def stage_b1(u):
                t, kh = u
                b, sc, its = st_[u]
                s0 = rot("sm", 8)
                mx, ng, ssum, dd = sm[:, s0, 0, :], sm[:, s0, 1, :], sm[:, s0, 2, :], sm[:, s0, 3, :]
                S.op("vector", lambda e: e.tensor_reduce(mx, sc, AX.X, ALU.max), reads=[("ps", b), ("ps", b + 1)], writes=[("sm", s0, 0)])
                S.op("vector", lambda e: e.scalar_tensor_tensor(out=ng, in0=mx, scalar=-SCALE, in1=nsnk[:, l, 4 * kh:4 * kh + 4],
                                                                op0=ALU.mult, op1=ALU.min),
                     reads=[("sm", s0, 0), "nsnk"], writes=[("sm", s0, 1)])
                S.op("vector", lambda e: e.memset(ssum, 0.0), writes=[("sm", s0, 2)])
                st_[u] = (b, sc, its, s0)

            def stage_b1b(u):
                t, kh = u
                b, sc, its, s0 = st_[u][:4]
                ng, dd = sm[:, s0, 1, :], sm[:, s0, 3, :]
                S.op("scalar", lambda e: e.activation(out=dd, in_=ng, func=AF.Exp), reads=[("sm", s0, 1)], writes=[("sm", s0, 3)])

            def stage_b2(u):
                t, kh = u
                b, sc, its, s0 = st_[u]
                ng, ssum = sm[:, s0, 1, :], sm[:, s0, 2, :]
                eb = rot("es", 2)
                for g in range(4):
                    S.op("scalar", lambda e, g=g: e.activation(
                        out=esb[:, eb, g, :], in_=sc[:, g, :], func=AF.Exp, bias=ng[:, g:g + 1], scale=SCALE, accum_out=ssum[:, g:g + 1]),
                        reads=[("ps", b), ("ps", b + 1), ("sm", s0, 1), ("sm", s0, 2)], writes=[("esb", eb, g), ("sm", s0, 2)])
                S.reserved.difference_update((b, b + 1))
                st_[u] = (b, sc, its, s0, eb)

            def stage_b3(u):
                t, kh = u
                b, sc, its, s0, eb = st_[u]
                ssum, dd = sm[:, s0, 2, :], sm[:, s0, 3, :]
                S.op("vector", lambda e: e.tensor_tensor(dd, dd, esnk[:, l, 4 * kh:4 * kh + 4], ALU.mult), reads=[("sm", s0, 3), "esnk"], writes=[("sm", s0, 3)])
                S.op("vector", lambda e: e.tensor_tensor(dd, dd, ssum, ALU.add), reads=[("sm", s0, 3), ("sm", s0, 2)], writes=[("sm", s0, 3)])
                S.op("vector", lambda e: e.reciprocal(dd, dd), reads=[("sm", s0, 3)], writes=[("sm", s0, 3)])
                S.op("gpsimd", lambda e: e.tensor_tensor(pbf[:, eb], esb[:, eb], dd[:, :, None].broadcast_to([128, 4, 256]), ALU.mult),
                     reads=[("esb", eb, g) for g in range(4)] + [("sm", s0, 3)], writes=[("pbf", eb)])
                st_[u] = (b, sc, its, eb)

            def stage_c(u):
                t, kh = u
                b, sc, its, eb = st_[u]
                bt = S.bank()
                ptv = ps[:, bt, :].bitcast(BF16).rearrange("p (g k q) -> p g k q", g=4, k=2)
                fns = []
                for g in range(4):
                    for blk in range(2):
                        fns.append(lambda e, g=g, blk=blk: e.transpose(ptv[:, g, blk, :], pbf[:, eb, g, blk * 128:(blk + 1) * 128], idb[:]))
                S.op("tensor", fns, reads=[("pbf", eb), "idb"], writes=[("ps", bt)])
                pb_ = rot("kv", 2)
                copy_op("vector", pT[:, pb_], ptv, reads=[("ps", bt)], writes=[("pT", pb_)])
                st_[u] = (its, pb_)

            def stage_c2(u):
                t, kh = u
                its, pb_ = st_.pop(u)
                bo = S.bank()
                fns = []
                for g in range(4):
                    for blk in range(2):
                        fns.append(lambda e, g=g, blk=blk: e.matmul(ps[:, bo, g * 128:(g + 1) * 128], Vt[:, t + blk, kh, :], pT[:, pb_, g, blk, :],
                                                                    start=(blk == 0), stop=(blk == 1)))
                S.op("tensor", fns, reads=[("pT", pb_), ("Vt", t), ("Vt", t + 1)], writes=[("ps", bo)])
                eng_ = evac_eng()
                for hh in range(2):
                    pb = hh * 64
                    copy_op(eng_, aT[pb:pb + 64, 2 * kh:2 * kh + 2, t * 128:(t + 1) * 128],
                            ps[pb:pb + 64, bo, :].rearrange("p (u j q) -> p u j q", u=2, j=2)[:, hh, :, :],
                            reads=[("ps", bo)], writes=[("aT", 2 * kh, i) for i in its] + [("aT", 2 * kh + 1, i) for i in its])

            nu = len(units)
            def un(i):
                return units[i] if 0 <= i < nu else None
            for k in range(-5, nu):
                for off, fn in ((5, stage_a), (4, stage_b1), (2, stage_b3), (3, stage_b2), (4, stage_b1b), (1, stage_c), (0, stage_c2)):
                    u = un(k + off)
                    if u is not None:
                        fn(u)
            S.op("vector", lambda e: e.tensor_copy(kprev[:, l], kT[:, :, tc:tc + 128]),
                 reads=[("kT", kh, i) for kh in range(4) for i in range(nn)], writes=[("kprev", l)])
            S.op("vector", lambda e: e.tensor_copy(vprev[:, l], Vt[:, ntl]), reads=[("Vt", ntl)], writes=[("vprev", l)])

        def attention_sample(v, l):
            bo = S.bank()
            S.reserved.add(bo)
            so, si = v.so, v.si
            S.op("vector", lambda e: e.memset(R[:, 0:8, so + NS:so + 32], 0.0), writes=[("Rpad", 0)])
            for bq in range(NS):
              def unit(bq=bq):
                st = rot("kv", 2)
                S.dma("sync", kvst[:, st, 0, :], ck[l, bq, :, :], "kvk%d" % st, writes=[("kvst", st, 0)])
                S.dma("sync", kvst[:, st, 1, :], cv[l, bq, :, :], "kvv%d" % st, writes=[("kvst", st, 1)])
                S.dma("sync", k_s[l, bq, 0:127, :], kvst[1:128, st, 0, :], "ksok%d" % st, reads=[("kvst", st, 0)], writes=["OUT"])
                S.dma("sync", v_s[l, bq, 0:127, :], kvst[1:128, st, 1, :], "ksov%d" % st, reads=[("kvst", st, 1)], writes=["OUT"])
                S.op("vector", lambda e, st=st: e.tensor_copy(kdup[:, 0].rearrange("p k (u d) -> p k u d", u=2),
                                                            kvst[:, st, 0, :].rearrange("p (k d) -> p k d", k=4)[:, :, None, :].broadcast_to([128, 4, 2, 64])),
                     reads=[("kvst", st, 0)], writes=[("kdup", 0)])
                S.op("vector", lambda e, st=st: e.tensor_copy(Vb[:, st].rearrange("p k (u d) -> p k u d", u=2),
                                                            kvst[:, st, 1, :].rearrange("p (k d) -> p k d", k=4)[:, :, None, :].broadcast_to([128, 4, 2, 64])),
                     reads=[("kvst", st, 1)], writes=[("pbf", st)])
                bt = S.bank()
                S.op("tensor", [lambda e, kh=kh: e.transpose(ps[:, bt, kh * 128:(kh + 1) * 128], kdup[:, 0, kh, :], idf[:]) for kh in range(4)],
                     reads=[("kdup", 0), "idf"], writes=[("ps", bt)])
                copy_op(evac_eng(), KTb[:, st], ps[:, bt, :].rearrange("p (k s) -> p k s", k=4), reads=[("ps", bt)], writes=[("pbf", st)])
                b = S.bank(2)
                scv = ps[:, b:b + 2, :].rearrange("p a (k c) -> p (a k) c", k=2)
                fns = []
                for pbsel in range(2):
                    for kh in range(4):
                        for g in (pbsel, pbsel + 2):
                            h = 4 * kh + g
                            jq, pb = h // 2, (h % 2) * 64
                            fns.append(lambda e, kh=kh, g=g, jq=jq, pb=pb: e.matmul(
                                scv[32 * g:32 * g + 32, kh, 0:128], R[pb:pb + 64, jq, so:so + 32], KTb[pb:pb + 64, st, kh, :],
                                start=True, stop=True, tile_position=(pb, 32 * g)))
                            fns.append(lambda e, kh=kh, g=g, jq=jq, pb=pb: e.matmul(
                                scv[32 * g:32 * g + 32, kh, 128:144], R[pb:pb + 64, jq, so:so + 32], kT[pb:pb + 64, kh, 128 + so:128 + so + NS],
                                start=True, stop=True, tile_position=(pb, 32 * g)))
                S.op("tensor", fns, reads=[("R", j, si) for j in range(8)] + [("Rpad", 0), ("pbf", st)] + [("kT", kh, si) for kh in range(4)],
                     writes=[("ps", b), ("ps", b + 1)])
                eb = 0
                S.op("vector", lambda e, eb=eb: e.tensor_tensor(ess[:, eb, :, 128:144], scv[:, :, 128:144], masksb[:], ALU.add),
                     reads=[("ps", b), ("ps", b + 1), "masksb"], writes=[("essn", eb)])
                s0 = rot("sm", 8)
                mx, ng, ssum, dd = sm[:, s0, 0, :], sm[:, s0, 1, :], sm[:, s0, 2, :], sm[:, s0, 3, :]
                m2, s2 = sm[:, s0, 4, :], sm[:, s0, 5, :]
                S.op("vector", lambda e, mx=mx: e.tensor_reduce(mx, scv[:, :, 0:128], AX.X, ALU.max), reads=[("ps", b), ("ps", b + 1)], writes=[("sm", s0, 0)])
                S.op("vector", lambda e, m2=m2, eb=eb: e.tensor_reduce(m2, ess[:, eb, :, 128:144], AX.X, ALU.max), reads=[("essn", eb)], writes=[("sm", s0, 4)])
                S.op("vector", lambda e, mx=mx, m2=m2: e.tensor_tensor(mx, mx, m2, ALU.max), reads=[("sm", s0, 0), ("sm", s0, 4)], writes=[("sm", s0, 0)])
                S.op("vector", lambda e, mx=mx, ng=ng: e.scalar_tensor_tensor(out=ng, in0=mx, scalar=-SCALE, in1=nsnkc[:, l, :], op0=ALU.mult, op1=ALU.min),
                     reads=[("sm", s0, 0), "nsnkc"], writes=[("sm", s0, 1)])
                S.op("vector", lambda e, ssum=ssum, s2=s2: e.memset(sm[:, s0, 2:4, :], 0.0), writes=[("sm", s0, 2), ("sm", s0, 3)])
                S.op("vector", lambda e, s2=s2: e.memset(s2, 0.0), writes=[("sm", s0, 5)])
                for kh in range(4):
                    S.op("scalar", lambda e, kh=kh, eb=eb, ng=ng, ssum=ssum: e.activation(
                        out=ess[:, eb, kh, 0:128], in_=scv[:, kh, 0:128], func=AF.Exp, bias=ng[:, kh:kh + 1], scale=SCALE, accum_out=ssum[:, kh:kh + 1]),
                        reads=[("ps", b), ("ps", b + 1), ("sm", s0, 1), ("sm", s0, 2)], writes=[("essc", eb, kh), ("sm", s0, 2)])
                    S.op("scalar", lambda e, kh=kh, eb=eb, ng=ng, s2=s2: e.activation(
                        out=ess[:, eb, kh, 128:144], in_=ess[:, eb, kh, 128:144], func=AF.Exp, bias=ng[:, kh:kh + 1], scale=SCALE, accum_out=s2[:, kh:kh + 1]),
                        reads=[("essn", eb), ("sm", s0, 1), ("sm", s0, 5)], writes=[("essn", eb), ("sm", s0, 5)])
                S.op("vector", lambda e, dd=dd, ng=ng: e.tensor_tensor(dd, snkc[:, l, :], ng, ALU.add), reads=["snkc", ("sm", s0, 1)], writes=[("sm", s0, 3)])
                S.op("scalar", lambda e, dd=dd: e.activation(out=dd, in_=dd, func=AF.Exp), reads=[("sm", s0, 3)], writes=[("sm", s0, 3)])
                S.op("vector", lambda e, dd=dd, ssum=ssum: e.tensor_tensor(dd, dd, ssum, ALU.add), reads=[("sm", s0, 3), ("sm", s0, 2)], writes=[("sm", s0, 3)])
                S.op("vector", lambda e, dd=dd, s2=s2: e.tensor_tensor(dd, dd, s2, ALU.add), reads=[("sm", s0, 3), ("sm", s0, 5)], writes=[("sm", s0, 3)])
                S.op("vector", lambda e, dd=dd: e.reciprocal(dd, dd), reads=[("sm", s0, 3)], writes=[("sm", s0, 3)])
                S.op("vector", lambda e, dd=dd, bq=bq: e.tensor_scalar(dd, dd, rowselt[:, bq:bq + 1], None, ALU.mult), reads=[("sm", s0, 3), "rowsel"], writes=[("sm", s0, 3)])
                S.op("vector", lambda e, dd=dd, eb=eb: e.tensor_tensor(ess[:, eb], ess[:, eb], dd[:, :, None].broadcast_to([128, 4, 144]), ALU.mult),
                     reads=[("essc", eb, kh) for kh in range(4)] + [("essn", eb), ("sm", s0, 3)], writes=[("essc", eb, kh) for kh in range(4)] + [("essn", eb)])
                btp = S.bank()
                S.op("tensor", [lambda e, kh=kh: e.transpose(ps[:, btp, kh * 128:(kh + 1) * 128], ess[:, eb, kh, 0:128], idf[:]) for kh in range(4)],
                     reads=[("essc", eb, kh) for kh in range(4)] + ["idf"], writes=[("ps", btp)])
                copy_op(evac_eng(), pTs[:, st], ps[:, btp, :].rearrange("p (k s) -> p k s", k=4), reads=[("ps", btp)], writes=[("pT", st)])
                bt2 = S.bank()
                S.op("tensor", [lambda e, kh=kh: e.transpose(ps[0:NS, bt2, kh * 128:(kh + 1) * 128], ess[:, eb, kh, 128:144], idf[:]) for kh in range(4)],
                     reads=[("essn", eb), "idf"], writes=[("ps", bt2)])
                copy_op(evac_eng(), pTn[:, st], ps[0:NS, bt2, :].rearrange("p (k s) -> p k s", k=4), reads=[("ps", bt2)], writes=[("pT", st)])
                fns = []
                for kh in range(4):
                    fns.append(lambda e, kh=kh: e.matmul(ps[:, bo, kh * 128:(kh + 1) * 128], Vb[:, st, kh, :], pTs[:, st, kh, :],
                                                         start=(bq == 0 and kh == 0), stop=False, skip_group_check=True))
                    fns.append(lambda e, kh=kh: e.matmul(ps[:, bo, kh * 128:(kh + 1) * 128], Vnew[:, kh, :], pTn[:, st, kh, :],
                                                         start=False, stop=(bq == NS - 1 and kh == 3), skip_group_check=True))
                S.op("tensor", fns, reads=[("pbf", st), ("pT", st), ("Vt", 5)], writes=[("ps", bo)])
              unit()
            S.reserved.discard(bo)
            ov = ps[:, bo, :].rearrange("p (k g s) -> p k g s", k=4, g=4)
            for kh in range(4):
                for g in range(4):
                    h = 4 * kh + g
                    jq, pb = h // 2, (h % 2) * 64
                    copy_op("vector", aT[pb:pb + 64, jq, so:so + NS], ov[pb:pb + 64, kh, g, 0:NS], reads=[("ps", bo)], writes=[("aT", jq, si)])

        def gates_out(v, l):
            for j in range(8):
                w1, f1 = load_w(w_pw[l, :, j * 128:(j + 1) * 128])
                w2, f2 = load_w(w_in[l, :, 3 * D + 512 + j * 128:3 * D + 512 + (j + 1) * 128])
                w3, f3 = load_w(w_ao[l, :, j * 128:(j + 1) * 128])
                w4, f4 = load_w(w_in[l, :, 4 * D + 512 + j * 128:4 * D + 512 + (j + 1) * 128])
                for i, (c0, n) in enumerate(v.nts):
                    b1 = proj(w1, f1, v.s, kR(16, i), c0, n)
                    b2 = proj(w2, f2, v.xn, kxn(i), c0, n)
                    t1 = rot("tmp", 4)
                    S.op("scalar", lambda e, b2=b2, t1=t1, n=n: e.activation(out=tmp[:, t1, 0:n], in_=ps[:, b2, 0:n], func=AF.Sigmoid),
                         reads=[("ps", b2)], writes=[("tmp", t1)])
                    S.op("vector", lambda e, b1=b1, t1=t1, n=n: e.tensor_tensor(tmp[:, t1, 0:n], ps[:, b1, 0:n], tmp[:, t1, 0:n], ALU.mult),
                         reads=[("ps", b1), ("tmp", t1)], writes=[("tmp", t1)])
                    b3 = proj(w3, f3, v.a, ka(i), c0, n)
                    b4 = proj(w4, f4, v.xn, kxn(i), c0, n)
                    t2 = rot("tmp", 4)
                    S.op("scalar", lambda e, b4=b4, t2=t2, n=n: e.activation(out=tmp[:, t2, 0:n], in_=ps[:, b4, 0:n], func=AF.Sigmoid),
                         reads=[("ps", b4)], writes=[("tmp", t2)])
                    S.op("vector", lambda e, b3=b3, t2=t2, n=n: e.tensor_tensor(tmp[:, t2, 0:n], ps[:, b3, 0:n], tmp[:, t2, 0:n], ALU.mult),
                         reads=[("ps", b3), ("tmp", t2)], writes=[("tmp", t2)])
                    S.op("vector", lambda e, t1=t1, t2=t2, j=j, c0=c0, n=n: e.tensor_tensor(v.m(j, c0, n), tmp[:, t1, 0:n], tmp[:, t2, 0:n], ALU.add),
                         reads=[("tmp", t1), ("tmp", t2)], writes=[("R", 8 + j, i)])
            for j in range(8):
                wo, fo = load_w(w_out[l, :, j * 128:(j + 1) * 128])
                for i, (c0, n) in enumerate(v.nts):
                    b = proj(wo, fo, v.m, kR(8, i), c0, n)
                    S.op("vector", lambda e, b=b, j=j, c0=c0, n=n: e.tensor_tensor(xT[:, j, c0:c0 + n], xT[:, j, c0:c0 + n], ps[:, b, 0:n], ALU.add),
                         reads=[("ps", b), ("xT", j, i)], writes=[("xT", j, i)])

        def ffn(v, l, sc_first, sc_last):
            tc, pc, so = v.tc, v.pc, v.so
            nn = len(v.nts)
            if so is not None:
                sl = rot("xs", 2)
                for k in range(2):
                    S.dma("sync", xstage[k * NS:(k + 1) * NS, sl, 0:1024], sffn[l, :, k, 0:1024], "xs%d" % sl, writes=[("xstage", sl)])
                sl2 = rot("xs", 2)
                for k in range(2):
                    S.dma("sync", xstage[k * NS:(k + 1) * NS, sl2, 0:1024], sffn[l, :, k, 1024:2048], "xs%d" % sl2, writes=[("xstage", sl2)])
                S.dma("sync", ffn_s[l, :, 0, :], sffn[l, :, 1, :], "ffs", writes=["OUT"])
                for part, slp in ((0, sl), (1, sl2)):
                    for g0 in range(0, 8, 4):
                        b = S.bank()
                        S.op("tensor", [lambda e, jj=jj, b=b, slp=slp: e.transpose(ps[:, b, (jj % 4) * 32:(jj % 4 + 1) * 32], xstage[0:32, slp, jj * 128:(jj + 1) * 128], idf[0:32, 0:32])
                                        for jj in range(g0, g0 + 4)], reads=[("xstage", slp), "idf"], writes=[("ps", b)])
                        eng_ = evac_eng()
                        for jj in range(g0, g0 + 4):
                            copy_op(eng_, hTs[:, part * 8 + jj, 0:2, :], ps[:, b, (jj % 4) * 32:(jj % 4 + 1) * 32].rearrange("p (k b) -> p k b", k=2),
                                    reads=[("ps", b)], writes=[("hTs", part * 8 + jj)])
                sl3 = rot("xs", 2)
                for k in range(2):
                    S.dma("sync", xstage[k * NS:(k + 1) * NS, sl3, 0:768], sffn[l, :, k, 2048:2816], "xs%d" % sl3, writes=[("xstage", sl3)])
                for g0 in range(0, 6, 3):
                    b = S.bank()
                    S.op("tensor", [lambda e, jj=jj, b=b: e.transpose(ps[:, b, (jj % 3) * 32:(jj % 3 + 1) * 32], xstage[0:32, sl3, jj * 128:(jj + 1) * 128], idf[0:32, 0:32])
                                    for jj in range(g0, g0 + 3)], reads=[("xstage", sl3), "idf"], writes=[("ps", b)])
                    eng_ = evac_eng()
                    for jj in range(g0, g0 + 3):
                        copy_op(eng_, hTs[:, 16 + jj, 0:2, :], ps[:, b, (jj % 3) * 32:(jj % 3 + 1) * 32].rearrange("p (k b) -> p k b", k=2),
                                reads=[("ps", b)], writes=[("hTs", 16 + jj)])
            rmsnorm_to_xn(v, l, O_NF)
            for jf in range(NFC):
                wh, fh = load_w(w_up[l, :, jf * 128:(jf + 1) * 128])
                wg, fg = load_w(w_up[l, :, DFF + jf * 128:DFF + (jf + 1) * 128])
                db_ = jf % 2
                S.op("vector", lambda e, jf=jf, db_=db_: e.tensor_tensor(
                    dgf[:, db_], idb[:, None, :].broadcast_to([128, 3, 128]),
                    vec[:, l * LV + O_FDW + jf * 3:l * LV + O_FDW + (jf + 1) * 3, None].broadcast_to([128, 3, 128]), ALU.mult),
                    reads=["idb", "vec"], writes=[("dgf", db_)])
                hb = jf % 2
                S.op("vector", lambda e, jf=jf, hb=hb: e.tensor_copy(hT[:, hb, 0:2], hhist[:, l, jf, :]),
                     reads=[("hhist", l, jf)], writes=[("hT", hb, "h")])
                for i, (c0, n) in enumerate(v.nts):
                    bh = proj(wh, fh, v.xn, kxn(i), c0, n)
                    S.op("scalar", lambda e, bh=bh, hb=hb, c0=c0, n=n: e.activation(out=hT[:, hb, 2 + c0:2 + c0 + n], in_=ps[:, bh, 0:n], func=AF.Copy),
                         reads=[("ps", bh)], writes=[("hT", hb, i)])
                    if sc_last and c0 <= pc - 2 and c0 + n >= pc:
                        lo = pc - 2 - c0
                        S.op("scalar", lambda e, bh=bh, jf=jf, lo=lo: e.activation(out=hf32p[:, jf, :], in_=ps[:, bh, lo:lo + 2], func=AF.Copy), reads=[("ps", bh)], writes=[("hf32p", jf)])
                    if so is not None and i == v.si:
                        lo = so - c0
                        S.op("scalar", lambda e, bh=bh, jf=jf, lo=lo: e.activation(out=hf32[:, jf, 0:NS], in_=ps[:, bh, lo:lo + NS], func=AF.Copy),
                             reads=[("ps", bh)], writes=[("hf32", jf)])
                        S.op("vector", lambda e, jf=jf: e.tensor_copy(hTs[:, jf, 2, :], hf32[:, jf, 0:NS]), reads=[("hf32", jf)], writes=[("hTs", jf)])
                if sc_first:
                    S.op("vector", lambda e, hb=hb: e.tensor_tensor(hT[:, hb, 2:2 + 384], hT[:, hb, 2:2 + 384], validb[:], ALU.mult),
                         reads=[("hT", hb, i) for i in range(nn)] + ["validb"], writes=[("hT", hb, i) for i in range(nn)])
                S.op("vector", lambda e, jf=jf, hb=hb: e.tensor_copy(hhist[:, l, jf, :], hT[:, hb, pc:pc + 2]),
                     reads=[("hT", hb, i) for i in range(nn)] + [("hT", hb, "h")], writes=[("hhist", l, jf)])
                for i, (c0, n) in enumerate(v.nts):
                    bg = proj(wg, fg, v.xn, kxn(i), c0, n)
                    bc = S.bank()
                    pairs = [(dgf[:, db_, k, :], hT[:, hb, c0 + k:c0 + k + n]) for k in range(3)]
                    rd = [("dgf", db_), ("hT", hb, "h")] + [("hT", hb, ii) for ii in range(max(0, i - 1), i + 1)]
                    mm_group(ps[:, bc, 0:n], pairs, reads=rd, bank_keys=[("ps", bc)])
                    ts = rot("tmp", 4)
                    S.op("scalar", lambda e, bc=bc, ts=ts, jf=jf, n=n: e.activation(out=tmp[:, ts, 0:n], in_=ps[:, bc, 0:n], func=AF.Gelu,
                                                                                  bias=vcol(l, O_FDB, jf), scale=1.0),
                         reads=[("ps", bc), "vec"], writes=[("tmp", ts)])
                    S.op("vector", lambda e, bg=bg, ts=ts, jf=jf, c0=c0, n=n: e.tensor_tensor(v.hg(jf, c0, n), tmp[:, ts, 0:n], ps[:, bg, 0:n], ALU.mult),
                         reads=[("ps", bg), ("tmp", ts)], writes=[("R", jf, i)])
                    if so is not None and i == v.si:
                        lo = so - c0
                        bcs = S.bank()
                        mm_group(ps[:, bcs, 0:NS], [(dgf[:, db_, k, :], hTs[:, jf, k, :]) for k in range(3)], reads=[("dgf", db_), ("hTs", jf)], bank_keys=[("ps", bcs)])
                        ts = rot("tmp", 4)
                        S.op("scalar", lambda e, bcs=bcs, ts=ts, jf=jf: e.activation(out=tmp[:, ts, 0:NS], in_=ps[:, bcs, 0:NS], func=AF.Gelu,
                                                                                   bias=vcol(l, O_FDB, jf), scale=1.0),
                             reads=[("ps", bcs), "vec"], writes=[("tmp", ts)])
                        S.op("vector", lambda e, bg=bg, ts=ts, jf=jf, lo=lo: e.tensor_tensor(v.hg(jf, so, NS), tmp[:, ts, 0:NS], ps[:, bg, lo:lo + NS], ALU.mult),
                             reads=[("ps", bg), ("tmp", ts)], writes=[("R", jf, i)])
            for j in range(8):
                wd_, fd = load_w(w_dn[l, :, j * 128:(j + 1) * 128], k_chunks=NFC)
                for i, (c0, n) in enumerate(v.nts):
                    b = proj(wd_, fd, v.hg, kR(0, i, NFC), c0, n, kc=NFC)
                    S.op("vector", lambda e, b=b, j=j, c0=c0, n=n: e.tensor_tensor(xT[:, j, c0:c0 + n], xT[:, j, c0:c0 + n], ps[:, b, 0:n], ALU.add),
                         reads=[("ps", b), ("xT", j, i)], writes=[("xT", j, i)])

        def final_out(v, t0, ntl):
            cols = [(t * 128, 128, (t0 + t - 3) * 128) for t in range(ntl) if t0 + t >= 3]
            if v.so is not None:
                cols.append((v.so, NS, NOWN * 128))
            for (c0, n, row0) in cols:
                its = nts_of(v, c0, c0 + n)
                kk = [("xT", j, i) for j in range(8) for i in its]
                sl = rms_stats(lambda c0_, n_: xT[:, :, c0_:c0_ + n_], kk, c0, n)
                for j in range(8):
                    S.op("vector", lambda e, j=j, c0=c0, n=n, sl=sl: e.scalar_tensor_tensor(
                        out=ytmp[:, j, 0:n], in0=xT[:, j, c0:c0 + n], scalar=vec[:, 2 * LV + j:2 * LV + j + 1],
                        in1=st1[:, sl, 0:n], op0=ALU.mult, op1=ALU.mult),
                        reads=kk + [("st1", sl), "vec"], writes=[("ytmp", j), ("esb", 0, j % 4)])
                xsl = rot("xs", 2)
                for half in range(2):
                    b = S.bank()
                    S.op("tensor", [lambda e, j=j, b=b, n=n: e.transpose(ps[0:n, b, (j % 4) * 128:(j % 4 + 1) * 128], ytmp[:, j, 0:n], idf[:])
                                    for j in range(half * 4, half * 4 + 4)], reads=[("ytmp", j) for j in range(8)] + [("esb", 0, g) for g in range(4)] + ["idf"], writes=[("ps", b)])
                    copy_op(evac_eng(), xstage[0:n, xsl, half * 512:(half + 1) * 512], ps[0:n, b, :], reads=[("ps", b)], writes=[("xstage", xsl)])
                S.dma("sync", y[row0:row0 + n, :], xstage[0:n, xsl, :], "xs%d" % xsl, reads=[("xstage", xsl)], writes=["OUT"])

        for l in range(2):
            S.dma("sync", conv_s[l, :, 0:29, :], sconv[l, :, 1:30, :], "cvs", writes=["OUT"])

        sc_list = [(t0, ntl, i == len(SCS) - 1) for i, (t0, ntl) in enumerate(SCS)]
        for si, (t0, ntl, has_s) in enumerate(sc_list):
            S.new_phase()
            v = make_views(ntl * 128, has_s)
            sc_first = (si == 0)
            sc_last = (si == len(SCS) - 1)
            load_x(v, t0, ntl)
            for l in range(2):
                if sc_first:
                    c_lo = 120 if l == 0 else 248
                    S.barrier()
                    h_ = ntl * 128 - c_lo
                    v.nts = [(c_lo, h_ // 2), (c_lo + h_ // 2, h_ - h_ // 2)]
                    v.si = len(v.nts) - 1
                    v.t_first = 1 if l == 0 else 2
                elif si == 1 and l == 0:
                    S.barrier()
                rmsnorm_to_xn(v, l, O_NM)
                conv_branch(v, l, sc_first, sc_last, part=0)
                q_proj(v, l)
                conv_branch(v, l, sc_first, sc_last, part=1)
                if sc_last:
                    emit_rows(l, lambda j: uf32[:, j, :], 30, 8, lambda c, w: conv_p[l, :, c:c + w], [("uf32", j) for j in range(8)])
                    emit_rows(l, lambda j: uf32s[:, j, :], NS, 8, lambda c, w: conv_s[l, :, 29, c:c + w], [("uf32s", j) for j in range(8)])
                qkv(v, l, sc_last)
                attention(v, l, t0)
                if has_s:
                    attention_sample(v, l)
                gates_out(v, l)
                ffn(v, l, sc_first, sc_last)
                if sc_last:
                    emit_rows(l, lambda j: hf32p[:, j, :], 2, NFC, lambda c, w: ffn_p[l, :, c:c + w], [("hf32p", j) for j in range(NFC)])
                    emit_rows(l, lambda j: hf32[:, j, 0:NS], NS, NFC, lambda c, w: ffn_s[l, :, 1, c:c + w], [("hf32", j) for j in range(NFC)])
            final_out(v, t0, ntl)

        waits = S._deps("sync", ["OUT"], [])
        allw = []
        for c in S.chan.values():
            allw.append((c[0], c[1]))
        S.ops["sync"].append((allw, [], None, ""))
        S.emit()
    return nc


_NC_CACHE = {}


def _host_layout(inputs):
    f = lambda a: np.ascontiguousarray(np.asarray(a, dtype=np.float32))
    xp = f(inputs["x_prompt"])[0]
    xsamp = f(inputs["x_sample"])[:, 0, :]
    meta = f(inputs["meta_tokens"])
    seq = np.concatenate([np.zeros((256 + 112, D), np.float32), meta, xp], axis=0)
    vecs = np.zeros((128, NV), np.float32)

    def colmaj(a):
        return a.reshape(-1, 128).T

    for l in range(2):
        o = l * LV
        vecs[:, o + O_NM:o + O_NM + 8] = colmaj(f(inputs["norm_mix"])[l])
        vecs[:, o + O_CDB:o + O_CDB + 8] = colmaj(f(inputs["conv_db"])[l])
        vecs[:, o + O_LG:o + O_LG + 8] = colmaj(f(inputs["conv_ln_g"])[l])
        vecs[:, o + O_LB:o + O_LB + 8] = colmaj(f(inputs["conv_ln_b"])[l])
        vecs[:, o + O_NF:o + O_NF + 8] = colmaj(f(inputs["norm_ffn"])[l])
        vecs[:, o + O_FDB:o + O_FDB + 22] = colmaj(f(inputs["ffn_db"])[l])
        cdw = f(inputs["conv_dw"])[l]
        vecs[:, o + O_CDW:o + O_CDW + 248] = cdw.T.reshape(8, 128, 31).transpose(1, 0, 2).reshape(128, 248)
        fdw = f(inputs["ffn_dw"])[l]
        vecs[:, o + O_FDW:o + O_FDW + 66] = fdw.T.reshape(22, 128, 3).transpose(1, 0, 2).reshape(128, 66)
    vecs[:, 2 * LV:2 * LV + 8] = colmaj(f(inputs["norm_final"]))
    sinks = f(inputs["attn_sinks"])
    perm = np.array([4 * k + g for k in range(4) for g in (0, 2, 1, 3)])
    sinkrow = np.ascontiguousarray(np.broadcast_to(sinks[None][:, :, perm], (128, 2, 16)))
    sinkcol = np.zeros((128, 2, 4), np.float32)
    for r in range(128):
        for kh in range(4):
            sinkcol[r, :, kh] = sinks[:, 4 * kh + r // 32]
    masks_s = np.full((128, 16), NEGM, np.float32)
    rowsel = np.zeros((128, 16), np.float32)
    for r in range(128):
        s = r % 32
        if s < 16:
            masks_s[r, s] = 0.0
            rowsel[r, s] = 1.0
    qi = np.arange(128)[:, None]
    kj = np.arange(256)[None, :]
    in_maps = []
    for c in range(NCORES):
        b0 = 16 * c - 2
        xin = seq[(b0 + 2) * 128:(b0 + 2 + NT) * 128]
        m = np.zeros((128, 5, 256), np.float32)
        for mi in range(5):
            gi = mi if mi < 4 else 8
            qpos = (b0 + gi) * 128 + qi
            kpos = (b0 + gi - 1) * 128 + kj
            rel = qpos - kpos
            ok = (rel >= 0) & (rel <= 128) & (kpos >= 112)
            if mi == 0:
                ok = ok & (kj >= 128)
            m[:, mi, :] = np.where(ok, 0.0, NEGM)
        pos = (b0 * 128 + np.arange(384))
        valid = np.ascontiguousarray(np.broadcast_to((pos >= 112).astype(np.float32)[None], (128, 384)))
        sl = slice(NS * c, NS * (c + 1))
        in_maps.append({
            "xin": np.ascontiguousarray(xin), "xs": np.ascontiguousarray(xsamp[sl]),
            "masks": m, "masks_s": masks_s, "rowsel": rowsel, "valid": valid, "vecs": vecs,
            "sinkrow": sinkrow, "sinkcol": sinkcol,
            "w_in": f(inputs["w_in"]), "w_pw": f(inputs["w_conv_pw"]), "w_ao": f(inputs["w_attn_o"]),
            "w_out": f(inputs["w_out"]), "w_up": f(inputs["w_ffn_up"]), "w_dn": f(inputs["w_ffn_down"]),
            "ck": np.ascontiguousarray(f(inputs["cache_swa_k"])[:, sl].reshape(2, NS, 128, 256)),
            "cv": np.ascontiguousarray(f(inputs["cache_swa_v"])[:, sl].reshape(2, NS, 128, 256)),
            "sconv": np.ascontiguousarray(f(inputs["state_conv"])[:, sl]),
            "sffn": np.ascontiguousarray(f(inputs["state_ffn_conv"])[:, sl]),
        })
    return in_maps


def kernel(**inputs):
    in_maps = _host_layout(inputs)
    if "nc" not in _NC_CACHE:
        _NC_CACHE["nc"] = build()
    nc = _NC_CACHE["nc"]
    res = run_bass_kernel_spmd(nc, in_maps, core_ids=list(range(NCORES)))
    r = res.results
    y_prompt = np.concatenate([r[c]["y"][:NOWN * 128] for c in range(NCORES)], axis=0)[None]
    y_sample = np.concatenate([r[c]["y"][NOWN * 128:] for c in range(NCORES)], axis=0)[:, None, :]
    last = r[NCORES - 1]
    k_p = last["kv_p"][:, 0].reshape(2, 1, 128, 4, 64)
    v_p = last["kv_p"][:, 1].reshape(2, 1, 128, 4, 64)
    c_p = last["conv_p"].reshape(2, 1, 30, D)
    f_p = last["ffn_p"].reshape(2, 1, 2, DFF)
    k_s = np.concatenate([r[c]["k_s"] for c in range(NCORES)], axis=1).reshape(2, 128, 128, 4, 64)
    v_s = np.concatenate([r[c]["v_s"] for c in range(NCORES)], axis=1).reshape(2, 128, 128, 4, 64)
    c_s = np.concatenate([r[c]["conv_s"] for c in range(NCORES)], axis=1)
    f_s = np.concatenate([r[c]["ffn_s"] for c in range(NCORES)], axis=1)
    f32 = lambda a: np.ascontiguousarray(a, dtype=np.float32)
    return (f32(y_prompt), f32(y_sample), f32(k_p), f32(v_p), f32(c_p), f32(f_p), f32(k_s), f32(v_s), f32(c_s), f32(f_s))
```

```python
import numpy as np
from contextlib import ExitStack
import concourse.bass as bass
import concourse.mybir as mybir
from concourse.bass_utils import run_bass_kernel_spmd

F32 = mybir.dt.float32
BF16 = mybir.dt.bfloat16
AF = mybir.ActivationFunctionType
ALU = mybir.AluOpType
AX = mybir.AxisListType

NCORES = 8
NT = 19
NOWN = 16
SCS = [(0, 5), (5, 5), (10, 5), (15, 4)]
TCM = 640
NS = 16
D = 1024
DFF = 2816
NFC = 22
DIN = 5632
EPS = 1e-6
SCALE = 0.125
LV = 8 * 5 + 22 + 8 * 31 + 22 * 3
NV = 2 * LV + 8
O_NM, O_CDB, O_LG, O_LB, O_NF, O_FDB, O_CDW, O_FDW = 0, 8, 16, 24, 32, 40, 62, 62 + 248
NEGM = -30000.0

ENGS = ("tensor", "vector", "scalar", "gpsimd", "sync")


class Sched:
    def __init__(self, nc, stack):
        self.nc = nc
        self.stack = stack
        self.ops = {e: [] for e in ENGS}
        self.sem = {}
        self.cnt = {}
        self.seen = {e: {} for e in ENGS}
        self.res = {}
        self.chan = {}
        self.nsem = 0
        self.pb = 0
        self.reserved = set()
        self.past = []
        self.new_phase()

    debug_tags = False
    names = []

    def _tag(self):
        if not self.debug_tags:
            return ""
        import traceback
        st = traceback.extract_stack(limit=6)
        return ">".join(str(f.lineno) for f in st[:-2])

    def _newsem(self, name):
        self.nsem += 1
        return self.stack.enter_context(self.nc.semaphore(name))

    def new_phase(self):
        for e in ("tensor", "vector", "scalar", "gpsimd"):
            if e in self.sem and self.cnt[e] > 0:
                self.past.append((self.sem[e], self.cnt[e]))
            self.sem[e] = self._newsem("s_%s_%d" % (e, self.nsem))
            self.cnt[e] = 0

    def bank(self, n=1):
        while True:
            if self.pb % 8 + n > 8:
                self.pb += 8 - self.pb % 8
            b = self.pb % 8
            if any((b + i) in self.reserved for i in range(n)):
                self.pb += 1
                continue
            self.pb += n
            return b

    def _deps(self, eng, reads, writes):
        evs = []
        for r in reads:
            st = self.res.get(r)
            if st is not None and st[0] is not None:
                evs.append(st[0])
            if st is not None and isinstance(r, tuple) and r[0] == "ps":
                evs.extend(ev for ev in st[1] if ev[2] != eng)
        for w in writes:
            st = self.res.get(w)
            if st is not None:
                if st[0] is not None:
                    evs.append(st[0])
                evs.extend(st[1])
        need = {}
        for (s, v, e) in evs:
            if e == "tensor" and eng == "tensor":
                continue
            k = id(s)
            if self.seen[eng].get(k, 0) >= v:
                continue
            if k not in need or need[k][1] < v:
                need[k] = (s, v)
        waits = []
        for k, (s, v) in need.items():
            self.seen[eng][k] = v
            waits.append((s, v))
        return waits

    def _commit(self, ev, reads, writes):
        for r in reads:
            st = self.res.setdefault(r, [None, []])
            st[1].append(ev)
        for w in writes:
            self.res[w] = [ev, []]

    def op(self, eng, fns, reads=(), writes=()):
        if callable(fns):
            fns = [fns]
        waits = self._deps(eng, reads, writes)
        self.cnt[eng] += 1
        ev = (self.sem[eng], self.cnt[eng], eng)
        self.ops[eng].append((waits, fns, (self.sem[eng], 1), self._tag()))
        self._commit(ev, reads, writes)

    def dma(self, queue, out, in_, chan, reads=(), writes=()):
        self.nout = getattr(self, "nout", 0) + 1
        writes = [("OUT", self.nout) if w == "OUT" else w for w in writes]
        if chan not in self.chan:
            self.chan[chan] = [self._newsem("d_%s" % chan), 0]
        c = self.chan[chan]
        waits = self._deps(queue, reads, writes)
        c[1] += 16
        ev = (c[0], c[1], "dma")
        self.ops[queue].append((waits, [lambda e: e.dma_start(out=out, in_=in_)], (c[0], 16), self._tag()))
        self._commit(ev, reads, writes)

    def barrier(self):
        evs = list(self.past)
        for e in ("tensor", "vector", "scalar", "gpsimd"):
            if self.cnt[e] > 0:
                evs.append((self.sem[e], self.cnt[e]))
        for c in self.chan.values():
            evs.append((c[0], c[1]))
        for eng in ENGS:
            waits = []
            for (s, v) in evs:
                if self.seen[eng].get(id(s), 0) < v:
                    self.seen[eng][id(s)] = v
                    waits.append((s, v))
            self.ops[eng].append((waits, [], None, ""))

    def emit(self):
        nc = self.nc
        with nc.Block() as block:
            for ename in ENGS:
                lst = self.ops[ename]

                def body(e, lst=lst):
                    for waits, fns, inc, tag in lst:
                        for (s, v) in waits:
                            e.wait_ge(s, v)
                        ins = None
                        for f in fns:
                            ins = f(e)
                            if self.debug_tags:
                                ins.annotate(tag)
                                self.names.append((ins.ins.name, ename, tag, str(ins)[:300]))
                        if inc is not None and ins is not None:
                            ins.then_inc(inc[0], inc[1])

                getattr(block, ename)(body)


def ntile_list(tc):
    if tc <= 320:
        return [(0, tc)]
    h = tc // 2
    return [(0, h), (h, tc - h)]


def build(dbg=None):
    nc = bass.Bass("TRN2", target_bir_lowering=False)

    def din(name, shape):
        return nc.dram_tensor(name, shape, F32, kind="ExternalInput").ap()

    def dout(name, shape):
        return nc.dram_tensor(name, shape, F32, kind="ExternalOutput").ap()

    xin = din("xin", [NT * 128, D])
    xs = din("xs", [NS, D])
    masks = din("masks", [128, 5, 256])
    masks_s = din("masks_s", [128, 16])
    rowsel = din("rowsel", [128, 16])
    valid = din("valid", [128, 384])
    vecs = din("vecs", [128, NV])
    sinkrow = din("sinkrow", [128, 2, 16])
    sinkcol = din("sinkcol", [128, 2, 4])
    w_in = din("w_in", [2, D, DIN])
    w_pw = din("w_pw", [2, D, D])
    w_ao = din("w_ao", [2, D, D])
    w_out = din("w_out", [2, D, D])
    w_up = din("w_up", [2, D, 2 * DFF])
    w_dn = din("w_dn", [2, DFF, D])
    ck = din("ck", [2, NS, 128, 256])
    cv = din("cv", [2, NS, 128, 256])
    sconv = din("sconv", [2, NS, 30, D])
    sffn = din("sffn", [2, NS, 2, DFF])

    y = dout("y", [NOWN * 128 + NS, D])
    kv_p = dout("kv_p", [2, 2, 128, 256])
    conv_p = dout("conv_p", [2, 30, D])
    ffn_p = dout("ffn_p", [2, 2, DFF])
    k_s = dout("k_s", [2, NS, 128, 256])
    v_s = dout("v_s", [2, NS, 128, 256])
    conv_s = dout("conv_s", [2, NS, 30, D])
    ffn_s = dout("ffn_s", [2, NS, 2, DFF])

    with ExitStack() as es:
        def sb(name, shape, dt):
            return es.enter_context(nc.sbuf_tensor(name, shape, dt))

        idf = sb("idf", [128, 128], F32)
        idb = sb("idb", [128, 128], BF16)
        onesb = sb("onesb", [128, 128], BF16)
        epsc = sb("epsc", [128, 1], F32)
        vec = sb("vec", [128, NV], F32)
        masksb = sb("masksb", [128, 4, 16], BF16)
        rowselt = sb("rowselt", [128, 16], F32)
        validb = sb("validb", [128, 384], BF16)
        snk = sb("snk", [128, 2, 16], F32)
        nsnk = sb("nsnk", [128, 2, 16], F32)
        snkc = sb("snkc", [128, 2, 4], F32)
        nsnkc = sb("nsnkc", [128, 2, 4], F32)

        xT = sb("xT", [128, 8, TCM], F32)
        xnT = sb("xnT", [128, 8, TCM], BF16)
        R = sb("R", [128, 24, TCM], BF16)
        aT = sb("aT", [128, 8, TCM], BF16)
        kT = sb("kT", [128, 4, 128 + TCM], BF16)
        Vt = sb("Vt", [128, 6, 4, 128], BF16)
        kprev = sb("kprev", [128, 2, 4, 128], BF16)
        vprev = sb("vprev", [128, 2, 4, 128], BF16)
        uT = sb("uT", [128, 2, 30 + TCM], BF16)
        uhist = sb("uhist", [128, 2, 8, 30], BF16)
        hT = sb("hT", [128, 2, 2 + TCM], BF16)
        hhist = sb("hhist", [128, 2, 22, 2], BF16)
        NWS = 6
        wsl = sb("wsl", [128, NWS, 8, 128], BF16)
        wdsl = sb("wdsl", [128, 2, 22, 128], BF16)
        wv = sb("wv", [128, 8, 256], BF16)
        wk = sb("wk", [128, 8, 256], BF16)
        dg = sb("dg", [128, 31, 128], BF16)
        dgf = sb("dgf", [128, 2, 3, 128], BF16)
        xstage = sb("xstage", [128, 2, 1024], F32)
        ostage = sb("ostage", [128, 2, 512], F32)
        sq = sb("sq", [128, 8, 320], BF16)
        st1 = sb("st1", [128, 2, 320], F32)
        st2 = sb("st2", [128, 2, 320], F32)
        tmp = sb("tmp", [128, 4, 320], F32)
        esb = sb("esb", [128, 2, 4, 256], F32)
        esnk = sb("esnk", [128, 2, 16], F32)
        pT = sb("pT", [128, 2, 4, 2, 128], BF16)
        pbf = sb("pbf", [128, 2, 4, 256], BF16)
        maskb2 = sb("maskb2", [128, 5, 512], BF16)
        sm = sb("sm", [128, 8, 8, 4], F32)
        ytmp = esb[:, 0].rearrange("p g (u c) -> p (g u) c", u=2)
        uf32 = sb("uf32", [128, 8, 30], F32)
        hf32 = sb("hf32", [128, 22, 16], F32)
        hf32p = sb("hf32p", [128, 22, 2], F32)
        uf32s = sb("uf32s", [128, 8, NS], F32)
        uTs = sb("uTs", [128, 8, 31, NS], BF16)
        hTs = R[:, 0:22, 544:544 + 3 * NS].rearrange("p j (k b) -> p j k b", k=3)
        kvst = sb("kvst", [128, 2, 2, 256], F32)
        kdup = sb("kdup", [128, 1, 4, 128], F32)
        ess = sb("ess", [128, 1, 4, 144], F32)
        KTb = pbf[:, :, :, 0:128]
        Vb = pbf[:, :, :, 128:256]
        pTs = pT[:, :, :, 0, :]
        pTn = pT[0:NS, :, :, 1, :]
        Vnew = Vt[0:NS, 5]

        ps = es.enter_context(nc.psum_tensor("ps", [128, 8, 512], F32))

        S = Sched(nc, es)
        ctr = {"w": 0, "wd": 0, "xs": 0, "os": 0, "tmp": 0, "st": 0, "es": 0, "sm": 0, "kv": 0, "ev": 0}

        def rot(name, n):
            v = ctr[name] % n
            ctr[name] += 1
            return v

        def evac_eng():
            return "vector" if rot("ev", 2) == 0 else "scalar"

        def copy_op(eng, out, in_, reads, writes):
            if eng == "scalar":
                S.op("scalar", lambda e: e.activation(out=out, in_=in_, func=AF.Copy), reads=reads, writes=writes)
            else:
                S.op(eng, lambda e: e.tensor_copy(out, in_), reads=reads, writes=writes)

        S.dma("sync", vec[:], vecs[:, :], "c_vec", writes=["vec"])
        S.dma("sync", snk[:], sinkrow[:, :, :], "c_snk", writes=["snk"])
        S.dma("sync", snkc[:], sinkcol[:, :, :], "c_snkc", writes=["snkc"])
        S.dma("sync", rowselt[:], rowsel[:, :], "c_rs", writes=["rowsel"])
        S.dma("gpsimd", maskb2[:, :, 0:256], masks[:, :, :], "c_mask", writes=["maskb"])
        S.dma("gpsimd", maskb2[:, :, 256:512], masks[:, :, :], "c_mask", writes=["maskb"])
        S.dma("gpsimd", validb[:], valid[:, :], "c_valid", writes=["validb"])
        for k4 in range(4):
            S.dma("gpsimd", masksb[:, k4, :], masks_s[:, :], "c_masks", writes=["masksb"])
        S.op("gpsimd", lambda e: e.memset(idf[:], 1.0), writes=["idf"])
        S.op("gpsimd", lambda e: e.affine_select(idf[:], idf[:], pattern=[[-1, 128]], compare_op=ALU.is_equal,
                                                 fill=0.0, base=0, channel_multiplier=1), reads=["idf"], writes=["idf"])
        S.op("vector", lambda e: e.tensor_copy(idb[:], idf[:]), reads=["idf"], writes=["idb"])
        S.op("vector", lambda e: e.memset(onesb[:], 1.0 / 1024.0), writes=["onesb"])
        S.op("vector", lambda e: e.memset(epsc[:], EPS), writes=["epsc"])
        S.op("gpsimd", lambda e: e.memset(uT[:], 0.0), writes=["init_uT"])
        S.op("gpsimd", lambda e: e.memset(hT[:], 0.0), writes=["init_hT"])
        S.op("gpsimd", lambda e: e.memset(xnT[:], 0.0), writes=["init_xnT"])
        S.op("gpsimd", lambda e: e.memset(kT[:], 0.0), writes=["init_kT"])
        S.op("gpsimd", lambda e: e.memset(Vt[:], 0.0), writes=["init_Vt"])
        S.op("gpsimd", lambda e: e.memset(aT[:], 0.0), writes=["init_aT"])
        S.op("vector", lambda e: e.memset(kprev[:], 0.0), writes=[("kprev", 0), ("kprev", 1)])
        S.op("vector", lambda e: e.memset(vprev[:], 0.0), writes=[("vprev", 0), ("vprev", 1)])
        S.op("vector", lambda e: e.memset(uhist[:], 0.0), writes=[("uhist", l, j) for l in range(2) for j in range(8)])
        S.op("vector", lambda e: e.memset(hhist[:], 0.0), writes=[("hhist", l, j) for l in range(2) for j in range(22)])
        S.op("vector", lambda e: e.tensor_scalar(nsnk[:], snk[:], -1.0, None, ALU.mult), reads=["snk"], writes=["nsnk"])
        S.op("scalar", lambda e: e.activation(out=esnk[:], in_=snk[:], func=AF.Exp), reads=["snk"], writes=["esnk"])
        S.op("vector", lambda e: e.tensor_scalar(nsnkc[:], snkc[:], -1.0, None, ALU.mult), reads=["snkc"], writes=["nsnkc"])

        def vcol(l, off, j):
            c = l * LV + off + j
            return vec[:, c:c + 1]

        def load_w(src, k_chunks=8):
            if k_chunks == 8:
                s = rot("w", NWS)
                S.dma("gpsimd", wsl[:, s, :, :], src.rearrange("(k p) m -> p k m", p=128), "w%d" % s, writes=[("w", s)])
                return ("w", s), (lambda k, s=s: wsl[:, s, k, :])
            s = rot("wd", 2)
            S.dma("gpsimd", wdsl[:, s, :, :], src.rearrange("(k p) m -> p k m", p=128), "wd%d" % s, writes=[("wd", s)])
            return ("wd", s), (lambda k, s=s: wdsl[:, s, k, :])

        def mm_group(out_ap, pairs, reads, bank_keys):
            n = len(pairs)
            fns = []
            for i, (l_, r_) in enumerate(pairs):
                fns.append(lambda e, l_=l_, r_=r_, i=i: e.matmul(out_ap, l_, r_, start=(i == 0), stop=(i == n - 1)))
            S.op("tensor", fns, reads=reads, writes=bank_keys)

        def proj(wkey, wfn, src, skeys, c0, n, kc=8):
            b = S.bank()
            mm_group(ps[:, b, 0:n], [(wfn(k), src(k, c0, n)) for k in range(kc)],
                     reads=[wkey] + skeys, bank_keys=[("ps", b)])
            return b

        class V:
            pass

        def make_views(pc, has_sample):
            v = V()
            v.pc = pc
            v.so = pc if has_sample else None
            v.tc = tc = pc + (NS if has_sample else 0)
            v.nts = ntile_list(tc)
            v.si = len(v.nts) - 1
            v.t_first = 0
            v.x = lambda j, c0, n: xT[:, j, c0:c0 + n]
            v.xn = lambda j, c0, n: xnT[:, j, c0:c0 + n]
            v.q = lambda j, c0, n: R[:, j, c0:c0 + n]
            v.m = lambda j, c0, n: R[:, 8 + j, c0:c0 + n]
            v.s = lambda j, c0, n: R[:, 16 + j, c0:c0 + n]
            v.hg = lambda j, c0, n: R[:, j, c0:c0 + n]
            v.a = lambda j, c0, n: aT[:, j, c0:c0 + n]
            return v

        def kx(i):
            return [("xT", j, i) for j in range(8)]

        def kxn(i):
            return [("xnT", j, i) for j in range(8)]

        def kR(base, i, cnt=8):
            return [("R", base + j, i) for j in range(cnt)]

        def ka(i):
            return [("aT", j, i) for j in range(8)]

        def nts_of(v, c0, c1):
            return [i for i, (a, n) in enumerate(v.nts) if a < c1 and a + n > c0]

        def rms_stats(src_fn, src_keys, c0, n):
            S.op("scalar", lambda e: e.activation(out=sq[:, :, 0:n], in_=src_fn(c0, n), func=AF.Square),
                 reads=src_keys, writes=["sq"])
            b = S.bank()
            mm_group(ps[:, b, 0:n], [(onesb[:], sq[:, j, 0:n]) for j in range(8)], reads=["sq", "onesb"], bank_keys=[("ps", b)])
            sl = rot("st", 2)
            S.op("scalar", lambda e: e.activation(out=st1[:, sl, 0:n], in_=ps[:, b, 0:n], func=AF.Sqrt, bias=epsc[:, 0:1], scale=1.0),
                 reads=[("ps", b), "epsc"], writes=[("st1", sl)])
            S.op("vector", lambda e: e.reciprocal(st1[:, sl, 0:n], st1[:, sl, 0:n]), reads=[("st1", sl)], writes=[("st1", sl)])
            return sl

        def rmsnorm_to_xn(v, l, goff):
            for i, (c0, n) in enumerate(v.nts):
                sl = rms_stats(lambda c0, n: xT[:, :, c0:c0 + n], kx(i), c0, n)
                for j in range(8):
                    S.op("vector", lambda e, j=j, c0=c0, n=n, sl=sl: e.scalar_tensor_tensor(
                        out=xnT[:, j, c0:c0 + n], in0=xT[:, j, c0:c0 + n], scalar=vcol(l, goff, j) if l < 2 else vec[:, 2 * LV + j:2 * LV + j + 1],
                        in1=st1[:, sl, 0:n], op0=ALU.mult, op1=ALU.mult),
                        reads=[("xT", j, i), ("st1", sl), "vec"], writes=[("xnT", j, i)])

        def load_x(v, t0, ntl):
            if v.so is not None:
                so = v.so
                sl = rot("xs", 2)
                S.dma("sync", xstage[0:NS, sl, :], xs[:, :], "xs%d" % sl, writes=[("xstage", sl)])
                b = S.bank()
                fns = [lambda e, j=j: e.transpose(ps[:, b, j * NS:(j + 1) * NS], xstage[0:NS, sl, j * 128:(j + 1) * 128], idf[0:NS, 0:NS])
                       for j in range(8)]
                S.op("tensor", fns, reads=[("xstage", sl), "idf"], writes=[("ps", b)])
                S.op("vector", lambda e: e.tensor_copy(xT[:, :, so:so + NS], ps[:, b, 0:8 * NS].rearrange("p (j t) -> p j t", j=8)),
                     reads=[("ps", b)], writes=kx(v.si))
            for t in range(ntl):
                sl = rot("xs", 2)
                S.dma("sync", xstage[:, sl, :], xin[(t0 + t) * 128:(t0 + t + 1) * 128, :], "xs%d" % sl, writes=[("xstage", sl)])
                for half in range(2):
                    b = S.bank()
                    fns = [lambda e, j=j, b=b, sl=sl: e.transpose(ps[:, b, (j % 4) * 128:(j % 4 + 1) * 128],
                                                           xstage[:, sl, j * 128:(j + 1) * 128], idf[:])
                           for j in range(half * 4, half * 4 + 4)]
                    S.op("tensor", fns, reads=[("xstage", sl), "idf"], writes=[("ps", b)])
                    its = nts_of(v, t * 128, t * 128 + 128)
                    copy_op(evac_eng(), xT[:, half * 4:half * 4 + 4, t * 128:(t + 1) * 128],
                            ps[:, b, :].rearrange("p (j t) -> p j t", j=4),
                            reads=[("ps", b)], writes=[("xT", j, i) for j in range(half * 4, half * 4 + 4) for i in its])

        def conv_branch(v, l, sc_first, sc_last, part=None):
            tc = v.tc
            if part != 1:
              conv_part(v, l, sc_first, sc_last)
            if part != 0:
              ln_part(v, l)

        def conv_part(v, l, sc_first, sc_last):
            tc, pc, so = v.tc, v.pc, v.so
            import os as _os
            if so is not None:
                for g in range(int(_os.environ.get("KG", "4"))):
                    sl = rot("xs", 2)
                    S.dma("sync", xstage[0:120, sl, :], sconv[l, 4 * g:4 * g + 4, :, :].rearrange("b k c -> (b k) c"),
                          "xs%d" % sl, writes=[("xstage", sl)])
                    for half in range(2):
                        b = S.bank()
                        kvar = _os.environ.get("KVAR", "")
                        W_ = 128 if "A" in kvar else 120
                        fns = [lambda e, j=j, b=b, sl=sl, W_=W_: e.transpose(ps[:, b, (j % 4) * W_:(j % 4) * W_ + 120],
                                                               xstage[0:120, sl, j * 128:(j + 1) * 128], idf[0:120, 0:120])
                               for j in range(half * 4, half * 4 + 4)]
                        S.op("tensor", fns, reads=[("xstage", sl), "idf"], writes=[("ps", b)])
                        eng_ = evac_eng()
                        for j in range(half * 4, half * 4 + 4):
                            copy_op(eng_, uTs[:, j, 0:30, 4 * g:4 * g + 4],
                                    ps[:, b, (j % 4) * W_:(j % 4) * W_ + 120].rearrange("p (b k) -> p k b", b=4),
                                    reads=[("ps", b)], writes=[("uTs", j)])
            import os as _os
            ksub = int(_os.environ.get("KSUB", "99"))
            if ksub <= 1:
                return
            for j in range(8):
                if ksub <= 2 and j >= 1:
                    return
                wa, fa = load_w(w_in[l, :, j * 128:(j + 1) * 128])
                wb, fb = load_w(w_in[l, :, D + j * 128:D + (j + 1) * 128])
                S.op("vector", lambda e, j=j: e.tensor_tensor(
                    dg[:], idb[:, None, :].broadcast_to([128, 31, 128]),
                    vec[:, l * LV + O_CDW + j * 31:l * LV + O_CDW + (j + 1) * 31, None].broadcast_to([128, 31, 128]), ALU.mult),
                    reads=["idb", "vec"], writes=["dg"])
                ub = j % 2
                S.op("vector", lambda e, j=j, ub=ub: e.tensor_copy(uT[:, ub, 0:30], uhist[:, l, j, :]),
                     reads=[("uhist", l, j)], writes=[("uT", ub, "h")])
                for i, (c0, n) in enumerate(v.nts):
                    ba = proj(wa, fa, v.xn, kxn(i), c0, n)
                    bb = proj(wb, fb, v.xn, kxn(i), c0, n)
                    ts = rot("tmp", 4)
                    S.op("scalar", lambda e, bb=bb, ts=ts, n=n: e.activation(out=tmp[:, ts, 0:n], in_=ps[:, bb, 0:n], func=AF.Sigmoid),
                         reads=[("ps", bb)], writes=[("tmp", ts)])
                    S.op("vector", lambda e, ba=ba, ts=ts, c0=c0, n=n, ub=ub: e.tensor_tensor(
                        uT[:, ub, 30 + c0:30 + c0 + n], ps[:, ba, 0:n], tmp[:, ts, 0:n], ALU.mult),
                        reads=[("ps", ba), ("tmp", ts)], writes=[("uT", ub, i)])
                    if sc_last and c0 <= pc - 30 and c0 + n >= pc:
                        lo = pc - 30 - c0
                        S.op("vector", lambda e, ba=ba, ts=ts, lo=lo, j=j: e.tensor_tensor(
                            uf32[:, j, :], ps[:, ba, lo:lo + 30], tmp[:, ts, lo:lo + 30], ALU.mult),
                            reads=[("ps", ba), ("tmp", ts)], writes=[("uf32", j)])
                    if so is not None and i == v.si:
                        lo = so - c0
                        S.op("vector", lambda e, ba=ba, ts=ts, lo=lo, j=j: e.tensor_tensor(uf32s[:, j, :], ps[:, ba, lo:lo + NS], tmp[:, ts, lo:lo + NS], ALU.mult),
                             reads=[("ps", ba), ("tmp", ts)], writes=[("uf32s", j)])
                        S.op("vector", lambda e, j=j: e.tensor_copy(uTs[:, j, 30, :], uf32s[:, j, :]),
                             reads=[("uf32s", j)], writes=[("uTs", j)])
                nn = len(v.nts)
                if sc_first:
                    S.op("vector", lambda e, ub=ub: e.tensor_tensor(uT[:, ub, 30:30 + 384], uT[:, ub, 30:30 + 384], validb[:], ALU.mult),
                         reads=[("uT", ub, i) for i in range(nn)] + ["validb"], writes=[("uT", ub, i) for i in range(nn)])
                S.op("vector", lambda e, j=j, ub=ub: e.tensor_copy(uhist[:, l, j, :], uT[:, ub, pc:pc + 30]),
                     reads=[("uT", ub, i) for i in range(nn)] + [("uT", ub, "h")], writes=[("uhist", l, j)])
                for i, (c0, n) in enumerate(v.nts):
                    b = S.bank()
                    pairs = [(dg[:, k, :], uT[:, ub, c0 + k:c0 + k + n]) for k in range(31)]
                    rd = ["dg", ("uT", ub, "h")] + [("uT", ub, ii) for ii in range(max(0, i - 1), i + 1)]
                    mm_group(ps[:, b, 0:n], pairs, reads=rd, bank_keys=[("ps", b)])
                    S.op("scalar", lambda e, b=b, j=j, c0=c0, n=n: e.activation(out=v.s(j, c0, n), in_=ps[:, b, 0:n], func=AF.Identity,
                                                                                 bias=vcol(l, O_CDB, j), scale=1.0),
                         reads=[("ps", b), "vec"], writes=[("R", 16 + j, i)])
                    if so is not None and i == v.si:
                        b = S.bank()
                        mm_group(ps[:, b, 0:NS], [(dg[:, k, :], uTs[:, j, k, :]) for k in range(31)], reads=["dg", ("uTs", j)], bank_keys=[("ps", b)])
                        S.op("scalar", lambda e, b=b, j=j: e.activation(out=v.s(j, so, NS), in_=ps[:, b, 0:NS], func=AF.Identity,
                                                                        bias=vcol(l, O_CDB, j), scale=1.0),
                             reads=[("ps", b), "vec"], writes=[("R", 16 + j, i)])
        def ln_part(v, l):
            for i, (c0, n) in enumerate(v.nts):
                bm = S.bank()
                mm_group(ps[:, bm, 0:n], [(onesb[:], v.s(j, c0, n)) for j in range(8)], reads=kR(16, i) + ["onesb"], bank_keys=[("ps", bm)])
                S.op("scalar", lambda e, c0=c0, n=n: e.activation(out=sq[:, :, 0:n], in_=R[:, 16:24, c0:c0 + n], func=AF.Square),
                     reads=kR(16, i), writes=["sq"])
                b2 = S.bank()
                mm_group(ps[:, b2, 0:n], [(onesb[:], sq[:, j, 0:n]) for j in range(8)], reads=["sq", "onesb"], bank_keys=[("ps", b2)])
                sl = rot("st", 2)
                S.op("vector", lambda e, sl=sl, bm=bm, n=n: e.tensor_copy(st2[:, sl, 0:n], ps[:, bm, 0:n]), reads=[("ps", bm)], writes=[("st2", sl)])
                S.op("vector", lambda e, sl=sl, n=n: e.tensor_tensor(st1[:, sl, 0:n], st2[:, sl, 0:n], st2[:, sl, 0:n], ALU.mult),
                     reads=[("st2", sl)], writes=[("st1", sl)])
                S.op("vector", lambda e, sl=sl, b2=b2, n=n: e.tensor_tensor(st1[:, sl, 0:n], ps[:, b2, 0:n], st1[:, sl, 0:n], ALU.subtract),
                     reads=[("ps", b2), ("st1", sl)], writes=[("st1", sl)])
                S.op("vector", lambda e, sl=sl, n=n: e.tensor_scalar(st1[:, sl, 0:n], st1[:, sl, 0:n], 0.0, None, ALU.max),
                     reads=[("st1", sl)], writes=[("st1", sl)])
                S.op("scalar", lambda e, sl=sl, n=n: e.activation(out=st1[:, sl, 0:n], in_=st1[:, sl, 0:n], func=AF.Sqrt, bias=epsc[:, 0:1], scale=1.0),
                     reads=[("st1", sl), "epsc"], writes=[("st1", sl)])
                S.op("vector", lambda e, sl=sl, n=n: e.reciprocal(st1[:, sl, 0:n], st1[:, sl, 0:n]), reads=[("st1", sl)], writes=[("st1", sl)])
                for j in range(8):
                    ts = rot("tmp", 4)
                    S.op("vector", lambda e, j=j, ts=ts, sl=sl, c0=c0, n=n: e.tensor_tensor(tmp[:, ts, 0:n], v.s(j, c0, n), st2[:, sl, 0:n], ALU.subtract),
                         reads=[("R", 16 + j, i), ("st2", sl)], writes=[("tmp", ts)])
                    S.op("vector", lambda e, ts=ts, sl=sl, n=n: e.tensor_tensor(tmp[:, ts, 0:n], tmp[:, ts, 0:n], st1[:, sl, 0:n], ALU.mult),
                         reads=[("tmp", ts), ("st1", sl)], writes=[("tmp", ts)])
                    S.op("scalar", lambda e, j=j, ts=ts, c0=c0, n=n: e.activation(out=v.s(j, c0, n), in_=tmp[:, ts, 0:n], func=AF.Silu,
                                                                                  bias=vcol(l, O_LB, j), scale=vcol(l, O_LG, j)),
                         reads=[("tmp", ts), "vec"], writes=[("R", 16 + j, i)])

        def emit_rows(l, src_fn, ncols, nchunks, dst_fn, keys):
            for g0 in range(0, nchunks, 4):
                g1 = min(nchunks, g0 + 4)
                b = S.bank()
                fns = [lambda e, j=j, b=b, g0=g0: e.transpose(ps[0:ncols, b, (j - g0) * 128:(j - g0 + 1) * 128], src_fn(j), idf[:])
                       for j in range(g0, g1)]
                S.op("tensor", fns, reads=keys + ["idf"], writes=[("ps", b)])
                sl = rot("os", 2)
                w = (g1 - g0) * 128
                copy_op(evac_eng(), ostage[0:ncols, sl, 0:w], ps[0:ncols, b, 0:w], reads=[("ps", b)], writes=[("ostage", sl)])
                S.dma("sync", dst_fn(g0 * 128, w), ostage[0:ncols, sl, 0:w], "os%d" % sl, reads=[("ostage", sl)], writes=["OUT"])

        def q_proj(v, l):
            for j in range(8):
                wq, fq = load_w(w_in[l, :, 2 * D + j * 128:2 * D + (j + 1) * 128])
                for i, (c0, n) in enumerate(v.nts):
                    b = proj(wq, fq, v.xn, kxn(i), c0, n)
                    copy_op(evac_eng(), v.q(j, c0, n), ps[:, b, 0:n], reads=[("ps", b)], writes=[("R", j, i)])

        def qkv(v, l, sc_last):
            tc, pc, so = v.tc, v.pc, v.so
            for kh in range(4):
                s = rot("w", NWS)
                src = w_in[l, :, 3 * D + kh * 64:3 * D + (kh + 1) * 64].rearrange("(k p) m -> p k m", p=128)
                S.dma("gpsimd", wsl[:, s, :, 0:64], src, "w%d" % s, writes=[("w", s)])
                S.dma("gpsimd", wsl[:, s, :, 64:128], src, "w%d" % s, writes=[("w", s)])
                fk = lambda k, s=s: wsl[:, s, k, :]
                for i, (c0, n) in enumerate(v.nts):
                    b = proj(("w", s), fk, v.xn, kxn(i), c0, n)
                    copy_op(evac_eng(), kT[:, kh, 128 + c0:128 + c0 + n], ps[:, b, 0:n], reads=[("ps", b)], writes=[("kT", kh, i)])
            S.dma("gpsimd", wv[:], w_in[l, :, 3 * D + 256:3 * D + 512].rearrange("(k p) m -> p k m", p=128), "wv", writes=["wv"])
            S.dma("gpsimd", wk[:], w_in[l, :, 3 * D:3 * D + 256].rearrange("(k p) m -> p k m", p=128), "wk", writes=["wk"])
            ntl = pc // 128
            for t in range(max(0, v.t_first - 1), ntl):
                its = nts_of(v, t * 128, t * 128 + 128)
                rk = [("xnT", j, i) for j in range(8) for i in its]
                b = S.bank()
                mm_group(ps[:, b, 0:256], [(xnT[:, k, t * 128:(t + 1) * 128], wv[:, k, :]) for k in range(8)], reads=rk + ["wv"], bank_keys=[("ps", b)])
                S.op("vector", lambda e, b=b, t=t: e.tensor_copy(Vt[:, 1 + t].rearrange("p k (u d) -> p k u d", u=2),
                                                                 ps[:, b, 0:256].rearrange("p (k d) -> p k d", k=4)[:, :, None, :].broadcast_to([128, 4, 2, 64])),
                     reads=[("ps", b)], writes=[("Vt", 1 + t)])
                if sc_last and t == ntl - 1:
                    sl = rot("os", 2)
                    S.op("vector", lambda e, b=b, sl=sl: e.tensor_copy(ostage[:, sl, 0:256], ps[:, b, 0:256]),
                         reads=[("ps", b)], writes=[("ostage", sl)])
                    S.dma("sync", kv_p[l, 1, :, :], ostage[:, sl, 0:256], "os%d" % sl, reads=[("ostage", sl)], writes=["OUT"])
                    b = S.bank()
                    mm_group(ps[:, b, 0:256], [(xnT[:, k, t * 128:(t + 1) * 128], wk[:, k, :]) for k in range(8)], reads=rk + ["wk"], bank_keys=[("ps", b)])
                    sl = rot("os", 2)
                    S.op("scalar", lambda e, b=b, sl=sl: e.activation(out=ostage[:, sl, 0:256], in_=ps[:, b, 0:256], func=AF.Copy),
                         reads=[("ps", b)], writes=[("ostage", sl)])
                    S.dma("sync", kv_p[l, 0, :, :], ostage[:, sl, 0:256], "os%d" % sl, reads=[("ostage", sl)], writes=["OUT"])
            if so is not None:
                b = S.bank()
                mm_group(ps[0:NS, b, 0:256], [(xnT[:, k, so:so + NS], wv[:, k, :]) for k in range(8)], reads=kxn(v.si) + ["wv"], bank_keys=[("ps", b)])
                S.op("vector", lambda e, b=b: e.tensor_copy(Vnew[:].rearrange("p k (u d) -> p k u d", u=2),
                                                          ps[0:NS, b, 0:256].rearrange("p (k d) -> p k d", k=4)[:, :, None, :].broadcast_to([NS, 4, 2, 64])),
                     reads=[("ps", b)], writes=[("Vt", 5)])
                sl = rot("os", 2)
                S.op("vector", lambda e, b=b, sl=sl: e.tensor_copy(ostage[0:NS, sl, 0:256], ps[0:NS, b, 0:256]),
                     reads=[("ps", b)], writes=[("ostage", sl)])
                S.dma("sync", v_s[l, :, 127, :], ostage[0:NS, sl, 0:256], "os%d" % sl, reads=[("ostage", sl)], writes=["OUT"])
                b = S.bank()
                mm_group(ps[0:NS, b, 0:256], [(xnT[:, k, so:so + NS], wk[:, k, :]) for k in range(8)], reads=kxn(v.si) + ["wk"], bank_keys=[("ps", b)])
                sl = rot("os", 2)
                S.op("scalar", lambda e, b=b, sl=sl: e.activation(out=ostage[0:NS, sl, 0:256], in_=ps[0:NS, b, 0:256], func=AF.Copy),
                     reads=[("ps", b)], writes=[("ostage", sl)])
                S.dma("sync", k_s[l, :, 127, :], ostage[0:NS, sl, 0:256], "os%d" % sl, reads=[("ostage", sl)], writes=["OUT"])

        def attention(v, l, t0):
            ntl = v.pc // 128
            nn = len(v.nts)
            tc = v.pc
            S.op("vector", lambda e: e.tensor_copy(kT[:, :, 0:128], kprev[:, l]), reads=[("kprev", l)], writes=[("kT", kh, "prev") for kh in range(4)])
            S.op("vector", lambda e: e.tensor_copy(Vt[:, 0], vprev[:, l]), reads=[("vprev", l)], writes=[("Vt", 0)])
            units = [(t, kh) for t in range(v.t_first, ntl) for kh in range(4)]
            st_ = {}

            def stage_a(u):
                t, kh = u
                gi = t0 + t
                mi = gi if gi < 4 else 4
                its = nts_of(v, t * 128, t * 128 + 128)
                b = S.bank(2)
                sc = ps[:, b:b + 2, :].rearrange("p a (g c) -> p (a g) c", g=2)
                fns = []
                for g in range(4):
                    h = 4 * kh + g
                    jq, pb = h // 2, (h % 2) * 64
                    i4 = (g % 2) * 2 + g // 2
                    fns.append(lambda e, i4=i4, g=g, jq=jq, pb=pb: e.matmul(sc[:, i4, :], R[pb:pb + 64, jq, t * 128:(t + 1) * 128],
                                                                     kT[pb:pb + 64, kh, t * 128:t * 128 + 256],
                                                                     start=(g < 2), stop=False, skip_group_check=True))
                for a_ in range(2):
                    fns.append(lambda e, a_=a_: e.matmul(ps[:, b + a_, :], idb[:], maskb2[:, mi, :], start=False, stop=True, skip_group_check=True))
                rk = [("R", jq, i) for jq in (2 * kh, 2 * kh + 1) for i in its] + [("kT", kh, i) for i in range(nn)] + [("kT", kh, "prev"), "idb", "maskb"]
                S.op("tensor", fns, reads=rk, writes=[("ps", b), ("ps", b + 1)])
                S.reserved.update((b, b + 1))
                st_[u] = (b, sc, its)

            def stage_b1(u):
                t, kh = u
                b, sc, its = st_[u]
                s0 = rot("sm", 8)
                mx, ng, ssum, dd = sm[:, s0, 0, :], sm[:, s0, 1, :], sm[:, s0, 2, :], sm[:, s0, 3, :]
                S.op("vector", lambda e: e.tensor_reduce(mx, sc, AX.X, ALU.max), reads=[("ps", b), ("ps", b + 1)], writes=[("sm", s0, 0)])
                S.op("vector", lambda e: e.scalar_tensor_tensor(out=ng, in0=mx, scalar=-SCALE, in1=nsnk[:, l, 4 * kh:4 * kh + 4],
                                                                op0=ALU.mult, op1=ALU.min),
                     reads=[("sm", s0, 0), "nsnk"], writes=[("sm", s0, 1)])
                S.op("vector", lambda e: e.memset(ssum, 0.0), writes=[("sm", s0, 2)])
                st_[u] = (b, sc, its, s0)

            def stage_b1b(u):
                t, kh = u
                b, sc, its, s0 = st_[u][:4]
                ng, dd = sm[:, s0, 1, :], sm[:, s0, 3, :]
                S.op("scalar", lambda e: e.activation(out=dd, in_=ng, func=AF.Exp), reads=[("sm", s0, 1)], writes=[("sm", s0, 3)])

            def stage_b2(u):
                t, kh = u
                b, sc, its, s0 = st_[u]
                ng, ssum = sm[:, s0, 1, :], sm[:, s0, 2, :]
                eb = rot("es", 2)
                for g in range(4):
                    S.op("scalar", lambda e, g=g: e.activation(
                        out=esb[:, eb, g, :], in_=sc[:, g, :], func=AF.Exp, bias=ng[:, g:g + 1], scale=SCALE, accum_out=ssum[:, g:g + 1]),
                        reads=[("ps", b), ("ps", b + 1), ("sm", s0, 1), ("sm", s0, 2)], writes=[("esb", eb, g), ("sm", s0, 2)])
                S.reserved.difference_update((b, b + 1))
                st_[u] = (b, sc, its, s0, eb)

            def stage_b3(u):
                t, kh = u
                b, sc, its, s0, eb = st_[u]
                ssum, dd = sm[:, s0, 2, :], sm[:, s0, 3, :]
                S.op("vector", lambda e: e.tensor_tensor(dd, dd, esnk[:, l, 4 * kh:4 * kh + 4], ALU.mult), reads=[("sm", s0, 3), "esnk"], writes=[("sm", s0, 3)])
                S.op("vector", lambda e: e.tensor_tensor(dd, dd, ssum, ALU.add), reads=[("sm", s0, 3), ("sm", s0, 2)], writes=[("sm", s0, 3)])
                S.op("vector", lambda e: e.reciprocal(dd, dd), reads=[("sm", s0, 3)], writes=[("sm", s0, 3)])
                S.op("gpsimd", lambda e: e.tensor_tensor(pbf[:, eb], esb[:, eb], dd[:, :, None].broadcast_to([128, 4, 256]), ALU.mult),
                     reads=[("esb", eb, g) for g in range(4)] + [("sm", s0, 3)], writes=[("pbf", eb)])
                st_[u] = (b, sc, its, eb)

            def stage_c(u):
                t, kh = u
                b, sc, its, eb = st_[u]
                bt = S.bank()
                ptv = ps[:, bt, :].bitcast(BF16).rearrange("p (g k q) -> p g k q", g=4, k=2)
                fns = []
                for g in range(4):
                    for blk in range(2):
                        fns.append(lambda e, g=g, blk=blk: e.transpose(ptv[:, g, blk, :], pbf[:, eb, g, blk * 128:(blk + 1) * 128], idb[:]))
                S.op("tensor", fns, reads=[("pbf", eb), "idb"], writes=[("ps", bt)])
                pb_ = rot("kv", 2)
                copy_op("vector", pT[:, pb_], ptv, reads=[("ps", bt)], writes=[("pT", pb_)])
                st_[u] = (its, pb_)

            def stage_c2(u):
                t, kh = u
                its, pb_ = st_.pop(u)
                bo = S.bank()
                fns = []
                for g in range(4):
                    for blk in range(2):
                        fns.append(lambda e, g=g, blk=blk: e.matmul(ps[:, bo, g * 128:(g + 1) * 128], Vt[:, t + blk, kh, :], pT[:, pb_, g, blk, :],
                                                                    start=(blk == 0), stop=(blk == 1)))
                S.op("tensor", fns, reads=[("pT", pb_), ("Vt", t), ("Vt", t + 1)], writes=[("ps", bo)])
                eng_ = evac_eng()
                for hh in range(2):
                    pb = hh * 64
                    copy_op(eng_, aT[pb:pb + 64, 2 * kh:2 * kh + 2, t * 128:(t + 1) * 128],
                            ps[pb:pb + 64, bo, :].rearrange("p (u j q) -> p u j q", u=2, j=2)[:, hh, :, :],
                            reads=[("ps", bo)], writes=[("aT", 2 * kh, i) for i in its] + [("aT", 2 * kh + 1, i) for i in its])

            nu = len(units)
            def un(i):
                return units[i] if 0 <= i < nu else None
            for k in range(-5, nu):
                for off, fn in ((5, stage_a), (4, stage_b1), (2, stage_b3), (3, stage_b2), (4, stage_b1b), (1, stage_c), (0, stage_c2)):
                    u = un(k + off)
                    if u is not None:
                        fn(u)
            S.op("vector", lambda e: e.tensor_copy(kprev[:, l], kT[:, :, tc:tc + 128]),
                 reads=[("kT", kh, i) for kh in range(4) for i in range(nn)], writes=[("kprev", l)])
            S.op("vector", lambda e: e.tensor_copy(vprev[:, l], Vt[:, ntl]), reads=[("Vt", ntl)], writes=[("vprev", l)])

        def attention_sample(v, l):
            so, si = v.so, v.si
            bo = S.bank()
            S.reserved.add(bo)
            S.op("vector", lambda e: e.memset(R[:, 0:8, so + NS:so + 32], 0.0), writes=[("Rpad", 0)])
            uflat = uTs[:].rearrange("p a b c -> p (a b c)").bitcast(F32)
            kdups = [kdup[:, 0], uflat[:, 0:512].rearrange("p (k c) -> p k c", k=4)]
            esss = [ess[:, 0], uflat[:, 512:512 + 576].rearrange("p (k c) -> p k c", k=4)]
            stt = {}

            def stage_x(bq):
                st = bq % 2
                kd = kdups[st]
                S.dma("sync", kvst[:, st, 0, :], ck[l, bq, :, :], "kvk%d" % st, writes=[("kvst", st, 0)])
                S.dma("sync", kvst[:, st, 1, :], cv[l, bq, :, :], "kvv%d" % st, writes=[("kvst", st, 1)])
                S.dma("sync", k_s[l, bq, 0:127, :], kvst[1:128, st, 0, :], "ksok%d" % st, reads=[("kvst", st, 0)], writes=["OUT"])
                S.dma("sync", v_s[l, bq, 0:127, :], kvst[1:128, st, 1, :], "ksov%d" % st, reads=[("kvst", st, 1)], writes=["OUT"])
                S.op("vector", lambda e: e.tensor_copy(kd.rearrange("p k (u d) -> p k u d", u=2),
                                                       kvst[:, st, 0, :].rearrange("p (k d) -> p k d", k=4)[:, :, None, :].broadcast_to([128, 4, 2, 64])),
                     reads=[("kvst", st, 0)], writes=[("kdup", st)] + [("uTs", j) for j in range(5)])
                S.op("vector", lambda e: e.tensor_copy(Vb[:, st].rearrange("p k (u d) -> p k u d", u=2),
                                                       kvst[:, st, 1, :].rearrange("p (k d) -> p k d", k=4)[:, :, None, :].broadcast_to([128, 4, 2, 64])),
                     reads=[("kvst", st, 1)], writes=[("pbfV", st)])
                bt = S.bank()
                S.op("tensor", [lambda e, kh=kh: e.transpose(ps[:, bt, kh * 128:(kh + 1) * 128], kd[:, kh, :], idf[:]) for kh in range(4)],
                     reads=[("kdup", st), "idf"], writes=[("ps", bt)])
                copy_op("scalar", KTb[:, st], ps[:, bt, :].rearrange("p (k s) -> p k s", k=4), reads=[("ps", bt)], writes=[("pbfK", st)])
                b = S.bank(2)
                S.reserved.update((b, b + 1))
                scv = ps[:, b:b + 2, :].rearrange("p a (k c) -> p (a k) c", k=2)
                fns = []
                for pbsel in range(2):
                    for kh in range(4):
                        for g in (pbsel, pbsel + 2):
                            h = 4 * kh + g
                            jq, pb = h // 2, (h % 2) * 64
                            fns.append(lambda e, kh=kh, g=g, jq=jq, pb=pb: e.matmul(
                                scv[32 * g:32 * g + 32, kh, 0:128], R[pb:pb + 64, jq, so:so + 32], KTb[pb:pb + 64, st, kh, :],
                                start=True, stop=True, tile_position=(pb, 32 * g)))
                            fns.append(lambda e, kh=kh, g=g, jq=jq, pb=pb: e.matmul(
                                scv[32 * g:32 * g + 32, kh, 128:144], R[pb:pb + 64, jq, so:so + 32], kT[pb:pb + 64, kh, 128 + so:128 + so + NS],
                                start=True, stop=True, tile_position=(pb, 32 * g)))
                S.op("tensor", fns, reads=[("R", j, si) for j in range(8)] + [("Rpad", 0), ("pbfK", st)] + [("kT", kh, si) for kh in range(4)],
                     writes=[("ps", b), ("ps", b + 1)])
                stt[bq] = (st, b, scv)

            def stage_y(bq):
                st, b, scv = stt.pop(bq)
                es_ = esss[st]
                S.op("vector", lambda e: e.tensor_tensor(es_[:, :, 128:144], scv[:, :, 128:144], masksb[:], ALU.add),
                     reads=[("ps", b), ("ps", b + 1), "masksb"], writes=[("essn", st)] + [("uTs", j) for j in range(5)])
                s0 = rot("sm", 8)
                mx, ng, ssum, dd = sm[:, s0, 0, :], sm[:, s0, 1, :], sm[:, s0, 2, :], sm[:, s0, 3, :]
                m2, s2 = sm[:, s0, 4, :], sm[:, s0, 5, :]
                S.op("vector", lambda e: e.tensor_reduce(mx, scv[:, :, 0:128], AX.X, ALU.max), reads=[("ps", b), ("ps", b + 1)], writes=[("sm", s0, 0)])
                S.op("vector", lambda e: e.tensor_reduce(m2, es_[:, :, 128:144], AX.X, ALU.max), reads=[("essn", st)], writes=[("sm", s0, 4)])
                S.op("vector", lambda e: e.tensor_tensor(mx, mx, m2, ALU.max), reads=[("sm", s0, 0), ("sm", s0, 4)], writes=[("sm", s0, 0)])
                S.op("vector", lambda e: e.scalar_tensor_tensor(out=ng, in0=mx, scalar=-SCALE, in1=nsnkc[:, l, :], op0=ALU.mult, op1=ALU.min),
                     reads=[("sm", s0, 0), "nsnkc"], writes=[("sm", s0, 1)])
                S.op("vector", lambda e: e.memset(sm[:, s0, 2:4, :], 0.0), writes=[("sm", s0, 2), ("sm", s0, 3)])
                S.op("vector", lambda e: e.memset(s2, 0.0), writes=[("sm", s0, 5)])
                for kh in range(4):
                    S.op("scalar", lambda e, kh=kh: e.activation(
                        out=es_[:, kh, 0:128], in_=scv[:, kh, 0:128], func=AF.Exp, bias=ng[:, kh:kh + 1], scale=SCALE, accum_out=ssum[:, kh:kh + 1]),
                        reads=[("ps", b), ("ps", b + 1), ("sm", s0, 1), ("sm", s0, 2)], writes=[("essc", st, kh), ("sm", s0, 2)])
                    S.op("scalar", lambda e, kh=kh: e.activation(
                        out=es_[:, kh, 128:144], in_=es_[:, kh, 128:144], func=AF.Exp, bias=ng[:, kh:kh + 1], scale=SCALE, accum_out=s2[:, kh:kh + 1]),
                        reads=[("essn", st), ("sm", s0, 1), ("sm", s0, 5)], writes=[("essn", st), ("sm", s0, 5)])
                S.reserved.difference_update((b, b + 1))
                S.op("vector", lambda e: e.tensor_tensor(dd, snkc[:, l, :], ng, ALU.add), reads=["snkc", ("sm", s0, 1)], writes=[("sm", s0, 3)])
                S.op("scalar", lambda e: e.activation(out=dd, in_=dd, func=AF.Exp), reads=[("sm", s0, 3)], writes=[("sm", s0, 3)])
                S.op("vector", lambda e: e.tensor_tensor(dd, dd, ssum, ALU.add), reads=[("sm", s0, 3), ("sm", s0, 2)], writes=[("sm", s0, 3)])
                S.op("vector", lambda e: e.tensor_tensor(dd, dd, s2, ALU.add), reads=[("sm", s0, 3), ("sm", s0, 5)], writes=[("sm", s0, 3)])
                S.op("vector", lambda e: e.reciprocal(dd, dd), reads=[("sm", s0, 3)], writes=[("sm", s0, 3)])
                S.op("vector", lambda e: e.tensor_scalar(dd, dd, rowselt[:, bq:bq + 1], None, ALU.mult), reads=[("sm", s0, 3), "rowsel"], writes=[("sm", s0, 3)])
                S.op("gpsimd", lambda e: e.tensor_tensor(es_, es_, dd[:, :, None].broadcast_to([128, 4, 144]), ALU.mult),
                     reads=[("essc", st, kh) for kh in range(4)] + [("essn", st), ("sm", s0, 3)], writes=[("essc", st, kh) for kh in range(4)] + [("essn", st)])
                btp = S.bank()
                S.op("tensor", [lambda e, kh=kh: e.transpose(ps[:, btp, kh * 128:(kh + 1) * 128], es_[:, kh, 0:128], idf[:]) for kh in range(4)],
                     reads=[("essc", st, kh) for kh in range(4)] + ["idf"], writes=[("ps", btp)])
                copy_op("vector", pTs[:, st], ps[:, btp, :].rearrange("p (k s) -> p k s", k=4), reads=[("ps", btp)], writes=[("pT", st)])
                bt2 = S.bank()
                S.op("tensor", [lambda e, kh=kh: e.transpose(ps[0:NS, bt2, kh * 128:(kh + 1) * 128], es_[:, kh, 128:144], idf[:]) for kh in range(4)],
                     reads=[("essn", st), "idf"], writes=[("ps", bt2)])
                copy_op("scalar", pTn[:, st], ps[0:NS, bt2, :].rearrange("p (k s) -> p k s", k=4), reads=[("ps", bt2)], writes=[("pT", st)])
                fns = []
                for kh in range(4):
                    fns.append(lambda e, kh=kh: e.matmul(ps[:, bo, kh * 128:(kh + 1) * 128], Vb[:, st, kh, :], pTs[:, st, kh, :],
                                                         start=(bq == 0 and kh == 0), stop=False, skip_group_check=True))
                    fns.append(lambda e, kh=kh: e.matmul(ps[:, bo, kh * 128:(kh + 1) * 128], Vnew[:, kh, :], pTn[:, st, kh, :],
                                                         start=False, stop=(bq == NS - 1 and kh == 3), skip_group_check=True))
                S.op("tensor", fns, reads=[("pbfV", st), ("pT", st), ("Vt", 5)], writes=[("ps", bo)])

            stage_x(0)
            for bq in range(NS):
                if bq + 1 < NS:
                    stage_x(bq + 1)
                stage_y(bq)
            S.reserved.discard(bo)
            ov = ps[:, bo, :].rearrange("p (k g s) -> p k g s", k=4, g=4)
            for kh in range(4):
                for g in range(4):
                    h = 4 * kh + g
                    jq, pb = h // 2, (h % 2) * 64
                    copy_op("vector", aT[pb:pb + 64, jq, so:so + NS], ov[pb:pb + 64, kh, g, 0:NS], reads=[("ps", bo)], writes=[("aT", jq, si)])

        def gates_out(v, l):
            for j in range(8):
                w1, f1 = load_w(w_pw[l, :, j * 128:(j + 1) * 128])
                w2, f2 = load_w(w_in[l, :, 3 * D + 512 + j * 128:3 * D + 512 + (j + 1) * 128])
                w3, f3 = load_w(w_ao[l, :, j * 128:(j + 1) * 128])
                w4, f4 = load_w(w_in[l, :, 4 * D + 512 + j * 128:4 * D + 512 + (j + 1) * 128])
                for i, (c0, n) in enumerate(v.nts):
                    b1 = proj(w1, f1, v.s, kR(16, i), c0, n)
                    b2 = proj(w2, f2, v.xn, kxn(i), c0, n)
                    t1 = rot("tmp", 4)
                    S.op("scalar", lambda e, b2=b2, t1=t1, n=n: e.activation(out=tmp[:, t1, 0:n], in_=ps[:, b2, 0:n], func=AF.Sigmoid),
                         reads=[("ps", b2)], writes=[("tmp", t1)])
                    S.op("vector", lambda e, b1=b1, t1=t1, n=n: e.tensor_tensor(tmp[:, t1, 0:n], ps[:, b1, 0:n], tmp[:, t1, 0:n], ALU.mult),
                         reads=[("ps", b1), ("tmp", t1)], writes=[("tmp", t1)])
                    b3 = proj(w3, f3, v.a, ka(i), c0, n)
                    b4 = proj(w4, f4, v.xn, kxn(i), c0, n)
                    t2 = rot("tmp", 4)
                    S.op("scalar", lambda e, b4=b4, t2=t2, n=n: e.activation(out=tmp[:, t2, 0:n], in_=ps[:, b4, 0:n], func=AF.Sigmoid),
                         reads=[("ps", b4)], writes=[("tmp", t2)])
                    S.op("vector", lambda e, b3=b3, t2=t2, n=n: e.tensor_tensor(tmp[:, t2, 0:n], ps[:, b3, 0:n], tmp[:, t2, 0:n], ALU.mult),
                         reads=[("ps", b3), ("tmp", t2)], writes=[("tmp", t2)])
                    S.op("vector", lambda e, t1=t1, t2=t2, j=j, c0=c0, n=n: e.tensor_tensor(v.m(j, c0, n), tmp[:, t1, 0:n], tmp[:, t2, 0:n], ALU.add),
                         reads=[("tmp", t1), ("tmp", t2)], writes=[("R", 8 + j, i)])
            for j in range(8):
                wo, fo = load_w(w_out[l, :, j * 128:(j + 1) * 128])
                for i, (c0, n) in enumerate(v.nts):
                    b = proj(wo, fo, v.m, kR(8, i), c0, n)
                    S.op("vector", lambda e, b=b, j=j, c0=c0, n=n: e.tensor_tensor(xT[:, j, c0:c0 + n], xT[:, j, c0:c0 + n], ps[:, b, 0:n], ALU.add),
                         reads=[("ps", b), ("xT", j, i)], writes=[("xT", j, i)])

        def ffn(v, l, sc_first, sc_last):
            tc, pc, so = v.tc, v.pc, v.so
            nn = len(v.nts)
            if so is not None:
                sl = rot("xs", 2)
                for k in range(2):
                    S.dma("sync", xstage[k * NS:(k + 1) * NS, sl, 0:1024], sffn[l, :, k, 0:1024], "xs%d" % sl, writes=[("xstage", sl)])
                sl2 = rot("xs", 2)
                for k in range(2):
                    S.dma("sync", xstage[k * NS:(k + 1) * NS, sl2, 0:1024], sffn[l, :, k, 1024:2048], "xs%d" % sl2, writes=[("xstage", sl2)])
                S.dma("sync", ffn_s[l, :, 0, :], sffn[l, :, 1, :], "ffs", writes=["OUT"])
                for part, slp in ((0, sl), (1, sl2)):
                    for g0 in range(0, 8, 4):
                        b = S.bank()
                        S.op("tensor", [lambda e, jj=jj, b=b, slp=slp: e.transpose(ps[:, b, (jj % 4) * 32:(jj % 4 + 1) * 32], xstage[0:32, slp, jj * 128:(jj + 1) * 128], idf[0:32, 0:32])
                                        for jj in range(g0, g0 + 4)], reads=[("xstage", slp), "idf"], writes=[("ps", b)])
                        eng_ = evac_eng()
                        for jj in range(g0, g0 + 4):
                            copy_op(eng_, hTs[:, part * 8 + jj, 0:2, :], ps[:, b, (jj % 4) * 32:(jj % 4 + 1) * 32].rearrange("p (k b) -> p k b", k=2),
                                    reads=[("ps", b)], writes=[("hTs", part * 8 + jj)])
                sl3 = rot("xs", 2)
                for k in range(2):
                    S.dma("sync", xstage[k * NS:(k + 1) * NS, sl3, 0:768], sffn[l, :, k, 2048:2816], "xs%d" % sl3, writes=[("xstage", sl3)])
                for g0 in range(0, 6, 3):
                    b = S.bank()
                    S.op("tensor", [lambda e, jj=jj, b=b: e.transpose(ps[:, b, (jj % 3) * 32:(jj % 3 + 1) * 32], xstage[0:32, sl3, jj * 128:(jj + 1) * 128], idf[0:32, 0:32])
                                    for jj in range(g0, g0 + 3)], reads=[("xstage", sl3), "idf"], writes=[("ps", b)])
                    eng_ = evac_eng()
                    for jj in range(g0, g0 + 3):
                        copy_op(eng_, hTs[:, 16 + jj, 0:2, :], ps[:, b, (jj % 3) * 32:(jj % 3 + 1) * 32].rearrange("p (k b) -> p k b", k=2),
                                reads=[("ps", b)], writes=[("hTs", 16 + jj)])
            rmsnorm_to_xn(v, l, O_NF)
            for jf in range(NFC):
                wh, fh = load_w(w_up[l, :, jf * 128:(jf + 1) * 128])
                wg, fg = load_w(w_up[l, :, DFF + jf * 128:DFF + (jf + 1) * 128])
                db_ = jf % 2
                S.op("vector", lambda e, jf=jf, db_=db_: e.tensor_tensor(
                    dgf[:, db_], idb[:, None, :].broadcast_to([128, 3, 128]),
                    vec[:, l * LV + O_FDW + jf * 3:l * LV + O_FDW + (jf + 1) * 3, None].broadcast_to([128, 3, 128]), ALU.mult),
                    reads=["idb", "vec"], writes=[("dgf", db_)])
                hb = jf % 2
                S.op("vector", lambda e, jf=jf, hb=hb: e.tensor_copy(hT[:, hb, 0:2], hhist[:, l, jf, :]),
                     reads=[("hhist", l, jf)], writes=[("hT", hb, "h")])
                for i, (c0, n) in enumerate(v.nts):
                    bh = proj(wh, fh, v.xn, kxn(i), c0, n)
                    S.op("scalar", lambda e, bh=bh, hb=hb, c0=c0, n=n: e.activation(out=hT[:, hb, 2 + c0:2 + c0 + n], in_=ps[:, bh, 0:n], func=AF.Copy),
                         reads=[("ps", bh)], writes=[("hT", hb, i)])
                    if sc_last and c0 <= pc - 2 and c0 + n >= pc:
                        lo = pc - 2 - c0
                        S.op("scalar", lambda e, bh=bh, jf=jf, lo=lo: e.activation(out=hf32p[:, jf, :], in_=ps[:, bh, lo:lo + 2], func=AF.Copy), reads=[("ps", bh)], writes=[("hf32p", jf)])
                    if so is not None and i == v.si:
                        lo = so - c0
                        S.op("scalar", lambda e, bh=bh, jf=jf, lo=lo: e.activation(out=hf32[:, jf, 0:NS], in_=ps[:, bh, lo:lo + NS], func=AF.Copy),
                             reads=[("ps", bh)], writes=[("hf32", jf)])
                        S.op("vector", lambda e, jf=jf: e.tensor_copy(hTs[:, jf, 2, :], hf32[:, jf, 0:NS]), reads=[("hf32", jf)], writes=[("hTs", jf)])
                if sc_first:
                    S.op("vector", lambda e, hb=hb: e.tensor_tensor(hT[:, hb, 2:2 + 384], hT[:, hb, 2:2 + 384], validb[:], ALU.mult),
                         reads=[("hT", hb, i) for i in range(nn)] + ["validb"], writes=[("hT", hb, i) for i in range(nn)])
                S.op("vector", lambda e, jf=jf, hb=hb: e.tensor_copy(hhist[:, l, jf, :], hT[:, hb, pc:pc + 2]),
                     reads=[("hT", hb, i) for i in range(nn)] + [("hT", hb, "h")], writes=[("hhist", l, jf)])
                for i, (c0, n) in enumerate(v.nts):
                    bg = proj(wg, fg, v.xn, kxn(i), c0, n)
                    bc = S.bank()
                    pairs = [(dgf[:, db_, k, :], hT[:, hb, c0 + k:c0 + k + n]) for k in range(3)]
                    rd = [("dgf", db_), ("hT", hb, "h")] + [("hT", hb, ii) for ii in range(max(0, i - 1), i + 1)]
                    mm_group(ps[:, bc, 0:n], pairs, reads=rd, bank_keys=[("ps", bc)])
                    ts = rot("tmp", 4)
                    S.op("scalar", lambda e, bc=bc, ts=ts, jf=jf, n=n: e.activation(out=tmp[:, ts, 0:n], in_=ps[:, bc, 0:n], func=AF.Gelu,
                                                                                  bias=vcol(l, O_FDB, jf), scale=1.0),
                         reads=[("ps", bc), "vec"], writes=[("tmp", ts)])
                    S.op("vector", lambda e, bg=bg, ts=ts, jf=jf, c0=c0, n=n: e.tensor_tensor(v.hg(jf, c0, n), tmp[:, ts, 0:n], ps[:, bg, 0:n], ALU.mult),
                         reads=[("ps", bg), ("tmp", ts)], writes=[("R", jf, i)])
                    if so is not None and i == v.si:
                        lo = so - c0
                        bcs = S.bank()
                        mm_group(ps[:, bcs, 0:NS], [(dgf[:, db_, k, :], hTs[:, jf, k, :]) for k in range(3)], reads=[("dgf", db_), ("hTs", jf)], bank_keys=[("ps", bcs)])
                        ts = rot("tmp", 4)
                        S.op("scalar", lambda e, bcs=bcs, ts=ts, jf=jf: e.activation(out=tmp[:, ts, 0:NS], in_=ps[:, bcs, 0:NS], func=AF.Gelu,
                                                                                   bias=vcol(l, O_FDB, jf), scale=1.0),
                             reads=[("ps", bcs), "vec"], writes=[("tmp", ts)])
                        S.op("vector", lambda e, bg=bg, ts=ts, jf=jf, lo=lo: e.tensor_tensor(v.hg(jf, so, NS), tmp[:, ts, 0:NS], ps[:, bg, lo:lo + NS], ALU.mult),
                             reads=[("ps", bg), ("tmp", ts)], writes=[("R", jf, i)])
            for j in range(8):
                wd_, fd = load_w(w_dn[l, :, j * 128:(j + 1) * 128], k_chunks=NFC)
                for i, (c0, n) in enumerate(v.nts):
                    b = proj(wd_, fd, v.hg, kR(0, i, NFC), c0, n, kc=NFC)
                    S.op("vector", lambda e, b=b, j=j, c0=c0, n=n: e.tensor_tensor(xT[:, j, c0:c0 + n], xT[:, j, c0:c0 + n], ps[:, b, 0:n], ALU.add),
                         reads=[("ps", b), ("xT", j, i)], writes=[("xT", j, i)])

        def final_out(v, t0, ntl):
            cols = [(t * 128, 128, (t0 + t - 3) * 128) for t in range(ntl) if t0 + t >= 3]
            if v.so is not None:
                cols.append((v.so, NS, NOWN * 128))
            for (c0, n, row0) in cols:
                its = nts_of(v, c0, c0 + n)
                kk = [("xT", j, i) for j in range(8) for i in its]
                sl = rms_stats(lambda c0_, n_: xT[:, :, c0_:c0_ + n_], kk, c0, n)
                for j in range(8):
                    S.op("vector", lambda e, j=j, c0=c0, n=n, sl=sl: e.scalar_tensor_tensor(
                        out=ytmp[:, j, 0:n], in0=xT[:, j, c0:c0 + n], scalar=vec[:, 2 * LV + j:2 * LV + j + 1],
                        in1=st1[:, sl, 0:n], op0=ALU.mult, op1=ALU.mult),
                        reads=kk + [("st1", sl), "vec"], writes=[("ytmp", j), ("esb", 0, j % 4)])
                xsl = rot("xs", 2)
                for half in range(2):
                    b = S.bank()
                    S.op("tensor", [lambda e, j=j, b=b, n=n: e.transpose(ps[0:n, b, (j % 4) * 128:(j % 4 + 1) * 128], ytmp[:, j, 0:n], idf[:])
                                    for j in range(half * 4, half * 4 + 4)], reads=[("ytmp", j) for j in range(8)] + [("esb", 0, g) for g in range(4)] + ["idf"], writes=[("ps", b)])
                    copy_op(evac_eng(), xstage[0:n, xsl, half * 512:(half + 1) * 512], ps[0:n, b, :], reads=[("ps", b)], writes=[("xstage", xsl)])
                S.dma("sync", y[row0:row0 + n, :], xstage[0:n, xsl, :], "xs%d" % xsl, reads=[("xstage", xsl)], writes=["OUT"])

        for l in range(2):
            S.dma("sync", conv_s[l, :, 0:29, :], sconv[l, :, 1:30, :], "cvs", writes=["OUT"])

        sc_list = [(t0, ntl, i == len(SCS) - 1) for i, (t0, ntl) in enumerate(SCS)]
        for si, (t0, ntl, has_s) in enumerate(sc_list):
            S.new_phase()
            v = make_views(ntl * 128, has_s)
            sc_first = (si == 0)
            sc_last = (si == len(SCS) - 1)
            load_x(v, t0, ntl)
            for l in range(2):
                if sc_first:
                    c_lo = 120 if l == 0 else 248
                    S.barrier()
                    h_ = ntl * 128 - c_lo
                    v.nts = [(c_lo, h_ // 2), (c_lo + h_ // 2, h_ - h_ // 2)]
                    v.si = len(v.nts) - 1
                    v.t_first = 1 if l == 0 else 2
                elif si == 1 and l == 0:
                    S.barrier()
                rmsnorm_to_xn(v, l, O_NM)
                conv_branch(v, l, sc_first, sc_last, part=0)
                q_proj(v, l)
                conv_branch(v, l, sc_first, sc_last, part=1)
                if sc_last:
                    emit_rows(l, lambda j: uf32[:, j, :], 30, 8, lambda c, w: conv_p[l, :, c:c + w], [("uf32", j) for j in range(8)])
                    emit_rows(l, lambda j: uf32s[:, j, :], NS, 8, lambda c, w: conv_s[l, :, 29, c:c + w], [("uf32s", j) for j in range(8)])
                qkv(v, l, sc_last)
                attention(v, l, t0)
                if has_s:
                    attention_sample(v, l)
                gates_out(v, l)
                ffn(v, l, sc_first, sc_last)
                if sc_last:
                    emit_rows(l, lambda j: hf32p[:, j, :], 2, NFC, lambda c, w: ffn_p[l, :, c:c + w], [("hf32p", j) for j in range(NFC)])
                    emit_rows(l, lambda j: hf32[:, j, 0:NS], NS, NFC, lambda c, w: ffn_s[l, :, 1, c:c + w], [("hf32", j) for j in range(NFC)])
            final_out(v, t0, ntl)

        waits = S._deps("sync", ["OUT"], [])
        allw = []
        for c in S.chan.values():
            allw.append((c[0], c[1]))
        S.ops["sync"].append((allw, [], None, ""))
        S.emit()
    return nc


_NC_CACHE = {}


def _host_layout(inputs):
    f = lambda a: np.ascontiguousarray(np.asarray(a, dtype=np.float32))
    xp = f(inputs["x_prompt"])[0]
    xsamp = f(inputs["x_sample"])[:, 0, :]
    meta = f(inputs["meta_tokens"])
    seq = np.concatenate([np.zeros((256 + 112, D), np.float32), meta, xp], axis=0)
    vecs = np.zeros((128, NV), np.float32)

    def colmaj(a):
        return a.reshape(-1, 128).T

    for l in range(2):
        o = l * LV
        vecs[:, o + O_NM:o + O_NM + 8] = colmaj(f(inputs["norm_mix"])[l])
        vecs[:, o + O_CDB:o + O_CDB + 8] = colmaj(f(inputs["conv_db"])[l])
        vecs[:, o + O_LG:o + O_LG + 8] = colmaj(f(inputs["conv_ln_g"])[l])
        vecs[:, o + O_LB:o + O_LB + 8] = colmaj(f(inputs["conv_ln_b"])[l])
        vecs[:, o + O_NF:o + O_NF + 8] = colmaj(f(inputs["norm_ffn"])[l])
        vecs[:, o + O_FDB:o + O_FDB + 22] = colmaj(f(inputs["ffn_db"])[l])
        cdw = f(inputs["conv_dw"])[l]
        vecs[:, o + O_CDW:o + O_CDW + 248] = cdw.T.reshape(8, 128, 31).transpose(1, 0, 2).reshape(128, 248)
        fdw = f(inputs["ffn_dw"])[l]
        vecs[:, o + O_FDW:o + O_FDW + 66] = fdw.T.reshape(22, 128, 3).transpose(1, 0, 2).reshape(128, 66)
    vecs[:, 2 * LV:2 * LV + 8] = colmaj(f(inputs["norm_final"]))
    sinks = f(inputs["attn_sinks"])
    perm = np.array([4 * k + g for k in range(4) for g in (0, 2, 1, 3)])
    sinkrow = np.ascontiguousarray(np.broadcast_to(sinks[None][:, :, perm], (128, 2, 16)))
    sinkcol = np.zeros((128, 2, 4), np.float32)
    for r in range(128):
        for kh in range(4):
            sinkcol[r, :, kh] = sinks[:, 4 * kh + r // 32]
    masks_s = np.full((128, 16), NEGM, np.float32)
    rowsel = np.zeros((128, 16), np.float32)
    for r in range(128):
        s = r % 32
        if s < 16:
            masks_s[r, s] = 0.0
            rowsel[r, s] = 1.0
    qi = np.arange(128)[:, None]
    kj = np.arange(256)[None, :]
    in_maps = []
    for c in range(NCORES):
        b0 = 16 * c - 2
        xin = seq[(b0 + 2) * 128:(b0 + 2 + NT) * 128]
        m = np.zeros((128, 5, 256), np.float32)
        for mi in range(5):
            gi = mi if mi < 4 else 8
            qpos = (b0 + gi) * 128 + qi
            kpos = (b0 + gi - 1) * 128 + kj
            rel = qpos - kpos
            ok = (rel >= 0) & (rel <= 128) & (kpos >= 112)
            if mi == 0:
                ok = ok & (kj >= 128)
            m[:, mi, :] = np.where(ok, 0.0, NEGM)
        pos = (b0 * 128 + np.arange(384))
        valid = np.ascontiguousarray(np.broadcast_to((pos >= 112).astype(np.float32)[None], (128, 384)))
        sl = slice(NS * c, NS * (c + 1))
        in_maps.append({
            "xin": np.ascontiguousarray(xin), "xs": np.ascontiguousarray(xsamp[sl]),
            "masks": m, "masks_s": masks_s, "rowsel": rowsel, "valid": valid, "vecs": vecs,
            "sinkrow": sinkrow, "sinkcol": sinkcol,
            "w_in": f(inputs["w_in"]), "w_pw": f(inputs["w_conv_pw"]), "w_ao": f(inputs["w_attn_o"]),
            "w_out": f(inputs["w_out"]), "w_up": f(inputs["w_ffn_up"]), "w_dn": f(inputs["w_ffn_down"]),
            "ck": np.ascontiguousarray(f(inputs["cache_swa_k"])[:, sl].reshape(2, NS, 128, 256)),
            "cv": np.ascontiguousarray(f(inputs["cache_swa_v"])[:, sl].reshape(2, NS, 128, 256)),
            "sconv": np.ascontiguousarray(f(inputs["state_conv"])[:, sl]),
            "sffn": np.ascontiguousarray(f(inputs["state_ffn_conv"])[:, sl]),
        })
    return in_maps


def kernel(**inputs):
    in_maps = _host_layout(inputs)
    if "nc" not in _NC_CACHE:
        _NC_CACHE["nc"] = build()
    nc = _NC_CACHE["nc"]
    res = run_bass_kernel_spmd(nc, in_maps, core_ids=list(range(NCORES)))
    r = res.results
    y_prompt = np.concatenate([r[c]["y"][:NOWN * 128] for c in range(NCORES)], axis=0)[None]
    y_sample = np.concatenate([r[c]["y"][NOWN * 128:] for c in range(NCORES)], axis=0)[:, None, :]
    last = r[NCORES - 1]
    k_p = last["kv_p"][:, 0].reshape(2, 1, 128, 4, 64)
    v_p = last["kv_p"][:, 1].reshape(2, 1, 128, 4, 64)
    c_p = last["conv_p"].reshape(2, 1, 30, D)
    f_p = last["ffn_p"].reshape(2, 1, 2, DFF)
    k_s = np.concatenate([r[c]["k_s"] for c in range(NCORES)], axis=1).reshape(2, 128, 128, 4, 64)
    v_s = np.concatenate([r[c]["v_s"] for c in range(NCORES)], axis=1).reshape(2, 128, 128, 4, 64)
    c_s = np.concatenate([r[c]["conv_s"] for c in range(NCORES)], axis=1)
    f_s = np.concatenate([r[c]["ffn_s"] for c in range(NCORES)], axis=1)
    f32 = lambda a: np.ascontiguousarray(a, dtype=np.float32)
    return (f32(y_prompt), f32(y_sample), f32(k_p), f32(v_p), f32(c_p), f32(f_p), f32(k_s), f32(v_s), f32(c_s), f32(f_s))
```
